# Optimizing a Trainium2 kernel written in Bass

```python
import jax, jax.numpy as jnp
from jax import lax
import numpy as np

D_MODEL = 1024
BATCH = 2
SEQ = 8192
DEPTH = 1
DEC_BATCH = 4
DEC_SEQ = 8192
PAST_LEN = 128

HG_HEADS = 8
HG_KDIM = 128
HG_VDIM = D_MODEL // HG_HEADS
HG_FDIM = HG_HEADS * HG_KDIM
HG_IDIM = HG_HEADS * HG_VDIM
CHUNK = 64
ATTN_GROUPS = ((128, 1), (512, 4), (2048, 16))
ATTN_HEADS = 4
ATTN_HEAD_DIM = 128
ATTN_WIDTH = ATTN_HEADS * ATTN_HEAD_DIM
ROT_DIM = ATTN_HEAD_DIM // 4
ROPE_THETA = 500000.0
N_MEM = 256
XA_HEADS = 4
XA_HEAD_DIM = D_MODEL // XA_HEADS
D_FF = 4 * D_MODEL
RMS_EPS = 1e-6

IN_SIZES = (HG_FDIM, HG_FDIM, HG_FDIM, HG_IDIM, HG_IDIM) + (ATTN_WIDTH,) * (3 * len(ATTN_GROUPS)) + (D_MODEL, D_MODEL)
N_IN = sum(IN_SIZES)
IN_SPLITS = [int(c) for c in np.cumsum(IN_SIZES)[:-1]]

kernel_name = "hgrn2_dilated_attn_parallel_encoder"


def _rmsnorm(x, g):
    xf = x.astype(jnp.float32)
    y = xf * lax.rsqrt(jnp.mean(xf * xf, axis=-1, keepdims=True) + RMS_EPS)
    return (y * g.astype(jnp.float32)).astype(x.dtype)


def _gla_chunk_scan(q, k, logf, v):
    B, H, S, K = q.shape
    V = v.shape[-1]
    n = S // CHUNK

    def chunks(t):
        return t.reshape(B, H, n, CHUNK, t.shape[-1]).transpose(2, 0, 1, 3, 4)

    causal = jnp.tril(jnp.ones((CHUNK, CHUNK), dtype=bool))[:, :, None]

    def step(S0, xs):
        qb, kb, fb, vb = xs
        b = jnp.cumsum(fb, axis=2)
        o_inter = jnp.einsum('bhck,bhkv->bhcv', qb * jnp.exp(b), S0)
        diff = b[:, :, :, None, :] - b[:, :, None, :, :]
        decay = jnp.exp(jnp.where(causal, diff, -jnp.inf))
        attn = jnp.einsum('bhtk,bhsk,bhtsk->bhts', qb, kb, decay)
        o_intra = jnp.einsum('bhts,bhsv->bhtv', attn, vb)
        b_last = b[:, :, -1:, :]
        S_new = jnp.exp(b_last[:, :, 0, :])[..., None] * S0 + jnp.einsum(
            'bhsk,bhsv->bhkv', kb * jnp.exp(b_last - b), vb)
        return S_new, o_inter + o_intra

    S0 = jnp.zeros((B, H, K, V), jnp.float32)
    _, o = lax.scan(step, S0, (chunks(q), chunks(k), chunks(logf), chunks(v)))
    return o.transpose(1, 2, 0, 3, 4).reshape(B, H, S, V)


def _hgrn2_bidir(q, f_fwd, f_bwd, i, g, lb, gnorm_g):
    B, S, _ = q.shape
    f32 = jnp.float32

    def heads(t, d):
        return t.reshape(B, S, HG_HEADS, d).transpose(0, 2, 1, 3).astype(f32)

    qh = jax.nn.silu(heads(q, HG_KDIM))
    vh = heads(i, HG_VDIM)
    lbh = lb.astype(f32).reshape(2, HG_HEADS, 1, HG_KDIM)

    def gate(fpre, lbd):
        fg = lbd + (1.0 - lbd) * jax.nn.sigmoid(heads(fpre, HG_KDIM))
        return 1.0 - fg, jnp.log(fg)

    k_f, lf_f = gate(f_fwd, lbh[0])
    k_b, lf_b = gate(f_bwd, lbh[1])
    o_f = _gla_chunk_scan(qh, k_f, lf_f, vh)
    rev = lambda t: jnp.flip(t, axis=2)
    o_b = rev(_gla_chunk_scan(rev(qh), rev(k_b), rev(lf_b), rev(vh)))
    o = (o_f + o_b).transpose(0, 2, 1, 3)
    o = _rmsnorm(o, gnorm_g) * jax.nn.silu(g.reshape(B, S, HG_HEADS, HG_VDIM).astype(f32))
    return o.reshape(B, S, HG_IDIM).astype(q.dtype)


def _partial_rotary(t, pos):
    half = ROT_DIM // 2
    inv = ROPE_THETA ** (-jnp.arange(half, dtype=jnp.float32) * 2.0 / ROT_DIM)
    ang = pos.astype(jnp.float32)[:, None] * inv[None, :]
    cos = jnp.cos(ang)[None, :, None, :]
    sin = jnp.sin(ang)[None, :, None, :]
    tr = t[..., :ROT_DIM].astype(jnp.float32)
    x1, x2 = tr[..., :half], tr[..., half:]
    rot = jnp.concatenate([x1 * cos - x2 * sin, x2 * cos + x1 * sin], axis=-1)
    return jnp.concatenate([rot.astype(t.dtype), t[..., ROT_DIM:]], axis=-1)


def _dilated_window_attention(q, k, v, span, dil):
    B, S, H, Dh = q.shape
    L = S // dil
    N = B * dil
    blk = span
    nb = -(-L // blk)
    Lp = nb * blk

    def residue(t):
        return t.reshape(B, L, dil, H, Dh).transpose(0, 2, 1, 3, 4).reshape(N, L, H, Dh)

    qr, kr, vr = residue(q), residue(k), residue(v)
    qb = jnp.pad(qr, ((0, 0), (0, Lp - L), (0, 0), (0, 0))).reshape(N, nb, blk, H, Dh)

    def kv_blocks(t):
        tp = jnp.pad(t, ((0, 0), (blk, Lp - L + blk), (0, 0), (0, 0))).reshape(N, nb + 2, blk, H, Dh)
        return jnp.concatenate([tp[:, :-2], tp[:, 1:-1], tp[:, 2:]], axis=2)

    kb, vb = kv_blocks(kr), kv_blocks(vr)
    m_q = jnp.arange(nb)[:, None, None] * blk + jnp.arange(blk)[None, :, None]
    m_k = jnp.arange(nb)[:, None, None] * blk - blk + jnp.arange(3 * blk)[None, None, :]
    mask = ((jnp.abs(m_k - m_q) <= span) & (m_k >= 0) & (m_k < L)) | (m_k == m_q)
    s = jnp.einsum('nbqhd,nbkhd->nbhqk', qb, kb).astype(jnp.float32) * (Dh ** -0.5)
    s = jnp.where(mask[None, :, None], s, -jnp.inf)
    lse = jax.nn.logsumexp(s, axis=-1)
    p = jnp.exp(s - lse[..., None])
    o = jnp.einsum('nbhqk,nbkhd->nbqhd', p.astype(v.dtype), vb).reshape(N, Lp, H, Dh)[:, :L]
    lse = lse.transpose(0, 1, 3, 2).reshape(N, Lp, H)[:, :L]
    o = o.reshape(B, dil, L, H, Dh).transpose(0, 2, 1, 3, 4).reshape(B, S, H, Dh)
    lse = lse.reshape(B, dil, L, H).transpose(0, 2, 1, 3).reshape(B, S, H)
    return o, lse


def _memory_cross_attention(u, mem_n, w_q, w_kv, w_o):
    B, S, _ = u.shape
    M = mem_n.shape[1]
    q = (u @ w_q).reshape(B, S, XA_HEADS, XA_HEAD_DIM)
    k, v = jnp.split(mem_n @ w_kv, 2, axis=-1)
    k = k.reshape(B, M, XA_HEADS, XA_HEAD_DIM)
    v = v.reshape(B, M, XA_HEADS, XA_HEAD_DIM)
    s = jnp.einsum('bshd,bmhd->bhsm', q, k).astype(jnp.float32) * (XA_HEAD_DIM ** -0.5)
    p = jax.nn.softmax(s, axis=-1)
    o = jnp.einsum('bhsm,bmhd->bshd', p.astype(v.dtype), v).reshape(B, S, D_MODEL)
    return o @ w_o


def _encode(x, mem, mix_norm_g, w_in, hgrn_lb_logits, hgrn_gnorm_g, w_hgrn_o, w_attn_o, w_out,
            xa_norm_g, mem_norm_g, w_xq, w_xkv, w_xo, ffn_norm_g, w_ffn1, w_ffn2, final_norm_g):
    B, S, _ = x.shape
    pos = jnp.arange(S)
    lb_all = jnp.cumsum(jax.nn.softmax(hgrn_lb_logits.astype(jnp.float32), axis=1), axis=1)
    h = x
    for l in range(DEPTH):
        u = _rmsnorm(h, mix_norm_g[l])
        parts = jnp.split(u @ w_in[l], IN_SPLITS, axis=-1)
        hq, hf_f, hf_b, hi, hg = parts[:5]
        attn_parts = parts[5:5 + 3 * len(ATTN_GROUPS)]
        gate_h, gate_a = parts[5 + 3 * len(ATTN_GROUPS):]

        y_h = _hgrn2_bidir(hq, hf_f, hf_b, hi, hg, lb_all[:, l], hgrn_gnorm_g[l]) @ w_hgrn_o[l]

        outs, lses = [], []
        for gi, (win, dil) in enumerate(ATTN_GROUPS):
            qg, kg, vg = [t.reshape(B, S, ATTN_HEADS, ATTN_HEAD_DIM) for t in attn_parts[3 * gi:3 * gi + 3]]
            o, lse = _dilated_window_attention(_partial_rotary(qg, pos), _partial_rotary(kg, pos), vg,
                                               (win // 2) // dil, dil)
            outs.append(o)
            lses.append(lse)
        wts = jax.nn.softmax(jnp.stack(lses), axis=0)
        attn = jnp.einsum('gbsh,gbshd->bshd', wts, jnp.stack(outs).astype(jnp.float32))
        y_a = attn.reshape(B, S, ATTN_WIDTH).astype(h.dtype) @ w_attn_o[l]

        merged = jax.nn.sigmoid(gate_h) * y_h + jax.nn.sigmoid(gate_a) * y_a
        h = h + merged @ w_out[l]
        h = h + _memory_cross_attention(_rmsnorm(h, xa_norm_g[l]), _rmsnorm(mem, mem_norm_g[l]),
                                        w_xq[l], w_xkv[l], w_xo[l])
        u = _rmsnorm(h, ffn_norm_g[l])
        h = h + jnp.square(jax.nn.relu(u @ w_ffn1[l])) @ w_ffn2[l]
    return _rmsnorm(h, final_norm_g)


def setup_inputs(seed: int = 0) -> dict:
    key = jax.random.key(seed)
    ks = jax.random.split(key, 24)
    f32 = jnp.float32

    def w(k, shape, fan_in):
        return jax.random.normal(k, shape, f32) * (fan_in ** -0.5)

    def gain(k, shape):
        return 1.0 + 0.02 * jax.random.normal(k, shape, f32)

    return {
        "x_prompt": jax.random.normal(ks[0], (BATCH, SEQ, D_MODEL), f32),
        "x_sample": jax.random.normal(ks[1], (DEC_BATCH, DEC_SEQ, D_MODEL), f32),
        "mem_prompt": jax.random.normal(ks[2], (BATCH, N_MEM, D_MODEL), f32),
        "mem_sample": jax.random.normal(ks[3], (DEC_BATCH, N_MEM, D_MODEL), f32),
        "mix_norm_g": gain(ks[4], (DEPTH, D_MODEL)),
        "w_in": w(ks[5], (DEPTH, D_MODEL, N_IN), D_MODEL),
        "hgrn_lb_logits": 0.5 * jax.random.normal(ks[6], (2, DEPTH + 1, HG_FDIM), f32),
        "hgrn_gnorm_g": gain(ks[7], (DEPTH, HG_VDIM)),
        "w_hgrn_o": w(ks[8], (DEPTH, HG_IDIM, D_MODEL), HG_IDIM),
        "w_attn_o": w(ks[9], (DEPTH, ATTN_WIDTH, D_MODEL), ATTN_WIDTH),
        "w_out": w(ks[10], (DEPTH, D_MODEL, D_MODEL), D_MODEL),
        "xa_norm_g": gain(ks[11], (DEPTH, D_MODEL)),
        "mem_norm_g": gain(ks[12], (DEPTH, D_MODEL)),
        "w_xq": w(ks[13], (DEPTH, D_MODEL, XA_HEADS * XA_HEAD_DIM), D_MODEL),
        "w_xkv": w(ks[14], (DEPTH, D_MODEL, 2 * XA_HEADS * XA_HEAD_DIM), D_MODEL),
        "w_xo": w(ks[15], (DEPTH, XA_HEADS * XA_HEAD_DIM, D_MODEL), XA_HEADS * XA_HEAD_DIM),
        "ffn_norm_g": gain(ks[16], (DEPTH, D_MODEL)),
        "w_ffn1": w(ks[17], (DEPTH, D_MODEL, D_FF), D_MODEL),
        "w_ffn2": w(ks[18], (DEPTH, D_FF, D_MODEL), D_FF),
        "final_norm_g": gain(ks[19], (D_MODEL,)),
    }


def reference(x_prompt, x_sample, mem_prompt, mem_sample, mix_norm_g, w_in, hgrn_lb_logits, hgrn_gnorm_g,
              w_hgrn_o, w_attn_o, w_out, xa_norm_g, mem_norm_g, w_xq, w_xkv, w_xo, ffn_norm_g, w_ffn1,
              w_ffn2, final_norm_g):
    y_prompt = _encode(x_prompt, mem_prompt, mix_norm_g, w_in, hgrn_lb_logits, hgrn_gnorm_g, w_hgrn_o,
                       w_attn_o, w_out, xa_norm_g, mem_norm_g, w_xq, w_xkv, w_xo, ffn_norm_g, w_ffn1,
                       w_ffn2, final_norm_g)
    y_sample = _encode(x_sample, mem_sample, mix_norm_g, w_in, hgrn_lb_logits, hgrn_gnorm_g, w_hgrn_o,
                       w_attn_o, w_out, xa_norm_g, mem_norm_g, w_xq, w_xkv, w_xo, ffn_norm_g, w_ffn1,
                       w_ffn2, final_norm_g)
    return (y_prompt, y_sample)
```

```python
import numpy as np
from contextlib import ExitStack
import concourse.bass as bass
import concourse.mybir as mybir
from concourse.bass_utils import run_bass_kernel_spmd

F32 = mybir.dt.float32
BF16 = mybir.dt.bfloat16
AF = mybir.ActivationFunctionType
ALU = mybir.AluOpType

D = 1024
T = 512
NIN = 11776
NMEM = 256
EPS = 1e-6
NSLOT = 4
GROUPS = (1, 4, 16)
ATT_SCALE = 128 ** -0.5
XA_SCALE = 256 ** -0.5


class Buf:
    __slots__ = ("name", "w", "r", "excl")

    def __init__(self, name, excl=False):
        self.name = name
        self.w = None
        self.r = {}
        self.excl = excl


class Sched:
    def __init__(self, same_engine_raw=True):
        self.streams = {e: [] for e in ("pe", "act", "dve", "pool", "sp")}
        self.chan = {}
        self.ser = same_engine_raw
        self.disabled = False

    def add(self, eng, fn, reads=(), writes=(), chan=None, extra=None):
        if self.disabled:
            return None
        deps = dict(extra) if extra else {}
        mychan = None if chan is None else "#" + chan

        def need(a, raw):
            if a is None:
                return
            s, i = a
            if s == eng:
                if not raw or not self.ser or eng in ("pe", "sp"):
                    return
            if mychan is not None and s == mychan:
                return
            if deps.get(s, 0) < i:
                deps[s] = i

        for b in reads:
            need(b.w, True)
            if b.excl:
                for s, i in b.r.items():
                    need((s, i), False)
        for b in writes:
            need(b.w, False)
            for s, i in b.r.items():
                need((s, i), False)
        lst = self.streams[eng]
        rec = [fn, deps, False, chan, 0]
        lst.append(rec)
        if chan is None:
            me = (eng, len(lst))
        else:
            c = self.chan.get(chan, 0) + 1
            self.chan[chan] = c
            me = (mychan, c)
        for b in reads:
            if b.r.get(me[0], 0) < me[1]:
                b.r[me[0]] = me[1]
        for b in writes:
            b.w = me
            b.r = {}
        return me

    def finalize(self):
        for lst in self.streams.values():
            for rec in lst:
                for s, i in rec[1].items():
                    if s[0] != "#":
                        self.streams[s][i - 1][2] = True
        for lst in self.streams.values():
            cum = 0
            for rec in lst:
                if rec[2]:
                    cum += 1
                rec[4] = cum

    def emit(self, eng, e, sems, chansems, final_wait=False):
        seen = {}
        for rec in self.streams[eng]:
            fn, deps, marked, chan, _ = rec
            for s, i in deps.items():
                if s[0] == "#":
                    sem = chansems[s[1:]]
                    v = 16 * i
                else:
                    sem = sems[s]
                    v = self.streams[s][i - 1][4]
                if seen.get(s, 0) >= v:
                    continue
                seen[s] = v
                e.wait_ge(sem, v)
            ins = fn(e)
            if chan is not None:
                ins.then_inc(chansems[chan], 16)
            elif marked:
                ins.then_inc(sems[eng], 1)
        if final_wait:
            for c, n in self.chan.items():
                e.wait_ge(chansems[c], 16 * n)


def weight_blocks():
    blocks = []
    index = {}

    def addw(key, src, K, N):
        ids = []
        for nb in range(N // 512):
            for kg in range(max(1, K // 1024)):
                kc = min(8, K // 128)
                ids.append(len(blocks))
                blocks.append((src, kg * 1024, kc, nb * 512))
        index[key] = ids

    addw("in", "w_in", 1024, NIN)
    addw("ho", "w_hgrn_o", 1024, 1024)
    addw("ao", "w_attn_o", 512, 1024)
    addw("wo", "w_out", 1024, 1024)
    addw("xq", "w_xq", 1024, 1024)
    addw("xkv", "w_xkv", 1024, 2048)
    addw("xo", "w_xo", 1024, 1024)
    addw("f1", "w_ffn1", 1024, 4096)
    addw("f2", "w_ffn2", 4096, 1024)
    return blocks, index


class StopBuild(Exception):
    pass


class Kern:
    def __init__(self, S, limit=99, debug=False):
        self.limit = limit
        self.debug = debug
        self.dbg_outs = []
        self.stepno = 0
        self.S = S
        self.NT = S // T
        self.nc = bass.Bass("TRN2", target_bir_lowering=False)
        self.P = Sched()
        self.es = ExitStack()
        self.blocks, self.widx = weight_blocks()

    def sb(self, name, shape, dt):
        return self.es.enter_context(self.nc.sbuf_tensor("sb_" + name, shape, dt))

    def din(self, name, shape, dt=F32):
        return self.nc.dram_tensor(name, shape, dt, kind="ExternalInput").ap()

    def mm(self, out, lhsT, rhs, start, stop, reads, writes, skip=False):
        self.P.add("pe", lambda e: e.matmul(out, lhsT, rhs, start=start, stop=stop, skip_group_check=skip),
                   reads, writes)

    def tr(self, out, in_, ident, reads, writes):
        self.P.add("pe", lambda e: e.transpose(out, in_, ident), reads, writes)

    def act(self, out, in_, func, reads, writes, bias=None, scale=None, eng="act"):
        kw = {}
        if bias is not None:
            kw["bias"] = bias
        if scale is not None:
            kw["scale"] = scale
        self.P.add("act", lambda e: e.activation(out, in_, func, **kw), reads, writes)

    def tt(self, out, in0, in1, op, reads, writes, eng="dve"):
        self.P.add(eng, lambda e: e.tensor_tensor(out, in0, in1, op), reads, writes)

    def ts(self, out, in0, s1, s2, op0, op1, reads, writes, eng="dve"):
        if op1 is None:
            self.P.add(eng, lambda e: e.tensor_scalar(out, in0, s1, None, op0), reads, writes)
        else:
            self.P.add(eng, lambda e: e.tensor_scalar(out, in0, s1, s2, op0, op1), reads, writes)

    def stt(self, out, in0, scalar, in1, op0, op1, reads, writes):
        self.P.add("dve", lambda e: e.scalar_tensor_tensor(out, in0, scalar, in1, op0, op1), reads, writes)

    def copy(self, out, in_, reads, writes, eng="dve"):
        if eng == "act":
            self.P.add("act", lambda e: e.activation(out, in_, AF.Copy), reads, writes)
        else:
            self.P.add(eng, lambda e: e.tensor_copy(out, in_), reads, writes)

    def memset(self, ap, val, writes, eng="dve"):
        self.P.add(eng, lambda e: e.memset(ap, val), (), writes)

    def dma(self, out, in_, reads, writes, chan, q="sp"):
        self.P.add(q, lambda e: e.dma_start(out=out, in_=in_), reads, writes, chan=chan)

    def step(self, name):
        self.stepno += 1
        self.P.disabled = (self.stepno > self.limit) or (self.stepno in _SKIP)
        if self.debug:
            print("step", self.stepno, name)

    def dbg(self, name, ap, bufs, dt=F32):
        if not self.debug:
            return
        shape = list(ap.shape)
        t = self.nc.dram_tensor(name, shape, dt, kind="ExternalOutput").ap()
        self.dbg_outs.append(name)
        self.dma(t, ap, bufs, (), chan="dbg", q="pool")

    def bank(self):
        i = self.free_banks.pop(0)
        return i

    def release(self, i):
        self.free_banks.append(i)

    def wseq_build(self):
        W = self.widx
        seq = []
        p1 = [W["in"][b] for b in (4, 0, 5, 1, 6, 7, 11, 12, 14, 15, 17, 18)]
        for _ in range(self.NT):
            seq += p1
        p2 = [W["in"][b] for b in (2, 0, 3, 1, 8, 9, 10, 13, 16)]
        for b in range(2):
            p2 += [W["ho"][b], W["in"][19 + b], W["ao"][b], W["in"][21 + b]]
        p2 += W["wo"] + W["xq"] + W["xo"] + W["f1"] + W["f2"]
        for _ in range(self.NT):
            seq += p2
        self.wseq = seq
        self.wpos = 0
        self.wissued = 0
        self.wdone = 0

    def w_issue(self):
        i = self.wissued
        slot = i % NSLOT
        blk = self.wseq[i]
        kc = self.blocks[blk][2]
        self.dma(self.wring[:, slot, 0:kc * 512], self.wbf[blk, :, 0:kc * 512], [self.wbfB[self.wgroup[blk]]],
                 [self.wringB[slot]], chan="w%d" % slot)
        self.wissued += 1

    def w_prefetch(self):
        while self.wissued < len(self.wseq) and self.wissued - NSLOT < self.wdone:
            self.w_issue()

    def w_next(self, blk):
        i = self.wpos
        assert self.wseq[i] == blk, (i, self.wseq[i], blk)
        self.w_prefetch()
        assert self.wissued > i, (i, self.wissued, self.wdone)
        slot = i % NSLOT
        self.wpos += 1
        kc = self.blocks[blk][2]
        view = self.wring[:, slot, 0:kc * 512].rearrange("p (c n) -> p c n", c=kc)
        return view, self.wringB[slot], kc

    def w_done(self):
        self.wdone = self.wpos
        self.w_prefetch()

    def build(self):
        nc, P, S, NT = self.nc, self.P, self.S, self.NT
        x = self.din("x", [S, D])
        mem = self.din("mem", [NMEM, D])
        wsrc = {
            "w_in": self.din("w_in", [D, NIN]), "w_hgrn_o": self.din("w_hgrn_o", [D, D]),
            "w_attn_o": self.din("w_attn_o", [512, D]), "w_out": self.din("w_out", [D, D]),
            "w_xq": self.din("w_xq", [D, D]), "w_xkv": self.din("w_xkv", [D, 2 * D]),
            "w_xo": self.din("w_xo", [D, D]), "w_ffn1": self.din("w_ffn1", [D, 4 * D]),
            "w_ffn2": self.din("w_ffn2", [4 * D, D]),
        }
        gains_d = self.din("gains", [128, 5, 8])
        gng_d = self.din("gng", [128, 1])
        lbl_d = self.din("lbl", [128, 2, 2, 8])
        rope_d = self.din("rope", [2, 32, S])
        identf_d = self.din("identf", [128, 128])
        perm_d = self.din("perm", [32, 32])
        hmask_d = self.din("hmask", [64, 2, 512])
        amask_d = self.din("amask", [128, 6, 512])
        y = nc.dram_tensor("y", [S, D], F32, kind="ExternalOutput").ap()
        nblk = len(self.blocks)
        obs = nc.dram_tensor("obs", [NT, 128, 8 * T], F32, kind="Internal").ap()
        vhs = nc.dram_tensor("vhs", [NT, 64, 8 * D], BF16, kind="Internal").ap()
        kts = nc.dram_tensor("kts", [3, 4, 128, S], BF16, kind="Internal").ap()
        vsc = nc.dram_tensor("vsc", [3, S, 512], BF16, kind="Internal").ap()
        self.wbf = nc.dram_tensor("wbf", [nblk, 128, 4096], BF16, kind="Internal").ap()

        hT = self.sb("hT", [128, 8, T], F32)
        hTB = [Buf("hT%d" % c) for c in range(8)]
        uT = self.sb("uT", [128, 8, T], BF16)
        uTBs = [Buf("uT%d" % c) for c in range(8)]
        UT = "uT-per-chunk"
        self.wring = self.sb("wring", [128, NSLOT, 4096], BF16)
        self.wringB = [Buf("wr%d" % i) for i in range(NSLOT)]
        identf = self.sb("identf", [128, 128], F32)
        identb = self.sb("identb", [128, 128], BF16)
        onesb = self.sb("onesb", [128, 128], BF16)
        permb = self.sb("permb", [32, 32], BF16)
        hmask = self.sb("hmask", [64, 2, 512], F32)
        amask = self.sb("amask", [128, 6, 512], BF16)
        gains = self.sb("gains", [128, 5, 8], F32)
        gng = self.sb("gng", [128, 1], F32)
        lbt = self.sb("lbt", [128, 2, 2, 8], F32)
        lb = self.sb("lb", [128, 2, 8], F32)
        oml = self.sb("oml", [128, 2, 8], F32)
        noml = self.sb("noml", [128, 2, 8], F32)
        epsn = self.sb("epsn", [128, 1], F32)
        constB = Buf("consts")
        rope = self.sb("rope", [32, 2, T], F32)
        ropeB = Buf("rope")
        sqb = self.sb("sqb", [128, 2, T], BF16)
        sqbB = [Buf("sqb0"), Buf("sqb1")]
        rs = self.sb("rs", [128, T], F32)
        rsB = Buf("rs")
        rinvn = self.sb("rinvn", [128, T], F32)
        rinvnB = Buf("rinvn")
        ym = self.sb("ym", [128, 16 * T], BF16)
        yin = ym[:, 0:8 * T].rearrange("p (c n) -> p c n", c=8)
        yinB = Buf("yin")
        obl = self.sb("obl", [128, 2, T], F32)
        oblB = [Buf("obl0"), Buf("obl1")]
        sgt = self.sb("sgt", [128, T], F32)
        sgtB = Buf("sgt")
        merged = ym[:, 8 * T:16 * T].rearrange("p (c n) -> p c n", c=8)
        xs = ym[:, :].bitcast(F32).rearrange("p (s d) -> p s d", s=4)
        xsB = Buf("xs")
        mergedB = Buf("merged")
        gtmp = self.sb("gtmp", [128, 2, T], F32)
        gtmpB = [Buf("gtmp0"), Buf("gtmp1")]
        Sst = self.sb("Sst", [128, 8, 128], F32)
        SstB = [Buf("Sst%d" % i) for i in range(8)]
        kxT = self.sb("kxT", [128, 8, NMEM], BF16)
        vx = self.sb("vx", [128, 2, D], BF16)
        kvxB = Buf("kvx")

        ARENA_B = 92672
        arena = self.sb("arena", [128, ARENA_B // 2], BF16)

        def carve(off, shape, dt, parts=128):
            n = int(np.prod(shape[1:]))
            esz = 4 if dt == F32 else 2
            assert off % 4 == 0
            assert off + n * esz <= ARENA_B, (off, n * esz)
            v = arena[0:parts, off // 2: off // 2 + n * esz // 2]
            if dt == F32:
                v = v.bitcast(F32)
            if len(shape) == 3:
                v = v.rearrange("p (a b) -> p a b", a=shape[1])
            elif len(shape) == 4:
                v = v.rearrange("p (a b c) -> p a b c", a=shape[1], b=shape[2])
            return v, off + n * esz

        banks = [self.es.enter_context(nc.psum_tensor("bank%d" % i, [128, 512], F32)) for i in range(8)]
        bankB = [Buf("bank%d" % i, excl=True) for i in range(8)]
        self.free_banks = list(range(8))

        K = 1024
        o = 0
        h_sf, o = carve(o, [128, 2, T], F32)
        h_kk, o = carve(o, [128, 2, T], F32)
        h_R, o = carve(o, [128, 2, T], F32)
        h_sq = h_R
        h_rinv4, o = carve(o, [128, 4, T], F32)
        h_KT, o = carve(o, [128, 8, T], BF16)
        h_QT, o = carve(o, [128, 8, T], BF16)
        h_Ktok, o = carve(o, [128, 8, 8, 128], BF16)
        h_V, o = carve(o, [128, 8, D], BF16)
        h_Sbf, o = carve(o, [128, 2, 8 * 128], BF16)
        h_Asb, o = carve(o, [128, 2, 8 * 64], BF16)
        h_eB, o = carve(o, [128, 8, 8], F32)
        h_OT, o = carve(o, [128, 8, T], F32)
        h_reb, o = carve(o, [128, 8, 8], F32)
        h_sfB = [Buf("h_sf0"), Buf("h_sf1")]
        h_kkB = [Buf("h_kk0"), Buf("h_kk1")]
        h_RB = [Buf("h_R0"), Buf("h_R1")]
        h_sqB = h_RB
        h_rinvB = [Buf("h_rinv%d" % i) for i in range(4)]
        h_KTB = [Buf("h_KT%d" % i) for i in range(8)]
        h_QTB = [Buf("h_QT%d" % i) for i in range(8)]
        h_KtokB = [Buf("h_Ktok%d" % i) for i in range(8)]
        h_VB = Buf("h_V")
        h_SbfB = [Buf("h_Sbf0"), Buf("h_Sbf1")]
        h_AsbB = [Buf("h_Asb0"), Buf("h_Asb1")]
        h_eBB = [Buf("h_eB%d" % i) for i in range(8)]
        h_rebB = [Buf("h_reb%d" % i) for i in range(8)]
        h_OTBs = [Buf("h_OT%d" % i) for i in range(8)]
        stageH = h_rebB + [h_VB] + h_OTBs + h_sfB + h_kkB + h_RB + h_SbfB + h_AsbB + h_rinvB + h_KTB + h_QTB + h_KtokB + h_eBB
        o = 0
        a_QT, o = carve(o, [128, 12, T], BF16)
        KTW = 640 + 1024 + 2560
        a_KT, o = carve(o, [128, 2, KTW], BF16)
        a_V, o = carve(o, [128, 2, 45, 128], BF16)
        a_PT, o = carve(o, [128, 2, T], BF16)
        a_out, o = carve(o, [128, 4, T], BF16)
        a_rd, o = carve(o, [128, T], F32)
        a_rot, o = carve(o, [128, 4, T], F32, parts=32)
        a_kst, o = carve(o, [128, 4, T], BF16)
        a_vst, o = carve(o, [128, 4, 512], BF16)
        a_QTB = [Buf("a_QT%d" % i) for i in range(12)]
        a_KTB = [Buf("a_KTw0"), Buf("a_KTw1")]
        a_VB = [Buf("a_Vw0"), Buf("a_Vw1")]
        a_PTB = [Buf("a_PT0"), Buf("a_PT1")]
        a_outB = Buf("a_out")
        a_rdB = Buf("a_rd")
        a_rotB = [Buf("a_rot0"), Buf("a_rot1")]
        a_kstB = [Buf("a_kst%d" % i) for i in range(4)]
        a_vstB = [Buf("a_vst%d" % i) for i in range(4)]
        stageA = a_QTB + a_KTB + a_VB + a_PTB + [a_outB, a_rdB] + a_rotB + a_kstB + a_vstB
        o = 0
        c_qx, o = carve(o, [128, 8, T], BF16)
        c_PT, o = carve(o, [128, 4, T], BF16)
        c_ox, o = carve(o, [128, 8, T], BF16)
        c_rd, o = carve(o, [128, T], F32)
        c_rl, o = carve(o, [128, 2, T], F32)
        c_hid, o = carve(o, [128, 32, T], BF16)
        c_yT, o = carve(o, [128, 2, T], F32)
        c_ysb, o = carve(o, [128, 4, 512], F32)
        c_qxB = Buf("c_qx")
        c_PTB = [Buf("c_PT%d" % i) for i in range(4)]
        c_oxB = Buf("c_ox")
        c_rdB = Buf("c_rd")
        c_rlB = [Buf("c_rl0"), Buf("c_rl1")]
        c_hidB = Buf("c_hid")
        c_yTB = [Buf("c_yT0"), Buf("c_yT1")]
        c_ysbB = [Buf("c_ysb%d" % i) for i in range(4)]
        stageC = [c_qxB, c_oxB, c_rdB, c_hidB] + c_PTB + c_rlB + c_yTB + c_ysbB
        o = 0
        p_st, o = carve(o, [128, 2, 4096], F32)
        p_bf, o = carve(o, [128, 2, 4096], BF16)
        p_stB = [Buf("p_st0"), Buf("p_st1")]
        p_bfB = [Buf("p_bf0"), Buf("p_bf1")]
        p_mem, o = carve(o, [128, 2, D], F32)
        p_memB = Buf("p_mem")
        stageP = p_stB + p_bfB + [p_memB]
        allstages = {"H": stageH, "A": stageA, "C": stageC, "P": stageP}
        self.cur_stage = [None]

        def enter_stage(name):
            if self.cur_stage[0] == name:
                return
            self.cur_stage[0] = name
            acc_r = {}
            for sn, lst in allstages.items():
                if sn == name:
                    continue
                for b in lst:
                    for s, i in b.r.items():
                        if acc_r.get(s, 0) < i:
                            acc_r[s] = i
                    if b.w is not None:
                        s, i = b.w
                        if acc_r.get(s, 0) < i:
                            acc_r[s] = i
            for b in allstages[name]:
                for s, i in acc_r.items():
                    if b.r.get(s, 0) < i:
                        b.r[s] = i

        def norm_sq(c, N=T):
            sl = c % 2
            self.act(sqb[:, sl, 0:N], hT[:, c, 0:N], AF.Square, [hTB[c]], [sqbB[sl]])

        def norm_mm(nb, c, N=T):
            sl = c % 2
            self.mm(banks[nb][:, 0:N], onesb[:, :], sqb[:, sl, 0:N], c == 0, c == 7, [sqbB[sl], constB], [bankB[nb]])

        def norm_acc(nb, c, N=T):
            norm_sq(c, N)
            norm_mm(nb, c, N)

        def norm_end(nb, gidx, N=T):
            self.act(rs[:, 0:N], banks[nb][:, 0:N], AF.Ln, [bankB[nb], constB], [rsB], bias=epsn[:, 0:1],
                     scale=1.0 / D)
            self.release(nb)
            self.act(rinvn[:, 0:N], rs[:, 0:N], AF.Exp, [rsB], [rinvnB], scale=-0.5)
            for c in range(8):
                self.stt(uT[:, c, 0:N], hT[:, c, 0:N], gains[:, gidx, c:c + 1], rinvn[:, 0:N], ALU.mult, ALU.mult,
                         [hTB[c], rinvnB, constB], [uTBs[c]])

        def rmsnorm(gidx, N=T):
            nb = self.bank()
            for c in range(8):
                norm_acc(nb, c, N)
            norm_end(nb, gidx, N)

        def alias_fence(dst, src):
            acc = {}
            for bb in src:
                for s_, i_ in list(bb.r.items()) + ([bb.w] if bb.w is not None else []):
                    if acc.get(s_, 0) < i_:
                        acc[s_] = i_
            for bb in dst:
                for s_, i_ in acc.items():
                    if bb.r.get(s_, 0) < i_:
                        bb.r[s_] = i_

        def issue_x(ti):
            alias_fence([xsB], [yinB, mergedB])
            t0 = ti * T
            self.dma(xs[:, :, :], x[t0:t0 + T, :].rearrange("(s p) d -> p s d", p=128), (), [xsB], chan="xs")

        def load_x(ti):
            nb = self.bank()
            for c in range(8):
                b = self.bank()
                for s4 in range(4):
                    self.tr(banks[b][:, s4 * 128:(s4 + 1) * 128], xs[:, s4, c * 128:(c + 1) * 128], identf[:, :],
                            [xsB, constB], [bankB[b]])
                if c > 0:
                    norm_mm(nb, c - 1)
                self.copy(hT[:, c, :], banks[b][:, :], [bankB[b]], [hTB[c]], eng="act")
                self.release(b)
                norm_sq(c)
            norm_mm(nb, 7)
            norm_end(nb, 0)

        def proj_fm(wv, wB, kc, j, rhs_of, rhsB, N=T):
            b = self.bank()
            for c in range(kc):
                self.mm(banks[b][:, 0:N], wv[:, c, j * 128:(j + 1) * 128], rhs_of(c), c == 0, c == kc - 1,
                        [wB] + ([uTBs[c]] if rhsB is UT else rhsB), [bankB[b]])
            return b

        uT_of = lambda c: uT[:, c, :]

        def hgrn_prep(direction, ti):
            enter_stage("H")
            self.memset(h_kk[:, :, :], 0.0, h_kkB, eng="pool")
            fblk0 = 2 if direction == 0 else 4

            def f_front(h, j, b):
                p = h % 2
                self.act(h_sf[:, p, :], banks[b][:, :], AF.Sigmoid, [bankB[b]], [h_sfB[p]])
                self.release(b)
                self.act(h_sf[:, p, :], h_sf[:, p, :], AF.Identity, [h_sfB[p], constB], [h_sfB[p]],
                         bias=lb[:, direction, h:h + 1], scale=oml[:, direction, h:h + 1])
                fg = h_sf[:, p, :]
                fgv = fg.rearrange("q (c t) -> q c t", c=8)
                d1s = h_kk[:, 0, :]
                d1e = h_kk[:, 1, :]
                self.copy(d1s.rearrange("q (c t) -> q c t", c=8)[:, :, 0], fgv[:, :, 0], [h_sfB[p]], [h_kkB[0]])
                self.copy(d1e.rearrange("q (c t) -> q c t", c=8)[:, :, 63], fgv[:, :, 63], [h_sfB[p]], [h_kkB[1]])
                if direction == 0:
                    Ppre, PpreB, Psuf, PsufB = h_rinv4[:, j, :], h_rinvB[j], h_R[:, p, :], h_RB[p]
                else:
                    Ppre, PpreB, Psuf, PsufB = h_R[:, p, :], h_RB[p], h_rinv4[:, j, :], h_rinvB[j]
                self.P.add("dve", lambda e: e.tensor_tensor_scan(Ppre, fg, d1s, 1.0, ALU.mult, ALU.max),
                           [h_sfB[p], h_kkB[0]], [PpreB])
                self.P.add("dve", lambda e: e.tensor_tensor_scan(Psuf[:, ::-1], fg[:, ::-1], d1e[:, ::-1], 1.0,
                                                                 ALU.mult, ALU.max), [h_sfB[p], h_kkB[1]], [PsufB])
                Pprev = Ppre.rearrange("q (c t) -> q c t", c=8)
                Psufv = Psuf.rearrange("q (c t) -> q c t", c=8)
                KTv = h_KT[:, h, :].rearrange("q (c t) -> q c t", c=8)
                if direction == 0:
                    self.copy(h_eB[:, h, :], Pprev[:, :, 63], [PpreB], [h_eBB[h]])
                    self.tt(KTv[:, :, 0:63], Psufv[:, :, 1:64], Psufv[:, :, 0:63], ALU.subtract, [PsufB], [h_KTB[h]],
                            eng="pool")
                    self.ts(KTv[:, :, 63:64], Psufv[:, :, 63:64], -1.0, 1.0, ALU.mult, ALU.add, [PsufB], [h_KTB[h]],
                            eng="pool")
                else:
                    self.copy(h_eB[:, h, :], Psufv[:, :, 0], [PsufB], [h_eBB[h]])
                    self.tt(KTv[:, :, 1:64], Pprev[:, :, 0:63], Pprev[:, :, 1:64], ALU.subtract, [PpreB], [h_KTB[h]],
                            eng="pool")
                    self.ts(KTv[:, :, 0:1], Pprev[:, :, 0:1], -1.0, 1.0, ALU.mult, ALU.add, [PpreB], [h_KTB[h]],
                            eng="pool")
                self.P.add("dve", lambda e: e.reciprocal(h_reb[:, h, :], h_eB[:, h, :]), [h_eBB[h]], [h_rebB[h]])

            def f_back(h):
                for half in range(2):
                    bt = self.bank()
                    btv = banks[bt][:, :].bitcast(BF16)
                    for cc in range(4):
                        c = half * 4 + cc
                        self.tr(btv[0:64, cc * 128:(cc + 1) * 128], h_KT[:, h, c * 64:(c + 1) * 64], identb[:, :],
                                [h_KTB[h], constB], [bankB[bt]])
                    self.copy(h_Ktok[0:64, h, half * 4:half * 4 + 4, :],
                              btv[0:64, 0:512].rearrange("q (c k) -> q c k", c=4), [bankB[bt]], [h_KtokB[h]],
                              eng="act")
                    self.release(bt)
                self.tt(h_KT[:, h, :].rearrange("q (c t) -> q c t", c=8), h_KT[:, h, :].rearrange("q (c t) -> q c t", c=8),
                        h_reb[:, h, :].unsqueeze(2).broadcast_to([128, 8, 64]), ALU.mult, [h_KTB[h], h_rebB[h]],
                        [h_KTB[h]])

            for g in range(2):
                wv, wB, kc = self.w_next(self.widx["in"][fblk0 + g])
                for j in range(4):
                    h = g * 4 + j
                    b = proj_fm(wv, wB, kc, j, uT_of, UT)
                    f_front(h, j, b)
                self.w_done()
                wv, wB, kc = self.w_next(self.widx["in"][0 + g])
                for j in range(4):
                    h = g * 4 + j
                    p = h % 2
                    b = proj_fm(wv, wB, kc, j, uT_of, UT)
                    f_back(h)
                    self.act(h_sq[:, p, :], banks[b][:, :], AF.Silu, [bankB[b]], [h_sqB[p]])
                    self.release(b)
                    self.tt(h_QT[:, h, :], h_sq[:, p, :], h_rinv4[:, j, :], ALU.mult, [h_sqB[p], h_rinvB[j]],
                            [h_QTB[h]], eng="pool")
                self.w_done()
            if direction == 0:
                self.dma(h_V[0:64, :, :].rearrange("q c n -> q (c n)"), vhs[ti, :, :], (), [h_VB], chan="vhl")
            for g in (range(2) if direction == 1 else ()):
                wv, wB, kc = self.w_next(self.widx["in"][6 + g])
                for c in range(8):
                    b = self.bank()
                    for kc_ in range(8):
                        self.mm(banks[b][0:64, :], uT[:, kc_, c * 64:(c + 1) * 64], wv[:, kc_, :], kc_ == 0, kc_ == 7,
                                [wB, uTBs[kc_]], [bankB[b]])
                    self.copy(h_V[0:64, c, g * 512:(g + 1) * 512], banks[b][0:64, :], [bankB[b]], [h_VB],
                              eng=("act" if c % 2 else "dve"))
                    self.release(b)
                self.w_done()

        def hgrn_scan(direction):
            order = list(range(8)) if direction == 0 else list(range(7, -1, -1))
            for n, c in enumerate(order):
                p = n % 2
                self.copy(h_Sbf[:, p, :].rearrange("q (h v) -> q h v", h=8), Sst[:, :, :], SstB, [h_SbfB[p]])
                ba = self.bank()
                for h in range(8):
                    self.mm(banks[ba][0:64, h * 64:(h + 1) * 64], h_KT[:, h, c * 64:(c + 1) * 64],
                            h_QT[:, h, c * 64:(c + 1) * 64], True, True, [h_KTB[h], h_QTB[h]], [bankB[ba]])
                bks = []
                for half in range(2):
                    bk = self.bank()
                    bks.append(bk)
                    for hh in range(4):
                        h = half * 4 + hh
                        self.mm(banks[bk][:, hh * 128:(hh + 1) * 128], h_Ktok[0:64, h, c, :],
                                h_V[0:64, c, h * 128:(h + 1) * 128], True, True, [h_KtokB[h], h_VB], [bankB[bk]])
                self.tt(h_Asb[0:64, p, :], banks[ba][0:64, :], hmask[:, direction, :], ALU.mult, [bankB[ba], constB],
                        [h_AsbB[p]])
                self.release(ba)
                for half in range(2):
                    for hh in range(4):
                        h = half * 4 + hh
                        self.stt(Sst[:, h, :], Sst[:, h, :], h_eB[:, h, c:c + 1], banks[bks[half]][:, hh * 128:(hh + 1) * 128],
                                 ALU.mult, ALU.add, [SstB[h], h_eBB[h], bankB[bks[half]], h_SbfB[p]], [SstB[h]])
                    self.release(bks[half])
                bo = self.bank()
                for h in range(8):
                    self.mm(banks[bo][:, h * 64:(h + 1) * 64], h_Sbf[:, p, h * 128:(h + 1) * 128],
                            h_QT[:, h, c * 64:(c + 1) * 64], True, False, [h_SbfB[p], h_QTB[h]], [bankB[bo]])
                    self.mm(banks[bo][:, h * 64:(h + 1) * 64], h_V[0:64, c, h * 128:(h + 1) * 128],
                            h_Asb[0:64, p, h * 64:(h + 1) * 64], False, True, [h_VB, h_AsbB[p]], [bankB[bo]])
                self.copy(h_OT[:, :, c * 64:(c + 1) * 64], banks[bo][:, :].rearrange("q (h t) -> q h t", h=8),
                          [bankB[bo]], h_OTBs, eng="act")
                self.release(bo)

        def rot_A(b, dst, dstB, k):
            self.copy(dst, banks[b][:, :], [bankB[b]], [dstB], eng="act")
            self.tt(a_rot[0:32, 2 * k, :], banks[b][0:32, :], rope[:, 0, :], ALU.mult, [bankB[b], ropeB], [a_rotB[k]])
            self.release(b)

        def rot_B(dst, dstB, k):
            b2 = self.bank()
            self.mm(banks[b2][0:32, :], permb[:, :], dst[0:32, :], True, True, [dstB, constB], [bankB[b2]])
            self.tt(a_rot[0:32, 2 * k + 1, :], banks[b2][0:32, :], rope[:, 1, :], ALU.mult, [bankB[b2], ropeB],
                    [a_rotB[k]])
            self.release(b2)
            self.tt(dst[0:32, :], a_rot[0:32, 2 * k, :], a_rot[0:32, 2 * k + 1, :], ALU.add, [a_rotB[k]], [dstB])

        def load_rope(ti):
            t0 = ti * T
            self.dma(rope[:, :, :], rope_d[:, :, t0:t0 + T].rearrange("a p t -> p a t"), (), [ropeB], chan="rope")

        self.step('consts a')
        enter_stage("P")
        self.dma(identf[:, :], identf_d[:, :], (), [constB], chan="c0")
        self.dma(gains[:, :, :], gains_d[:, :, :], (), [constB], chan="c0")
        self.dma(gng[:, :], gng_d[:, :], (), [constB], chan="c0")
        self.dma(lbt[:, :, :, :], lbl_d[:, :, :, :], (), [constB], chan="c0")
        self.dma(hmask[:, :, :], hmask_d[:, :, :], (), [constB], chan="c0")
        self.step('consts b')
        self.dma(p_st[:, 0, 0:3072], amask_d[:, :, :].rearrange("p a t -> p (a t)"), (), [p_stB[0]], chan="pst0")
        self.copy(amask[:, :, :].rearrange("p a t -> p (a t)"), p_st[:, 0, 0:3072], [p_stB[0]], [constB])
        self.step('consts c')
        self.dma(p_st[0:32, 1, 0:32], perm_d[:, :], (), [p_stB[1]], chan="pst1")
        self.copy(permb[:, :], p_st[0:32, 1, 0:32], [p_stB[1]], [constB])
        self.step('consts d')
        self.copy(identb[:, :], identf[:, :], [constB], [constB])
        self.memset(onesb[:, :], 1.0, [constB])
        onesf = self.sb("onesf", [128, 64], F32)
        self.memset(onesf[:, :], 1.0, [constB])
        self.memset(epsn[:, :], EPS, [constB])
        epsg = self.sb("epsg", [128, 1], F32)
        self.memset(epsg[:, :], EPS, [constB])
        self.step('consts e')
        self.tt(lb[:, :, :], lbt[:, :, 0, :], lbt[:, :, 1, :], ALU.subtract, [constB], [constB])
        self.step('consts e2')
        self.act(lb[:, :, :], lb[:, :, :], AF.Sigmoid, [constB], [constB])
        self.step('consts e3')
        self.ts(oml[:, :, :], lb[:, :, :], -1.0, 1.0, ALU.mult, ALU.add, [constB], [constB])
        self.step('consts e4')
        self.ts(noml[:, :, :], oml[:, :, :], -1.0, None, ALU.mult, None, [constB], [constB])

        self.step('wconv')
        p1_blocks = [self.widx["in"][i] for i in (4, 0, 5, 1, 6, 7, 11, 12, 14, 15, 17, 18)] + list(self.widx["xkv"])
        self.wgroup = {bi: (0 if bi in p1_blocks else 1) for bi in range(len(self.blocks))}
        self.wbfB = [Buf("wbf_g0"), Buf("wbf_g1")]

        def convert(bi):
            src, r0, kc, c0 = self.blocks[bi]
            w = wsrc[src]
            g = self.wgroup[bi]
            self.dma(self.wbf[bi, :, 0:kc * 512].rearrange("p (c n) -> p c n", c=kc),
                     w[r0:r0 + kc * 128, c0:c0 + 512].rearrange("(c p) n -> p c n", p=128), (), [self.wbfB[g]],
                     chan="cv%d" % g, q="pool")

        for bi in list(self.widx["xkv"]) + p1_blocks[:12]:
            convert(bi)
        conv_later = [bi for bi in range(len(self.blocks)) if self.wgroup[bi] == 1]

        self.step('memkv')
        self.dma(p_mem[:, :, :], mem[:, :].rearrange("(s p) d -> p s d", p=128), (), [p_memB], chan="pmem")
        for c in range(8):
            b = self.bank()
            for s2 in range(2):
                self.tr(banks[b][:, s2 * 128:(s2 + 1) * 128], p_mem[:, s2, c * 128:(c + 1) * 128], identf[:, :],
                        [p_memB, constB], [bankB[b]])
            self.copy(hT[:, c, 0:NMEM], banks[b][:, 0:NMEM], [bankB[b]], [hTB[c]], eng="act")
            self.release(b)
        rmsnorm(2, N=NMEM)
        self.wseq_pro = list(self.widx["xkv"])
        for n, blk in enumerate(self.widx["xkv"]):
            slot = n % NSLOT
            self.dma(self.wring[:, slot, :], self.wbf[blk, :, :], [self.wbfB[0]], [self.wringB[slot]],
                     chan="w%d" % slot)
            wv = self.wring[:, slot, :].rearrange("p (c n) -> p c n", c=8)
            if n < 2:
                for j in range(4):
                    b = proj_fm(wv, self.wringB[slot], 8, j, lambda c: uT[:, c, 0:NMEM], UT, N=NMEM)
                    self.copy(kxT[:, n * 4 + j, :], banks[b][:, 0:NMEM], [bankB[b]], [kvxB], eng="act")
                    self.release(b)
            else:
                for s2 in range(2):
                    b = self.bank()
                    for c in range(8):
                        self.mm(banks[b][:, :], uT[:, c, s2 * 128:(s2 + 1) * 128], wv[:, c, :], c == 0, c == 7,
                                [self.wringB[slot], uTBs[c]], [bankB[b]])
                    self.copy(vx[:, s2, (n - 2) * 512:(n - 1) * 512], banks[b][:, :], [bankB[b]], [kvxB], eng="act")
                    self.release(b)

        self.wseq_build()

        self.memset(Sst[:, :, :], 0.0, SstB)
        issue_x(NT - 1)
        for ti in range(NT - 1, -1, -1):
            t0 = ti * T
            self.step('p1 load %d' % ti)
            load_x(ti)
            issue_x(ti - 1 if ti > 0 else 0)
            load_rope(ti)
            self.step('p1 prep %d' % ti)
            hgrn_prep(1, ti)
            self.dma(vhs[ti, :, :], h_V[0:64, :, :].rearrange("q c n -> q (c n)"), [h_VB], (), chan="vhst", q="pool")
            self.step('p1 scan %d' % ti)
            hgrn_scan(1)
            self.step('p1 kv %d' % ti)
            for _ in range(3):
                if conv_later:
                    convert(conv_later.pop(0))
            self.dma(obs[ti, :, :], h_OT[:, :, :].rearrange("p h t -> p (h t)"), h_OTBs, (), chan="obst", q="sp")
            enter_stage("A")
            kpend = None

            def flush_k(pk):
                pgi, pj, psl = pk
                rot_B(a_kst[:, psl, :], a_kstB[psl], psl % 2)
                self.dma(kts[pgi, pj, :, t0:t0 + T], a_kst[:, psl, :], [a_kstB[psl]], (), chan="kst%d" % psl, q="pool")

            for gi in range(3):
                self.step('p1 k %d' % gi)
                wv, wB, kc = self.w_next(self.widx["in"][11 + 3 * gi])
                for j in range(4):
                    sl = j
                    b = proj_fm(wv, wB, kc, j, uT_of, UT)
                    if kpend is not None:
                        flush_k(kpend)
                    rot_A(b, a_kst[:, sl, :], a_kstB[sl], sl % 2)
                    kpend = (gi, j, sl)
                self.w_done()
                self.step('p1 v %d' % gi)
                wv, wB, kc = self.w_next(self.widx["in"][12 + 3 * gi])
                for s4 in range(4):
                    sl = s4
                    b = self.bank()
                    for c in range(8):
                        self.mm(banks[b][:, :], uT[:, c, s4 * 128:(s4 + 1) * 128], wv[:, c, :], c == 0, c == 7,
                                [wB, uTBs[c]], [bankB[b]])
                    self.copy(a_vst[:, sl, :], banks[b][:, :], [bankB[b]], [a_vstB[sl]], eng="act")
                    self.release(b)
                    self.dma(vsc[gi, t0 + s4 * 128:t0 + (s4 + 1) * 128, :], a_vst[:, sl, :], [a_vstB[sl]], (),
                             chan="vst%d" % sl, q="pool")
                self.w_done()
            flush_k(kpend)
        while conv_later:
            convert(conv_later.pop(0))
        fence = [("#" + c, self.P.chan.get(c, 0)) for c in ("obst", "kst0", "kst1", "kst2", "kst3", "vst0", "vst1", "vst2", "vst3", "vhst")]
        for bb in a_KTB + a_VB + oblB + [h_VB]:
            for s, i in fence:
                if i:
                    bb.r[s] = max(bb.r.get(s, 0), i)

        self.step('p2 init')
        self.memset(Sst[:, :, :], 0.0, SstB)
        enter_stage("A")
        self.memset(a_KT[:, :, :], 0.0, a_KTB, eng="pool")
        self.memset(a_V[:, :, :, :], 0.0, a_VB, eng="pool")

        KOFF = (0, 640, 1664)
        VOFF = (0, 5, 13)

        def load_windows(ti, head, sl):
            t0 = ti * T
            for gi, dil in enumerate(GROUPS):
                lo = t0 - 64 * dil
                hi = t0 + T + 64 * dil
                clo, chi = max(lo, 0), min(hi, S)
                self.dma(a_KT[:, sl, KOFF[gi] + (clo - lo):KOFF[gi] + (chi - lo)], kts[gi, head, :, clo:chi], (),
                         [a_KTB[sl]], chan="akt%d" % sl)
                L = S // dil
                mq0 = t0 // dil
                nqb = 4 if dil == 1 else 1
                for blk in range(nqb + 1):
                    m0 = mq0 - 64 + 128 * blk
                    vlo, vhi = max(0, -m0), min(128, L - m0)
                    if dil == 16 and blk == 1:
                        vhi = min(vhi, 32)
                    if vhi <= vlo:
                        continue
                    b0 = VOFF[gi] + blk
                    tok0 = dil * (m0 + vlo)
                    tok1 = dil * (m0 + vhi)
                    if dil == 1:
                        self.dma(a_V[vlo:vhi, sl, b0, :], vsc[gi, tok0:tok1, head * 128:(head + 1) * 128], (),
                                 [a_VB[sl]], chan="av%d" % sl)
                    else:
                        self.dma(a_V[vlo:vhi, sl, b0:b0 + (dil - 1) * (nqb + 1) + 1:(nqb + 1), :],
                                 vsc[gi, tok0:tok1, head * 128:(head + 1) * 128].rearrange("(i r) d -> i r d", r=dil),
                                 (), [a_VB[sl]], chan="av%d" % sl)

        def attention(ti):
            t0 = ti * T
            enter_stage("A")
            load_windows(ti, 0, 0)
            load_windows(ti, 1, 1)
            pend = None
            for gi in range(3):
                wv, wB, kc = self.w_next(self.widx["in"][10 + 3 * gi])
                for j in range(4):
                    qi = gi * 4 + j
                    b = proj_fm(wv, wB, kc, j, uT_of, UT)
                    if pend is not None:
                        rot_B(a_QT[:, pend, :], a_QTB[pend], pend % 2)
                    rot_A(b, a_QT[:, qi, :], a_QTB[qi], qi % 2)
                    pend = qi
                self.w_done()
            rot_B(a_QT[:, pend, :], a_QTB[pend], pend % 2)
            for head in range(4):
                sl = head % 2
                bo = self.bank()
                bd = self.bank()
                self.memset(banks[bo][:, :], 0.0, [bankB[bo]])
                self.memset(banks[bd][:, :], 0.0, [bankB[bd]])
                jobs = []
                for gi, dil in enumerate(GROUPS):
                    L = S // dil
                    mq0 = t0 // dil
                    nqb = 4 if dil == 1 else 1
                    for kb in range(2):
                        units = []
                        for r in range(dil):
                            for qb in range(nqb):
                                blk = qb + kb
                                m0 = mq0 - 64 + 128 * blk
                                vlo, vhi = max(0, -m0), min(128, L - m0)
                                if vhi <= vlo:
                                    continue
                                units.append((r, qb, blk, m0, vlo, vhi))
                        if units:
                            jobs.append((gi, dil, kb, nqb, units))

                def scores(job, ps):
                    gi, dil, kb, nqb, units = job
                    qh = a_QT[:, gi * 4 + head, :]
                    bs = self.bank()
                    full = len(units) == dil * nqb
                    short = (dil == 16 and kb == 1)
                    if (not full) or short:
                        self.memset(banks[bs][:, :], -30000.0, [bankB[bs]])
                    nk = 32 if short else 128
                    for (r, qb, blk, m0, vlo, vhi) in units:
                        woff = KOFF[gi] + (dil * m0 + r) - (t0 - 64 * dil)
                        kap = a_KT[:, sl, woff:woff + (nk - 1) * dil + 1:dil]
                        if dil == 1:
                            qap = qh[:, qb * 128:(qb + 1) * 128]
                            oap = banks[bs][0:nk, qb * 128:(qb + 1) * 128]
                        else:
                            qap = qh[:, r::dil]
                            oap = banks[bs][0:nk, r::dil]
                        self.mm(oap, kap, qap, True, True, [a_KTB[sl], a_QTB[gi * 4 + head]], [bankB[bs]], skip=True)
                    self.act(a_PT[:, ps, :], banks[bs][:, :], AF.Exp, [bankB[bs]], [a_PTB[ps]], scale=ATT_SCALE)
                    self.release(bs)
                    self.tt(a_PT[:, ps, :], a_PT[:, ps, :], amask[:, gi * 2 + kb, :], ALU.mult, [a_PTB[ps], constB],
                            [a_PTB[ps]])
                    if dil == 1:
                        for (r, qb, blk, m0, vlo, vhi) in units:
                            if vlo > 0:
                                self.memset(a_PT[0:vlo, ps, qb * 128:(qb + 1) * 128], 0.0, [a_PTB[ps]])
                            if vhi < 128:
                                self.memset(a_PT[vhi:128, ps, qb * 128:(qb + 1) * 128], 0.0, [a_PTB[ps]])
                    else:
                        rows = set((u[4], u[5]) for u in units)
                        assert len(rows) == 1, rows
                        vlo, vhi = units[0][4], units[0][5]
                        if vlo > 0:
                            self.memset(a_PT[0:vlo, ps, :], 0.0, [a_PTB[ps]])
                        if vhi < 128 and not (short and vhi >= 32):
                            self.memset(a_PT[vhi:128, ps, :], 0.0, [a_PTB[ps]])

                def pv(job, ps):
                    gi, dil, kb, nqb, units = job
                    self.mm(banks[bd][:, :], onesb[:, :], a_PT[:, ps, :], False, False, [a_PTB[ps], constB],
                            [bankB[bd]], skip=True)
                    for (r, qb, blk, m0, vlo_, vhi_) in units:
                        bidx = VOFF[gi] + r * (nqb + 1) + blk
                        if dil == 1:
                            pap = a_PT[:, ps, qb * 128:(qb + 1) * 128]
                            oap = banks[bo][:, qb * 128:(qb + 1) * 128]
                        else:
                            pap = a_PT[:, ps, r::dil]
                            oap = banks[bo][:, r::dil]
                        self.mm(oap, a_V[:, sl, bidx, :], pap, False, False, [a_VB[sl], a_PTB[ps]], [bankB[bo]],
                                skip=True)

                pendj = None
                for n, job in enumerate(jobs):
                    scores(job, n % 2)
                    if pendj is not None:
                        pv(*pendj)
                    pendj = (job, n % 2)
                pv(*pendj)
                self.act(a_rd[:, :], banks[bd][:, :], AF.Ln, [bankB[bd]], [a_rdB])
                self.release(bd)
                self.act(a_rd[:, :], a_rd[:, :], AF.Exp, [a_rdB], [a_rdB], scale=-1.0)
                self.tt(a_out[:, head, :], banks[bo][:, :], a_rd[:, :], ALU.mult, [bankB[bo], a_rdB], [a_outB])
                self.release(bo)
                if head + 2 < 4:
                    load_windows(ti, head + 2, sl)

        def hgrn_finish(ti):
            alias_fence([yinB, mergedB], [xsB])
            sg8 = merged
            rs2 = [rs, sgt]
            rs2B = [rsB, sgtB]
            fin_banks = {}

            def load_ob(h):
                self.dma(obl[:, h % 2, :], obs[ti, :, h * T:(h + 1) * T], (), [oblB[h % 2]], chan="obl%d" % (h % 2))

            load_ob(0)
            load_ob(1)
            for g in range(2):
                wv, wB, kc = self.w_next(self.widx["in"][8 + g])
                for j in range(4):
                    h = g * 4 + j
                    b = proj_fm(wv, wB, kc, j, uT_of, UT)
                    self.act(sg8[:, h, :], banks[b][:, :], AF.Silu, [bankB[b]], [mergedB])
                    self.release(b)
                self.w_done()

            def front_dve(h):
                p = h % 2
                self.tt(h_OT[:, h, :], h_OT[:, h, :], obl[:, p, :], ALU.add, [oblB[p], h_OTBs[h]], [h_OTBs[h]])
                if h + 2 < 8:
                    load_ob(h + 2)

            def front_act_pe(h):
                p = h % 2
                self.act(sqb[:, p, :], h_OT[:, h, :], AF.Square, [h_OTBs[h]], [sqbB[p]])
                b2 = self.bank()
                self.mm(banks[b2][:, :], onesb[:, :], sqb[:, p, :], True, True, [sqbB[p], constB], [bankB[b2]])
                fin_banks[h] = b2

            def back_act(h):
                p = h % 2
                b2 = fin_banks.pop(h)
                self.act(rs2[p][:, :], banks[b2][:, :], AF.Ln, [bankB[b2], constB], [rs2B[p]], bias=epsg[:, 0:1],
                         scale=1.0 / 128)
                self.release(b2)
                self.act(rs2[p][:, :], rs2[p][:, :], AF.Exp, [rs2B[p]], [rs2B[p]], scale=-0.5)

            def back_dve(h):
                p = h % 2
                self.stt(h_OT[:, h, :], h_OT[:, h, :], gng[:, 0:1], rs2[p][:, :], ALU.mult, ALU.mult,
                         [h_OTBs[h], rs2B[p], constB], [h_OTBs[h]])
                self.tt(yin[:, h, :], h_OT[:, h, :], sg8[:, h, :], ALU.mult, [h_OTBs[h], mergedB], [yinB])

            front_dve(0)
            front_act_pe(0)
            front_dve(1)
            front_act_pe(1)
            for h in range(8):
                if h + 2 < 8:
                    front_dve(h + 2)
                back_act(h)
                if h + 2 < 8:
                    front_act_pe(h + 2)
                back_dve(h)

        def merge_and_out():
            for bq in range(2):
                who, whoB, _ = self.w_next(self.widx["ho"][bq])
                wgh, wghB, _ = self.w_next(self.widx["in"][19 + bq])
                for j in range(4):
                    jj = bq * 4 + j
                    p = j % 2
                    b = proj_fm(wgh, wghB, 8, j, uT_of, UT)
                    self.act(gtmp[:, p, :], banks[b][:, :], AF.Sigmoid, [bankB[b]], [gtmpB[p]])
                    self.release(b)
                    b = proj_fm(who, whoB, 8, j, lambda c: yin[:, c, :], [yinB])
                    self.tt(merged[:, jj, :], banks[b][:, :], gtmp[:, p, :], ALU.mult, [bankB[b], gtmpB[p]], [mergedB])
                    self.release(b)
                self.w_done()
                wao, waoB, _ = self.w_next(self.widx["ao"][bq])
                wga, wgaB, _ = self.w_next(self.widx["in"][21 + bq])
                for j in range(4):
                    jj = bq * 4 + j
                    p = j % 2
                    b = proj_fm(wga, wgaB, 8, j, uT_of, UT)
                    self.act(gtmp[:, p, :], banks[b][:, :], AF.Sigmoid, [bankB[b]], [gtmpB[p]])
                    self.release(b)
                    b = proj_fm(wao, waoB, 4, j, lambda c: a_out[:, c, :], [a_outB])
                    self.tt(gtmp[:, p, :], banks[b][:, :], gtmp[:, p, :], ALU.mult, [bankB[b], gtmpB[p]], [gtmpB[p]])
                    self.release(b)
                    self.tt(merged[:, jj, :], merged[:, jj, :], gtmp[:, p, :], ALU.add, [mergedB, gtmpB[p]], [mergedB],
                            eng="pool")
                self.w_done()
            nb = self.bank()
            for bq in range(2):
                wv, wB, _ = self.w_next(self.widx["wo"][bq])
                for j in range(4):
                    jj = bq * 4 + j
                    b = proj_fm(wv, wB, 8, j, lambda c: merged[:, c, :], [mergedB])
                    if jj > 0:
                        norm_mm(nb, jj - 1)
                    self.tt(hT[:, jj, :], hT[:, jj, :], banks[b][:, :], ALU.add, [hTB[jj], bankB[b]], [hTB[jj]])
                    self.release(b)
                    norm_sq(jj)
                self.w_done()
            norm_mm(nb, 7)
            return nb

        def cross_attention(nb_in):
            enter_stage("C")
            norm_end(nb_in, 1)
            for bq in range(2):
                wv, wB, _ = self.w_next(self.widx["xq"][bq])
                for j in range(4):
                    b = proj_fm(wv, wB, 8, j, uT_of, UT)
                    self.copy(c_qx[:, bq * 4 + j, :], banks[b][:, :], [bankB[b]], [c_qxB], eng="act")
                    self.release(b)
                self.w_done()
            for hx in range(4):
                bd = self.bank()
                for mb in range(2):
                    bs = self.bank()
                    for dc in range(2):
                        self.mm(banks[bs][:, :], kxT[:, hx * 2 + dc, mb * 128:(mb + 1) * 128], c_qx[:, hx * 2 + dc, :],
                                dc == 0, dc == 1, [kvxB, c_qxB], [bankB[bs]])
                    ps = (hx * 2 + mb) % 4
                    self.act(c_PT[:, ps, :], banks[bs][:, :], AF.Exp, [bankB[bs]], [c_PTB[ps]], scale=XA_SCALE)
                    self.release(bs)
                    self.mm(banks[bd][:, :], onesb[:, :], c_PT[:, ps, :], mb == 0, mb == 1, [c_PTB[ps], constB],
                            [bankB[bd]])
                self.act(c_rd[:, :], banks[bd][:, :], AF.Ln, [bankB[bd]], [c_rdB])
                self.release(bd)
                self.act(c_rd[:, :], c_rd[:, :], AF.Exp, [c_rdB], [c_rdB], scale=-1.0)
                for dc in range(2):
                    bo = self.bank()
                    for mb in range(2):
                        ps = (hx * 2 + mb) % 4
                        self.mm(banks[bo][:, :], vx[:, mb, (hx * 2 + dc) * 128:(hx * 2 + dc + 1) * 128], c_PT[:, ps, :],
                                mb == 0, mb == 1, [kvxB, c_PTB[ps]], [bankB[bo]])
                    self.tt(c_ox[:, hx * 2 + dc, :], banks[bo][:, :], c_rd[:, :], ALU.mult, [bankB[bo], c_rdB],
                            [c_oxB])
                    self.release(bo)
            nb = self.bank()
            for bq in range(2):
                wv, wB, _ = self.w_next(self.widx["xo"][bq])
                for j in range(4):
                    jj = bq * 4 + j
                    b = proj_fm(wv, wB, 8, j, lambda c: c_ox[:, c, :], [c_oxB])
                    if jj > 0:
                        norm_mm(nb, jj - 1)
                    self.tt(hT[:, jj, :], hT[:, jj, :], banks[b][:, :], ALU.add, [hTB[jj], bankB[b]], [hTB[jj]])
                    self.release(b)
                    norm_sq(jj)
                self.w_done()
            norm_mm(nb, 7)
            return nb

        def ffn(nb_in):
            norm_end(nb_in, 3)
            nbank = self.bank()
            for bq in range(8):
                wv, wB, _ = self.w_next(self.widx["f1"][bq])
                for j in range(4):
                    sl = j % 2
                    b = proj_fm(wv, wB, 8, j, uT_of, UT)
                    self.act(c_rl[:, sl, :], banks[b][:, :], AF.Relu, [bankB[b]], [c_rlB[sl]])
                    self.release(b)
                    self.tt(c_hid[:, bq * 4 + j, :], c_rl[:, sl, :], c_rl[:, sl, :], ALU.mult, [c_rlB[sl]], [c_hidB],
                            eng="pool")
                self.w_done()
            for nb in range(2):
                accs = [self.bank() for _ in range(4)]
                for kg in range(4):
                    wv, wB, _ = self.w_next(self.widx["f2"][nb * 4 + kg])
                    for j in range(4):
                        for c in range(8):
                            self.mm(banks[accs[j]][:, :], wv[:, c, j * 128:(j + 1) * 128], c_hid[:, kg * 8 + c, :],
                                    kg == 0 and c == 0, kg == 3 and c == 7, [wB, c_hidB], [bankB[accs[j]]])
                    self.w_done()
                for j in range(4):
                    jj = nb * 4 + j
                    self.tt(hT[:, jj, :], hT[:, jj, :], banks[accs[j]][:, :], ALU.add, [hTB[jj], bankB[accs[j]]],
                            [hTB[jj]])
                    self.release(accs[j])
                    norm_acc(nbank, jj)
            return nbank

        def final_out(ti, nb_in):
            t0 = ti * T
            b = nb_in
            self.act(rs[:, :], banks[b][:, :], AF.Ln, [bankB[b], constB], [rsB], bias=epsn[:, 0:1], scale=1.0 / D)
            self.release(b)
            self.act(rinvn[:, :], rs[:, :], AF.Exp, [rsB], [rinvnB], scale=-0.5)
            for half in range(2):
                bts = [self.bank() for _ in range(4)]
                for cc in range(4):
                    c = half * 4 + cc
                    sl = c % 2
                    self.stt(c_yT[:, sl, :], hT[:, c, :], gains[:, 4, c:c + 1], rinvn[:, :], ALU.mult, ALU.mult,
                             [hTB[c], rinvnB, constB], [c_yTB[sl]])
                    for s4 in range(4):
                        self.tr(banks[bts[s4]][:, cc * 128:(cc + 1) * 128], c_yT[:, sl, s4 * 128:(s4 + 1) * 128],
                                identf[:, :], [c_yTB[sl], constB], [bankB[bts[s4]]])
                for s4 in range(4):
                    self.copy(c_ysb[:, s4, :], banks[bts[s4]][:, :], [bankB[bts[s4]]], [c_ysbB[s4]],
                              eng=("act" if s4 % 2 else "dve"))
                    self.release(bts[s4])
                    self.dma(y[t0 + s4 * 128:t0 + (s4 + 1) * 128, half * 512:(half + 1) * 512], c_ysb[:, s4, :],
                             [c_ysbB[s4]], (), chan="yst%d" % s4, q="pool")

        for ti in range(NT):
            self.step('p2 load %d' % ti)
            load_x(ti)
            load_rope(ti)
            self.step('p2 prep %d' % ti)
            hgrn_prep(0, ti)
            self.step('p2 scan %d' % ti)
            hgrn_scan(0)
            self.step('p2 finish %d' % ti)
            hgrn_finish(ti)
            self.step('p2 attn %d' % ti)
            attention(ti)
            self.step('p2 merge %d' % ti)
            nb1 = merge_and_out()
            if ti + 1 < NT:
                issue_x(ti + 1)
            self.step('p2 cross %d' % ti)
            nb2 = cross_attention(nb1)
            self.step('p2 ffn %d' % ti)
            nb3 = ffn(nb2)
            self.step('p2 final %d' % ti)
            final_out(ti, nb3)
        assert self.wpos == len(self.wseq), (self.wpos, len(self.wseq))
        if self.P.disabled:
            self.P.disabled = False
            dummyB = Buf("dummy")
            self.dma(sgt[:, 0:256], obs[0, :, 0:256], (), [dummyB], chan="dmy")
            self.dma(sqb[:, 0, 0:256], kts[0, 0, :, 0:256], (), [dummyB], chan="dmy")
            self.dma(sqb[:, 1, 0:256], vsc[0, 0:128, 0:256], (), [dummyB], chan="dmy")

        P.finalize()
        chans = sorted(P.chan.keys())
        sems = {}
        for e in ("pe", "act", "dve", "pool", "sp"):
            sems[e] = self.es.enter_context(nc.semaphore("s_" + e))
        chansems = {c: self.es.enter_context(nc.semaphore("c_" + c)) for c in chans}
        with nc.Block() as block:
            @block.tensor
            def _(e):
                P.emit("pe", e, sems, chansems)

            @block.scalar
            def _(e):
                P.emit("act", e, sems, chansems)

            @block.vector
            def _(e):
                P.emit("dve", e, sems, chansems)

            @block.gpsimd
            def _(e):
                P.emit("pool", e, sems, chansems, final_wait=True)

            @block.sync
            def _(e):
                P.emit("sp", e, sems, chansems)
        self.es.close()
        return nc


def host_consts(S):
    half = 16
    inv = (500000.0 ** (-np.arange(half, dtype=np.float32) * 2.0 / 32)).astype(np.float32)
    pos = np.arange(S, dtype=np.float32)
    ang = (pos[:, None] * inv[None, :]).astype(np.float32)
    cos = np.cos(ang).astype(np.float32).T
    sin = np.sin(ang).astype(np.float32).T
    rope = np.zeros((2, 32, S), np.float32)
    rope[0, 0:16] = cos
    rope[0, 16:32] = cos
    rope[1, 0:16] = -sin
    rope[1, 16:32] = sin
    perm = np.zeros((32, 32), np.float32)
    for m in range(32):
        perm[(m + 16) % 32, m] = 1.0
    s = np.arange(64)[:, None]
    t = np.arange(64)[None, :]
    hmask = np.zeros((64, 2, 512), np.float32)
    hmask[:, 0, :] = np.tile((s <= t).astype(np.float32), (1, 8))
    hmask[:, 1, :] = np.tile((s >= t).astype(np.float32), (1, 8))
    i = np.arange(128)[:, None]
    tt = np.arange(512)[None, :]
    amask = np.zeros((128, 6, 512), np.float32)
    for gi, dil in enumerate(GROUPS):
        j = (tt % 128) if dil == 1 else (tt // dil)
        amask[:, gi * 2 + 0, :] = (i >= j)
        amask[:, gi * 2 + 1, :] = (i <= j)
    return dict(rope=rope, perm=perm, hmask=hmask, amask=amask, identf=np.eye(128, dtype=np.float32))


def chunkmajor(g):
    return np.ascontiguousarray(np.asarray(g, np.float32).reshape(8, 128).T)


_CACHE = {}
_LIMIT = 10 ** 9
_SKIP = set()
_DEBUG = False


def run_cores(xs, mems, wts, S):
    if S not in _CACHE:
        _CACHE[S] = Kern(S, limit=_LIMIT, debug=_DEBUG).build()
    nc = _CACHE[S]
    hc = host_consts(S)
    gains = np.stack([chunkmajor(wts["mix_norm_g"]), chunkmajor(wts["xa_norm_g"]), chunkmajor(wts["mem_norm_g"]),
                      chunkmajor(wts["ffn_norm_g"]), chunkmajor(wts["final_norm_g"])], axis=1)
    gng = np.asarray(wts["hgrn_gnorm_g"], np.float32).reshape(128, 1)
    lbl = np.ascontiguousarray(np.asarray(wts["hgrn_lb_logits"], np.float32).reshape(2, 2, 8, 128).transpose(3, 0, 1, 2))
    shared = {
        "w_in": np.ascontiguousarray(wts["w_in"].reshape(D, NIN)),
        "w_hgrn_o": np.ascontiguousarray(wts["w_hgrn_o"].reshape(D, D)),
        "w_attn_o": np.ascontiguousarray(wts["w_attn_o"].reshape(512, D)),
        "w_out": np.ascontiguousarray(wts["w_out"].reshape(D, D)),
        "w_xq": np.ascontiguousarray(wts["w_xq"].reshape(D, D)),
        "w_xkv": np.ascontiguousarray(wts["w_xkv"].reshape(D, 2 * D)),
        "w_xo": np.ascontiguousarray(wts["w_xo"].reshape(D, D)),
        "w_ffn1": np.ascontiguousarray(wts["w_ffn1"].reshape(D, 4 * D)),
        "w_ffn2": np.ascontiguousarray(wts["w_ffn2"].reshape(4 * D, D)),
        "gains": np.ascontiguousarray(gains), "gng": gng, "lbl": lbl,
        "rope": hc["rope"], "identf": hc["identf"], "perm": hc["perm"], "hmask": hc["hmask"], "amask": hc["amask"],
    }
    in_maps = []
    for xc, mc in zip(xs, mems):
        m = dict(shared)
        m["x"] = np.ascontiguousarray(xc, dtype=np.float32)
        m["mem"] = np.ascontiguousarray(mc, dtype=np.float32)
        in_maps.append(m)
    res = run_bass_kernel_spmd(nc, in_maps, core_ids=list(range(len(xs))))
    return [r["y"] for r in res.results]


def kernel(x_prompt, x_sample, mem_prompt, mem_sample, mix_norm_g, w_in, hgrn_lb_logits, hgrn_gnorm_g,
           w_hgrn_o, w_attn_o, w_out, xa_norm_g, mem_norm_g, w_xq, w_xkv, w_xo, ffn_norm_g, w_ffn1,
           w_ffn2, final_norm_g):
    f = lambda a: np.asarray(a, dtype=np.float32)
    x_prompt, x_sample, mem_prompt, mem_sample = f(x_prompt), f(x_sample), f(mem_prompt), f(mem_sample)
    wts = dict(mix_norm_g=f(mix_norm_g), w_in=f(w_in), hgrn_lb_logits=f(hgrn_lb_logits), hgrn_gnorm_g=f(hgrn_gnorm_g),
               w_hgrn_o=f(w_hgrn_o), w_attn_o=f(w_attn_o), w_out=f(w_out), xa_norm_g=f(xa_norm_g),
               mem_norm_g=f(mem_norm_g), w_xq=f(w_xq), w_xkv=f(w_xkv), w_xo=f(w_xo), ffn_norm_g=f(ffn_norm_g),
               w_ffn1=f(w_ffn1), w_ffn2=f(w_ffn2), final_norm_g=f(final_norm_g))
    S = x_prompt.shape[1]
    seqs = [x_prompt[0], x_prompt[1], x_sample[0], x_sample[1], x_sample[2], x_sample[3], x_prompt[0], x_prompt[1]]
    mems = [mem_prompt[0], mem_prompt[1], mem_sample[0], mem_sample[1], mem_sample[2], mem_sample[3],
            mem_prompt[0], mem_prompt[1]]
    ys = run_cores(seqs, mems, wts, S)
    y_prompt = np.stack([ys[0], ys[1]], axis=0).astype(np.float32)
    y_sample = np.stack([ys[2], ys[3], ys[4], ys[5]], axis=0).astype(np.float32)
    return (y_prompt, y_sample)
```

```python
import numpy as np
from contextlib import ExitStack
import concourse.bass as bass
import concourse.mybir as mybir
from concourse.bass_utils import run_bass_kernel_spmd

F32 = mybir.dt.float32
BF16 = mybir.dt.bfloat16
AF = mybir.ActivationFunctionType
ALU = mybir.AluOpType

D = 1024
T = 512
NIN = 11776
NMEM = 256
EPS = 1e-6
NSLOT = 4
GROUPS = (1, 4, 16)
ATT_SCALE = 128 ** -0.5
XA_SCALE = 256 ** -0.5


class Buf:
    __slots__ = ("name", "w", "r", "excl")

    def __init__(self, name, excl=False):
        self.name = name
        self.w = None
        self.r = {}
        self.excl = excl


class Sched:
    def __init__(self, same_engine_raw=True):
        self.streams = {e: [] for e in ("pe", "act", "dve", "pool", "sp")}
        self.chan = {}
        self.ser = same_engine_raw
        self.disabled = False

    def add(self, eng, fn, reads=(), writes=(), chan=None, extra=None):
        if self.disabled:
            return None
        deps = dict(extra) if extra else {}
        mychan = None if chan is None else "#" + chan

        def need(a, raw):
            if a is None:
                return
            s, i = a
            if s == eng:
                if not raw or not self.ser or eng in ("pe", "sp"):
                    return
            if mychan is not None and s == mychan:
                return
            if deps.get(s, 0) < i:
                deps[s] = i

        for b in reads:
            need(b.w, True)
            if b.excl:
                for s, i in b.r.items():
                    need((s, i), False)
        for b in writes:
            need(b.w, False)
            for s, i in b.r.items():
                need((s, i), False)
        lst = self.streams[eng]
        rec = [fn, deps, False, chan, 0]
        lst.append(rec)
        if chan is None:
            me = (eng, len(lst))
        else:
            c = self.chan.get(chan, 0) + 1
            self.chan[chan] = c
            me = (mychan, c)
        for b in reads:
            if b.r.get(me[0], 0) < me[1]:
                b.r[me[0]] = me[1]
        for b in writes:
            b.w = me
            b.r = {}
        return me

    def finalize(self):
        for lst in self.streams.values():
            for rec in lst:
                for s, i in rec[1].items():
                    if s[0] != "#":
                        self.streams[s][i - 1][2] = True
        for lst in self.streams.values():
            cum = 0
            for rec in lst:
                if rec[2]:
                    cum += 1
                rec[4] = cum

    def emit(self, eng, e, sems, chansems, final_wait=False):
        seen = {}
        for rec in self.streams[eng]:
            fn, deps, marked, chan, _ = rec
            for s, i in deps.items():
                if s[0] == "#":
                    sem = chansems[s[1:]]
                    v = 16 * i
                else:
                    sem = sems[s]
                    v = self.streams[s][i - 1][4]
                if seen.get(s, 0) >= v:
                    continue
                seen[s] = v
                e.wait_ge(sem, v)
            ins = fn(e)
            if chan is not None:
                ins.then_inc(chansems[chan], 16)
            elif marked:
                ins.then_inc(sems[eng], 1)
        if final_wait:
            for c, n in self.chan.items():
                e.wait_ge(chansems[c], 16 * n)


def weight_blocks():
    blocks = []
    index = {}

    def addw(key, src, K, N):
        ids = []
        for nb in range(N // 512):
            for kg in range(max(1, K // 1024)):
                kc = min(8, K // 128)
                ids.append(len(blocks))
                blocks.append((src, kg * 1024, kc, nb * 512))
        index[key] = ids

    addw("in", "w_in", 1024, NIN)
    addw("ho", "w_hgrn_o", 1024, 1024)
    addw("ao", "w_attn_o", 512, 1024)
    addw("wo", "w_out", 1024, 1024)
    addw("xq", "w_xq", 1024, 1024)
    addw("xkv", "w_xkv", 1024, 2048)
    addw("xo", "w_xo", 1024, 1024)
    addw("f1", "w_ffn1", 1024, 4096)
    addw("f2", "w_ffn2", 4096, 1024)
    return blocks, index


class StopBuild(Exception):
    pass


class Kern:
    def __init__(self, S, limit=99, debug=False):
        self.limit = limit
        self.debug = debug
        self.dbg_outs = []
        self.stepno = 0
        self.S = S
        self.NT = S // T
        self.nc = bass.Bass("TRN2", target_bir_lowering=False)
        self.P = Sched()
        self.es = ExitStack()
        self.blocks, self.widx = weight_blocks()

    def sb(self, name, shape, dt):
        return self.es.enter_context(self.nc.sbuf_tensor("sb_" + name, shape, dt))

    def din(self, name, shape, dt=F32):
        return self.nc.dram_tensor(name, shape, dt, kind="ExternalInput").ap()

    def mm(self, out, lhsT, rhs, start, stop, reads, writes, skip=False):
        self.P.add("pe", lambda e: e.matmul(out, lhsT, rhs, start=start, stop=stop, skip_group_check=skip),
                   reads, writes)

    def tr(self, out, in_, ident, reads, writes):
        self.P.add("pe", lambda e: e.transpose(out, in_, ident), reads, writes)

    def act(self, out, in_, func, reads, writes, bias=None, scale=None, eng="act"):
        kw = {}
        if bias is not None:
            kw["bias"] = bias
        if scale is not None:
            kw["scale"] = scale
        self.P.add("act", lambda e: e.activation(out, in_, func, **kw), reads, writes)

    def tt(self, out, in0, in1, op, reads, writes, eng="dve"):
        self.P.add(eng, lambda e: e.tensor_tensor(out, in0, in1, op), reads, writes)

    def ts(self, out, in0, s1, s2, op0, op1, reads, writes, eng="dve"):
        if op1 is None:
            self.P.add(eng, lambda e: e.tensor_scalar(out, in0, s1, None, op0), reads, writes)
        else:
            self.P.add(eng, lambda e: e.tensor_scalar(out, in0, s1, s2, op0, op1), reads, writes)

    def stt(self, out, in0, scalar, in1, op0, op1, reads, writes):
        self.P.add("dve", lambda e: e.scalar_tensor_tensor(out, in0, scalar, in1, op0, op1), reads, writes)

    def copy(self, out, in_, reads, writes, eng="dve"):
        if eng == "act":
            self.P.add("act", lambda e: e.activation(out, in_, AF.Copy), reads, writes)
        else:
            self.P.add(eng, lambda e: e.tensor_copy(out, in_), reads, writes)

    def memset(self, ap, val, writes, eng="dve"):
        self.P.add(eng, lambda e: e.memset(ap, val), (), writes)

    def dma(self, out, in_, reads, writes, chan, q="sp"):
        self.P.add(q, lambda e: e.dma_start(out=out, in_=in_), reads, writes, chan=chan)

    def step(self, name):
        self.stepno += 1
        self.P.disabled = (self.stepno > self.limit) or (self.stepno in _SKIP)
        if self.debug:
            print("step", self.stepno, name)

    def dbg(self, name, ap, bufs, dt=F32):
        if not self.debug:
            return
        shape = list(ap.shape)
        t = self.nc.dram_tensor(name, shape, dt, kind="ExternalOutput").ap()
        self.dbg_outs.append(name)
        self.dma(t, ap, bufs, (), chan="dbg", q="pool")

    def bank(self):
        i = self.free_banks.pop(0)
        return i

    def release(self, i):
        self.free_banks.append(i)

    def wseq_build(self):
        W = self.widx
        seq = []
        p1 = [W["in"][b] for b in (4, 0, 5, 1, 6, 7, 11, 12, 14, 15, 17, 18)]
        for _ in range(self.NT):
            seq += p1
        p2 = [W["in"][b] for b in (2, 0, 3, 1, 8, 9, 10, 13, 16)]
        for b in range(2):
            p2 += [W["ho"][b], W["in"][19 + b], W["ao"][b], W["in"][21 + b]]
        p2 += W["wo"] + W["xq"] + W["xo"] + W["f1"] + W["f2"]
        for _ in range(self.NT):
            seq += p2
        self.wseq = seq
        self.wpos = 0
        self.wissued = 0
        self.wdone = 0

    def w_issue(self):
        i = self.wissued
        slot = i % NSLOT
        blk = self.wseq[i]
        kc = self.blocks[blk][2]
        self.dma(self.wring[:, slot, 0:kc * 512], self.wbf[blk, :, 0:kc * 512], [self.wbfB[self.wgroup[blk]]],
                 [self.wringB[slot]], chan="w%d" % slot)
        self.wissued += 1

    def w_prefetch(self):
        while self.wissued < len(self.wseq) and self.wissued - NSLOT < self.wdone:
            self.w_issue()

    def w_next(self, blk):
        i = self.wpos
        assert self.wseq[i] == blk, (i, self.wseq[i], blk)
        self.w_prefetch()
        assert self.wissued > i, (i, self.wissued, self.wdone)
        slot = i % NSLOT
        self.wpos += 1
        kc = self.blocks[blk][2]
        view = self.wring[:, slot, 0:kc * 512].rearrange("p (c n) -> p c n", c=kc)
        return view, self.wringB[slot], kc

    def w_done(self):
        self.wdone = self.wpos
        self.w_prefetch()

    def build(self):
        nc, P, S, NT = self.nc, self.P, self.S, self.NT
        x = self.din("x", [S, D])
        mem = self.din("mem", [NMEM, D])
        wsrc = {
            "w_in": self.din("w_in", [D, NIN]), "w_hgrn_o": self.din("w_hgrn_o", [D, D]),
            "w_attn_o": self.din("w_attn_o", [512, D]), "w_out": self.din("w_out", [D, D]),
            "w_xq": self.din("w_xq", [D, D]), "w_xkv": self.din("w_xkv", [D, 2 * D]),
            "w_xo": self.din("w_xo", [D, D]), "w_ffn1": self.din("w_ffn1", [D, 4 * D]),
            "w_ffn2": self.din("w_ffn2", [4 * D, D]),
        }
        gains_d = self.din("gains", [128, 5, 8])
        gng_d = self.din("gng", [128, 1])
        lbl_d = self.din("lbl", [128, 2, 2, 8])
        rope_d = self.din("rope", [2, 32, S])
        identf_d = self.din("identf", [128, 128])
        perm_d = self.din("perm", [32, 32])
        hmask_d = self.din("hmask", [64, 2, 512])
        amask_d = self.din("amask", [128, 6, 512])
        y = nc.dram_tensor("y", [S, D], F32, kind="ExternalOutput").ap()
        nblk = len(self.blocks)
        obs = nc.dram_tensor("obs", [NT, 128, 8 * T], F32, kind="Internal").ap()
        vhs = nc.dram_tensor("vhs", [NT, 64, 8 * D], BF16, kind="Internal").ap()
        kts = nc.dram_tensor("kts", [3, 4, 128, S], BF16, kind="Internal").ap()
        vsc = nc.dram_tensor("vsc", [3, S, 512], BF16, kind="Internal").ap()
        self.wbf = nc.dram_tensor("wbf", [nblk, 128, 4096], BF16, kind="Internal").ap()

        hT = self.sb("hT", [128, 8, T], F32)
        hTB = [Buf("hT%d" % c) for c in range(8)]
        uT = self.sb("uT", [128, 8, T], BF16)
        uTBs = [Buf("uT%d" % c) for c in range(8)]
        UT = "uT-per-chunk"
        self.wring = self.sb("wring", [128, NSLOT, 4096], BF16)
        self.wringB = [Buf("wr%d" % i) for i in range(NSLOT)]
        identf = self.sb("identf", [128, 128], F32)
        identb = self.sb("identb", [128, 128], BF16)
        onesb = self.sb("onesb", [128, 128], BF16)
        permb = self.sb("permb", [32, 32], BF16)
        hmask = self.sb("hmask", [64, 2, 512], F32)
        amask = self.sb("amask", [128, 6, 512], BF16)
        gains = self.sb("gains", [128, 5, 8], F32)
        gng = self.sb("gng", [128, 1], F32)
        lbt = self.sb("lbt", [128, 2, 2, 8], F32)
        lb = self.sb("lb", [128, 2, 8], F32)
        oml = self.sb("oml", [128, 2, 8], F32)
        noml = self.sb("noml", [128, 2, 8], F32)
        epsn = self.sb("epsn", [128, 1], F32)
        constB = Buf("consts")
        rope = self.sb("rope", [32, 2, T], F32)
        ropeB = Buf("rope")
        sqb = self.sb("sqb", [128, 2, T], BF16)
        sqbB = [Buf("sqb0"), Buf("sqb1")]
        rs = self.sb("rs", [128, T], F32)
        rsB = Buf("rs")
        rinvn = self.sb("rinvn", [128, T], F32)
        rinvnB = Buf("rinvn")
        ym = self.sb("ym", [128, 16 * T], BF16)
        yin = ym[:, 0:8 * T].rearrange("p (c n) -> p c n", c=8)
        yinB = Buf("yin")
        obl = self.sb("obl", [128, 2, T], F32)
        oblB = [Buf("obl0"), Buf("obl1")]
        sgt = self.sb("sgt", [128, T], F32)
        sgtB = Buf("sgt")
        merged = ym[:, 8 * T:16 * T].rearrange("p (c n) -> p c n", c=8)
        xs = ym[:, :].bitcast(F32).rearrange("p (s d) -> p s d", s=4)
        xsB = Buf("xs")
        mergedB = Buf("merged")
        gtmp = self.sb("gtmp", [128, 2, T], F32)
        gtmpB = [Buf("gtmp0"), Buf("gtmp1")]
        Sst = self.sb("Sst", [128, 8, 128], F32)
        SstB = [Buf("Sst%d" % i) for i in range(8)]
        kxT = self.sb("kxT", [128, 8, NMEM], BF16)
        vx = self.sb("vx", [128, 2, D], BF16)
        kvxB = Buf("kvx")

        ARENA_B = 92672
        arena = self.sb("arena", [128, ARENA_B // 2], BF16)

        def carve(off, shape, dt, parts=128):
            n = int(np.prod(shape[1:]))
            esz = 4 if dt == F32 else 2
            assert off % 4 == 0
            assert off + n * esz <= ARENA_B, (off, n * esz)
            v = arena[0:parts, off // 2: off // 2 + n * esz // 2]
            if dt == F32:
                v = v.bitcast(F32)
            if len(shape) == 3:
                v = v.rearrange("p (a b) -> p a b", a=shape[1])
            elif len(shape) == 4:
                v = v.rearrange("p (a b c) -> p a b c", a=shape[1], b=shape[2])
            return v, off + n * esz

        banks = [self.es.enter_context(nc.psum_tensor("bank%d" % i, [128, 512], F32)) for i in range(8)]
        bankB = [Buf("bank%d" % i, excl=True) for i in range(8)]
        self.free_banks = list(range(8))

        K = 1024
        o = 0
        h_sf, o = carve(o, [128, 2, T], F32)
        h_kk, o = carve(o, [128, 2, T], F32)
        h_R, o = carve(o, [128, 2, T], F32)
        h_sq = h_R
        h_rinv4, o = carve(o, [128, 4, T], F32)
        h_KT, o = carve(o, [128, 8, T], BF16)
        h_QT, o = carve(o, [128, 8, T], BF16)
        h_Ktok, o = carve(o, [128, 8, 8, 128], BF16)
        h_V, o = carve(o, [128, 8, D], BF16)
        h_Sbf, o = carve(o, [128, 2, 8 * 128], BF16)
        h_Asb, o = carve(o, [128, 2, 8 * 64], BF16)
        h_eB, o = carve(o, [128, 8, 8], F32)
        h_OT, o = carve(o, [128, 8, T], F32)
        h_reb, o = carve(o, [128, 8, 8], F32)
        h_sfB = [Buf("h_sf0"), Buf("h_sf1")]
        h_kkB = [Buf("h_kk0"), Buf("h_kk1")]
        h_RB = [Buf("h_R0"), Buf("h_R1")]
        h_sqB = h_RB
        h_rinvB = [Buf("h_rinv%d" % i) for i in range(4)]
        h_KTB = [Buf("h_KT%d" % i) for i in range(8)]
        h_QTB = [Buf("h_QT%d" % i) for i in range(8)]
        h_KtokB = [Buf("h_Ktok%d" % i) for i in range(8)]
        h_VB = Buf("h_V")
        h_SbfB = [Buf("h_Sbf0"), Buf("h_Sbf1")]
        h_AsbB = [Buf("h_Asb0"), Buf("h_Asb1")]
        h_eBB = [Buf("h_eB%d" % i) for i in range(8)]
        h_rebB = [Buf("h_reb%d" % i) for i in range(8)]
        h_OTBs = [Buf("h_OT%d" % i) for i in range(8)]
        stageH = h_rebB + [h_VB] + h_OTBs + h_sfB + h_kkB + h_RB + h_SbfB + h_AsbB + h_rinvB + h_KTB + h_QTB + h_KtokB + h_eBB
        o = 0
        a_QT, o = carve(o, [128, 12, T], BF16)
        KTW = 640 + 1024 + 2560
        a_KT, o = carve(o, [128, 2, KTW], BF16)
        a_V, o = carve(o, [128, 2, 45, 128], BF16)
        a_PT, o = carve(o, [128, 2, T], BF16)
        a_out, o = carve(o, [128, 4, T], BF16)
        a_rd, o = carve(o, [128, T], F32)
        a_rot, o = carve(o, [128, 4, T], F32, parts=32)
        a_kst, o = carve(o, [128, 4, T], BF16)
        a_vst, o = carve(o, [128, 4, 512], BF16)
        a_QTB = [Buf("a_QT%d" % i) for i in range(12)]
        a_KTB = [Buf("a_KTw0"), Buf("a_KTw1")]
        a_VB = [Buf("a_Vw0"), Buf("a_Vw1")]
        a_PTB = [Buf("a_PT0"), Buf("a_PT1")]
        a_outB = Buf("a_out")
        a_rdB = Buf("a_rd")
        a_rotB = [Buf("a_rot0"), Buf("a_rot1")]
        a_kstB = [Buf("a_kst%d" % i) for i in range(4)]
        a_vstB = [Buf("a_vst%d" % i) for i in range(4)]
        stageA = a_QTB + a_KTB + a_VB + a_PTB + [a_outB, a_rdB] + a_rotB + a_kstB + a_vstB
        o = 0
        c_qx, o = carve(o, [128, 8, T], BF16)
        c_PT, o = carve(o, [128, 4, T], BF16)
        c_ox, o = carve(o, [128, 8, T], BF16)
        c_rd, o = carve(o, [128, T], F32)
        c_rl, o = carve(o, [128, 2, T], F32)
        c_hid, o = carve(o, [128, 32, T], BF16)
        c_yT, o = carve(o, [128, 2, T], F32)
        c_ysb, o = carve(o, [128, 8, 512], F32)
        c_qxB = Buf("c_qx")
        c_PTB = [Buf("c_PT%d" % i) for i in range(4)]
        c_oxB = Buf("c_ox")
        c_rdB = Buf("c_rd")
        c_rlB = [Buf("c_rl0"), Buf("c_rl1")]
        c_hidB = Buf("c_hid")
        c_yTB = [Buf("c_yT0"), Buf("c_yT1")]
        c_ysbB = [Buf("c_ysb%d" % i) for i in range(8)]
        stageC = [c_qxB, c_oxB, c_rdB, c_hidB] + c_PTB + c_rlB + c_yTB + c_ysbB
        o = 0
        p_st, o = carve(o, [128, 2, 4096], F32)
        p_bf, o = carve(o, [128, 2, 4096], BF16)
        p_stB = [Buf("p_st0"), Buf("p_st1")]
        p_bfB = [Buf("p_bf0"), Buf("p_bf1")]
        p_mem, o = carve(o, [128, 2, D], F32)
        p_memB = Buf("p_mem")
        stageP = p_stB + p_bfB + [p_memB]
        allstages = {"H": stageH, "A": stageA, "C": stageC, "P": stageP}
        self.cur_stage = [None]

        def enter_stage(name):
            if self.cur_stage[0] == name:
                return
            self.cur_stage[0] = name
            acc_r = {}
            for sn, lst in allstages.items():
                if sn == name:
                    continue
                for b in lst:
                    for s, i in b.r.items():
                        if acc_r.get(s, 0) < i:
                            acc_r[s] = i
                    if b.w is not None:
                        s, i = b.w
                        if acc_r.get(s, 0) < i:
                            acc_r[s] = i
            for b in allstages[name]:
                for s, i in acc_r.items():
                    if b.r.get(s, 0) < i:
                        b.r[s] = i

        def norm_sq(c, N=T):
            sl = c % 2
            self.act(sqb[:, sl, 0:N], hT[:, c, 0:N], AF.Square, [hTB[c]], [sqbB[sl]])

        def norm_mm(nb, c, N=T):
            sl = c % 2
            self.mm(banks[nb][:, 0:N], onesb[:, :], sqb[:, sl, 0:N], c == 0, c == 7, [sqbB[sl], constB], [bankB[nb]])

        def norm_acc(nb, c, N=T):
            norm_sq(c, N)
            norm_mm(nb, c, N)

        def norm_end(nb, gidx, N=T):
            self.act(rs[:, 0:N], banks[nb][:, 0:N], AF.Ln, [bankB[nb], constB], [rsB], bias=epsn[:, 0:1],
                     scale=1.0 / D)
            self.release(nb)
            self.act(rinvn[:, 0:N], rs[:, 0:N], AF.Exp, [rsB], [rinvnB], scale=-0.5)
            for c in range(8):
                self.stt(uT[:, c, 0:N], hT[:, c, 0:N], gains[:, gidx, c:c + 1], rinvn[:, 0:N], ALU.mult, ALU.mult,
                         [hTB[c], rinvnB, constB], [uTBs[c]])

        def rmsnorm(gidx, N=T):
            nb = self.bank()
            for c in range(8):
                norm_acc(nb, c, N)
            norm_end(nb, gidx, N)

        def alias_fence(dst, src):
            acc = {}
            for bb in src:
                for s_, i_ in list(bb.r.items()) + ([bb.w] if bb.w is not None else []):
                    if acc.get(s_, 0) < i_:
                        acc[s_] = i_
            for bb in dst:
                for s_, i_ in acc.items():
                    if bb.r.get(s_, 0) < i_:
                        bb.r[s_] = i_

        def issue_x(ti):
            alias_fence([xsB], [yinB, mergedB])
            t0 = ti * T
            self.dma(xs[:, :, :], x[t0:t0 + T, :].rearrange("(s p) d -> p s d", p=128), (), [xsB], chan="xs")

        def load_x(ti):
            nb = self.bank()
            for c in range(8):
                b = self.bank()
                for s4 in range(4):
                    self.tr(banks[b][:, s4 * 128:(s4 + 1) * 128], xs[:, s4, c * 128:(c + 1) * 128], identf[:, :],
                            [xsB, constB], [bankB[b]])
                if c > 0:
                    norm_mm(nb, c - 1)
                self.copy(hT[:, c, :], banks[b][:, :], [bankB[b]], [hTB[c]], eng="act")
                self.release(b)
                norm_sq(c)
            norm_mm(nb, 7)
            norm_end(nb, 0)

        def proj_fm(wv, wB, kc, j, rhs_of, rhsB, N=T):
            b = self.bank()
            for c in range(kc):
                self.mm(banks[b][:, 0:N], wv[:, c, j * 128:(j + 1) * 128], rhs_of(c), c == 0, c == kc - 1,
                        [wB] + ([uTBs[c]] if rhsB is UT else rhsB), [bankB[b]])
            return b

        uT_of = lambda c: uT[:, c, :]

        def hgrn_prep(direction, ti):
            enter_stage("H")
            self.memset(h_kk[:, :, :], 0.0, h_kkB, eng="pool")
            fblk0 = 2 if direction == 0 else 4

            def f_front(h, j, b):
                p = h % 2
                self.act(h_sf[:, p, :], banks[b][:, :], AF.Sigmoid, [bankB[b]], [h_sfB[p]])
                self.release(b)
                self.act(h_sf[:, p, :], h_sf[:, p, :], AF.Identity, [h_sfB[p], constB], [h_sfB[p]],
                         bias=lb[:, direction, h:h + 1], scale=oml[:, direction, h:h + 1])
                fg = h_sf[:, p, :]
                fgv = fg.rearrange("q (c t) -> q c t", c=8)
                d1s = h_kk[:, 0, :]
                d1e = h_kk[:, 1, :]
                self.copy(d1s.rearrange("q (c t) -> q c t", c=8)[:, :, 0], fgv[:, :, 0], [h_sfB[p]], [h_kkB[0]])
                self.copy(d1e.rearrange("q (c t) -> q c t", c=8)[:, :, 63], fgv[:, :, 63], [h_sfB[p]], [h_kkB[1]])
                if direction == 0:
                    Ppre, PpreB, Psuf, PsufB = h_rinv4[:, j, :], h_rinvB[j], h_R[:, p, :], h_RB[p]
                else:
                    Ppre, PpreB, Psuf, PsufB = h_R[:, p, :], h_RB[p], h_rinv4[:, j, :], h_rinvB[j]
                self.P.add("dve", lambda e: e.tensor_tensor_scan(Ppre, fg, d1s, 1.0, ALU.mult, ALU.max),
                           [h_sfB[p], h_kkB[0]], [PpreB])
                self.P.add("dve", lambda e: e.tensor_tensor_scan(Psuf[:, ::-1], fg[:, ::-1], d1e[:, ::-1], 1.0,
                                                                 ALU.mult, ALU.max), [h_sfB[p], h_kkB[1]], [PsufB])
                Pprev = Ppre.rearrange("q (c t) -> q c t", c=8)
                Psufv = Psuf.rearrange("q (c t) -> q c t", c=8)
                KTv = h_KT[:, h, :].rearrange("q (c t) -> q c t", c=8)
                if direction == 0:
                    self.copy(h_eB[:, h, :], Pprev[:, :, 63], [PpreB], [h_eBB[h]])
                    self.tt(KTv[:, :, 0:63], Psufv[:, :, 1:64], Psufv[:, :, 0:63], ALU.subtract, [PsufB], [h_KTB[h]],
                            eng="pool")
                    self.ts(KTv[:, :, 63:64], Psufv[:, :, 63:64], -1.0, 1.0, ALU.mult, ALU.add, [PsufB], [h_KTB[h]],
                            eng="pool")
                else:
                    self.copy(h_eB[:, h, :], Psufv[:, :, 0], [PsufB], [h_eBB[h]])
                    self.tt(KTv[:, :, 1:64], Pprev[:, :, 0:63], Pprev[:, :, 1:64], ALU.subtract, [PpreB], [h_KTB[h]],
                            eng="pool")
                    self.ts(KTv[:, :, 0:1], Pprev[:, :, 0:1], -1.0, 1.0, ALU.mult, ALU.add, [PpreB], [h_KTB[h]],
                            eng="pool")
                self.P.add("dve", lambda e: e.reciprocal(h_reb[:, h, :], h_eB[:, h, :]), [h_eBB[h]], [h_rebB[h]])

            def f_back(h):
                for half in range(2):
                    bt = self.bank()
                    btv = banks[bt][:, :].bitcast(BF16)
                    for cc in range(4):
                        c = half * 4 + cc
                        self.tr(btv[0:64, cc * 128:(cc + 1) * 128], h_KT[:, h, c * 64:(c + 1) * 64], identb[:, :],
                                [h_KTB[h], constB], [bankB[bt]])
                    self.copy(h_Ktok[0:64, h, half * 4:half * 4 + 4, :],
                              btv[0:64, 0:512].rearrange("q (c k) -> q c k", c=4), [bankB[bt]], [h_KtokB[h]],
                              eng="act")
                    self.release(bt)
                self.tt(h_KT[:, h, :].rearrange("q (c t) -> q c t", c=8), h_KT[:, h, :].rearrange("q (c t) -> q c t", c=8),
                        h_reb[:, h, :].unsqueeze(2).broadcast_to([128, 8, 64]), ALU.mult, [h_KTB[h], h_rebB[h]],
                        [h_KTB[h]])

            for g in range(2):
                wv, wB, kc = self.w_next(self.widx["in"][fblk0 + g])
                for j in range(4):
                    h = g * 4 + j
                    b = proj_fm(wv, wB, kc, j, uT_of, UT)
                    f_front(h, j, b)
                self.w_done()
                wv, wB, kc = self.w_next(self.widx["in"][0 + g])
                for j in range(4):
                    h = g * 4 + j
                    p = h % 2
                    b = proj_fm(wv, wB, kc, j, uT_of, UT)
                    f_back(h)
                    self.act(h_sq[:, p, :], banks[b][:, :], AF.Silu, [bankB[b]], [h_sqB[p]])
                    self.release(b)
                    self.tt(h_QT[:, h, :], h_sq[:, p, :], h_rinv4[:, j, :], ALU.mult, [h_sqB[p], h_rinvB[j]],
                            [h_QTB[h]], eng="pool")
                self.w_done()
            if direction == 0:
                self.dma(h_V[0:64, :, :].rearrange("q c n -> q (c n)"), vhs[ti, :, :], (), [h_VB], chan="vhl")
            for g in (range(2) if direction == 1 else ()):
                wv, wB, kc = self.w_next(self.widx["in"][6 + g])
                for c in range(8):
                    b = self.bank()
                    for kc_ in range(8):
                        self.mm(banks[b][0:64, :], uT[:, kc_, c * 64:(c + 1) * 64], wv[:, kc_, :], kc_ == 0, kc_ == 7,
                                [wB, uTBs[kc_]], [bankB[b]])
                    self.copy(h_V[0:64, c, g * 512:(g + 1) * 512], banks[b][0:64, :], [bankB[b]], [h_VB],
                              eng=("act" if c % 2 else "dve"))
                    self.release(b)
                self.w_done()

        def hgrn_scan(direction):
            order = list(range(8)) if direction == 0 else list(range(7, -1, -1))
            for n, c in enumerate(order):
                p = n % 2
                self.copy(h_Sbf[:, p, :].rearrange("q (h v) -> q h v", h=8), Sst[:, :, :], SstB, [h_SbfB[p]])
                ba = self.bank()
                for h in range(8):
                    self.mm(banks[ba][0:64, h * 64:(h + 1) * 64], h_KT[:, h, c * 64:(c + 1) * 64],
                            h_QT[:, h, c * 64:(c + 1) * 64], True, True, [h_KTB[h], h_QTB[h]], [bankB[ba]])
                bks = []
                for half in range(2):
                    bk = self.bank()
                    bks.append(bk)
                    for hh in range(4):
                        h = half * 4 + hh
                        self.mm(banks[bk][:, hh * 128:(hh + 1) * 128], h_Ktok[0:64, h, c, :],
                                h_V[0:64, c, h * 128:(h + 1) * 128], True, True, [h_KtokB[h], h_VB], [bankB[bk]])
                self.tt(h_Asb[0:64, p, :], banks[ba][0:64, :], hmask[:, direction, :], ALU.mult, [bankB[ba], constB],
                        [h_AsbB[p]])
                self.release(ba)
                for half in range(2):
                    for hh in range(4):
                        h = half * 4 + hh
                        self.stt(Sst[:, h, :], Sst[:, h, :], h_eB[:, h, c:c + 1], banks[bks[half]][:, hh * 128:(hh + 1) * 128],
                                 ALU.mult, ALU.add, [SstB[h], h_eBB[h], bankB[bks[half]], h_SbfB[p]], [SstB[h]])
                    self.release(bks[half])
                bo = self.bank()
                for h in range(8):
                    self.mm(banks[bo][:, h * 64:(h + 1) * 64], h_Sbf[:, p, h * 128:(h + 1) * 128],
                            h_QT[:, h, c * 64:(c + 1) * 64], True, False, [h_SbfB[p], h_QTB[h]], [bankB[bo]])
                    self.mm(banks[bo][:, h * 64:(h + 1) * 64], h_V[0:64, c, h * 128:(h + 1) * 128],
                            h_Asb[0:64, p, h * 64:(h + 1) * 64], False, True, [h_VB, h_AsbB[p]], [bankB[bo]])
                self.copy(h_OT[:, :, c * 64:(c + 1) * 64], banks[bo][:, :].rearrange("q (h t) -> q h t", h=8),
                          [bankB[bo]], h_OTBs, eng="act")
                self.release(bo)

        def rot_A(b, dst, dstB, k):
            self.copy(dst, banks[b][:, :], [bankB[b]], [dstB], eng="act")
            self.tt(a_rot[0:32, 2 * k, :], banks[b][0:32, :], rope[:, 0, :], ALU.mult, [bankB[b], ropeB], [a_rotB[k]])
            self.release(b)

        def rot_B(dst, dstB, k):
            b2 = self.bank()
            self.mm(banks[b2][0:32, :], permb[:, :], dst[0:32, :], True, True, [dstB, constB], [bankB[b2]])
            self.tt(a_rot[0:32, 2 * k + 1, :], banks[b2][0:32, :], rope[:, 1, :], ALU.mult, [bankB[b2], ropeB],
                    [a_rotB[k]])
            self.release(b2)
            self.tt(dst[0:32, :], a_rot[0:32, 2 * k, :], a_rot[0:32, 2 * k + 1, :], ALU.add, [a_rotB[k]], [dstB])

        def load_rope(ti):
            t0 = ti * T
            self.dma(rope[:, :, :], rope_d[:, :, t0:t0 + T].rearrange("a p t -> p a t"), (), [ropeB], chan="rope")

        self.step('consts a')
        enter_stage("P")
        self.dma(identf[:, :], identf_d[:, :], (), [constB], chan="c0")
        self.dma(gains[:, :, :], gains_d[:, :, :], (), [constB], chan="c0")
        self.dma(gng[:, :], gng_d[:, :], (), [constB], chan="c0")
        self.dma(lbt[:, :, :, :], lbl_d[:, :, :, :], (), [constB], chan="c0")
        self.dma(hmask[:, :, :], hmask_d[:, :, :], (), [constB], chan="c0")
        self.step('consts b')
        self.dma(p_st[:, 0, 0:3072], amask_d[:, :, :].rearrange("p a t -> p (a t)"), (), [p_stB[0]], chan="pst0")
        self.copy(amask[:, :, :].rearrange("p a t -> p (a t)"), p_st[:, 0, 0:3072], [p_stB[0]], [constB])
        self.step('consts c')
        self.dma(p_st[0:32, 1, 0:32], perm_d[:, :], (), [p_stB[1]], chan="pst1")
        self.copy(permb[:, :], p_st[0:32, 1, 0:32], [p_stB[1]], [constB])
        self.step('consts d')
        self.copy(identb[:, :], identf[:, :], [constB], [constB])
        self.memset(onesb[:, :], 1.0, [constB])
        onesf = self.sb("onesf", [128, 64], F32)
        self.memset(onesf[:, :], 1.0, [constB])
        self.memset(epsn[:, :], EPS, [constB])
        epsg = self.sb("epsg", [128, 1], F32)
        self.memset(epsg[:, :], EPS, [constB])
        self.step('consts e')
        self.tt(lb[:, :, :], lbt[:, :, 0, :], lbt[:, :, 1, :], ALU.subtract, [constB], [constB])
        self.step('consts e2')
        self.act(lb[:, :, :], lb[:, :, :], AF.Sigmoid, [constB], [constB])
        self.step('consts e3')
        self.ts(oml[:, :, :], lb[:, :, :], -1.0, 1.0, ALU.mult, ALU.add, [constB], [constB])
        self.step('consts e4')
        self.ts(noml[:, :, :], oml[:, :, :], -1.0, None, ALU.mult, None, [constB], [constB])

        self.step('wconv')
        p1_blocks = [self.widx["in"][i] for i in (4, 0, 5, 1, 6, 7, 11, 12, 14, 15, 17, 18)] + list(self.widx["xkv"])
        self.wgroup = {bi: (0 if bi in p1_blocks else 1) for bi in range(len(self.blocks))}
        self.wbfB = [Buf("wbf_g0"), Buf("wbf_g1")]

        def convert(bi):
            src, r0, kc, c0 = self.blocks[bi]
            w = wsrc[src]
            g = self.wgroup[bi]
            self.dma(self.wbf[bi, :, 0:kc * 512].rearrange("p (c n) -> p c n", c=kc),
                     w[r0:r0 + kc * 128, c0:c0 + 512].rearrange("(c p) n -> p c n", p=128), (), [self.wbfB[g]],
                     chan="cv%d" % g, q="pool")

        for bi in list(self.widx["xkv"]) + p1_blocks[:12]:
            convert(bi)
        conv_later = [bi for bi in range(len(self.blocks)) if self.wgroup[bi] == 1]

        self.step('memkv')
        self.dma(p_mem[:, :, :], mem[:, :].rearrange("(s p) d -> p s d", p=128), (), [p_memB], chan="pmem")
        for c in range(8):
            b = self.bank()
            for s2 in range(2):
                self.tr(banks[b][:, s2 * 128:(s2 + 1) * 128], p_mem[:, s2, c * 128:(c + 1) * 128], identf[:, :],
                        [p_memB, constB], [bankB[b]])
            self.copy(hT[:, c, 0:NMEM], banks[b][:, 0:NMEM], [bankB[b]], [hTB[c]], eng="act")
            self.release(b)
        rmsnorm(2, N=NMEM)
        self.wseq_pro = list(self.widx["xkv"])
        for n, blk in enumerate(self.widx["xkv"]):
            slot = n % NSLOT
            self.dma(self.wring[:, slot, :], self.wbf[blk, :, :], [self.wbfB[0]], [self.wringB[slot]],
                     chan="w%d" % slot)
            wv = self.wring[:, slot, :].rearrange("p (c n) -> p c n", c=8)
            if n < 2:
                for j in range(4):
                    b = proj_fm(wv, self.wringB[slot], 8, j, lambda c: uT[:, c, 0:NMEM], UT, N=NMEM)
                    self.copy(kxT[:, n * 4 + j, :], banks[b][:, 0:NMEM], [bankB[b]], [kvxB], eng="act")
                    self.release(b)
            else:
                for s2 in range(2):
                    b = self.bank()
                    for c in range(8):
                        self.mm(banks[b][:, :], uT[:, c, s2 * 128:(s2 + 1) * 128], wv[:, c, :], c == 0, c == 7,
                                [self.wringB[slot], uTBs[c]], [bankB[b]])
                    self.copy(vx[:, s2, (n - 2) * 512:(n - 1) * 512], banks[b][:, :], [bankB[b]], [kvxB], eng="act")
                    self.release(b)

        self.wseq_build()

        self.memset(Sst[:, :, :], 0.0, SstB)
        issue_x(NT - 1)
        for ti in range(NT - 1, -1, -1):
            t0 = ti * T
            self.step('p1 load %d' % ti)
            load_x(ti)
            issue_x(ti - 1 if ti > 0 else 0)
            load_rope(ti)
            self.step('p1 prep %d' % ti)
            hgrn_prep(1, ti)
            self.dma(vhs[ti, :, :], h_V[0:64, :, :].rearrange("q c n -> q (c n)"), [h_VB], (), chan="vhst", q="pool")
            self.step('p1 scan %d' % ti)
            hgrn_scan(1)
            self.step('p1 kv %d' % ti)
            for _ in range(3):
                if conv_later:
                    convert(conv_later.pop(0))
            self.dma(obs[ti, :, :], h_OT[:, :, :].rearrange("p h t -> p (h t)"), h_OTBs, (), chan="obst", q="sp")
            enter_stage("A")
            kpend = None

            def flush_k(pk):
                pgi, pj, psl = pk
                rot_B(a_kst[:, psl, :], a_kstB[psl], psl % 2)
                self.dma(kts[pgi, pj, :, t0:t0 + T], a_kst[:, psl, :], [a_kstB[psl]], (), chan="kst%d" % psl, q="pool")

            for gi in range(3):
                self.step('p1 k %d' % gi)
                wv, wB, kc = self.w_next(self.widx["in"][11 + 3 * gi])
                for j in range(4):
                    sl = j
                    b = proj_fm(wv, wB, kc, j, uT_of, UT)
                    if kpend is not None:
                        flush_k(kpend)
                    rot_A(b, a_kst[:, sl, :], a_kstB[sl], sl % 2)
                    kpend = (gi, j, sl)
                self.w_done()
                self.step('p1 v %d' % gi)
                wv, wB, kc = self.w_next(self.widx["in"][12 + 3 * gi])
                for s4 in range(4):
                    sl = s4
                    b = self.bank()
                    for c in range(8):
                        self.mm(banks[b][:, :], uT[:, c, s4 * 128:(s4 + 1) * 128], wv[:, c, :], c == 0, c == 7,
                                [wB, uTBs[c]], [bankB[b]])
                    self.copy(a_vst[:, sl, :], banks[b][:, :], [bankB[b]], [a_vstB[sl]], eng="act")
                    self.release(b)
                    self.dma(vsc[gi, t0 + s4 * 128:t0 + (s4 + 1) * 128, :], a_vst[:, sl, :], [a_vstB[sl]], (),
                             chan="vst%d" % sl, q="pool")
                self.w_done()
            flush_k(kpend)
        while conv_later:
            convert(conv_later.pop(0))
        fence = [("#" + c, self.P.chan.get(c, 0)) for c in ("obst", "kst0", "kst1", "kst2", "kst3", "vst0", "vst1", "vst2", "vst3", "vhst")]
        for bb in a_KTB + a_VB + oblB + [h_VB]:
            for s, i in fence:
                if i:
                    bb.r[s] = max(bb.r.get(s, 0), i)

        self.step('p2 init')
        self.memset(Sst[:, :, :], 0.0, SstB)
        enter_stage("A")
        self.memset(a_KT[:, :, :], 0.0, a_KTB, eng="pool")
        self.memset(a_V[:, :, :, :], 0.0, a_VB, eng="pool")

        KOFF = (0, 640, 1664)
        VOFF = (0, 5, 13)

        def load_windows(ti, head, sl):
            t0 = ti * T
            for gi, dil in enumerate(GROUPS):
                lo = t0 - 64 * dil
                hi = t0 + T + 64 * dil
                clo, chi = max(lo, 0), min(hi, S)
                self.dma(a_KT[:, sl, KOFF[gi] + (clo - lo):KOFF[gi] + (chi - lo)], kts[gi, head, :, clo:chi], (),
                         [a_KTB[sl]], chan="akt%d" % sl)
                L = S // dil
                mq0 = t0 // dil
                nqb = 4 if dil == 1 else 1
                for blk in range(nqb + 1):
                    m0 = mq0 - 64 + 128 * blk
                    vlo, vhi = max(0, -m0), min(128, L - m0)
                    if dil == 16 and blk == 1:
                        vhi = min(vhi, 32)
                    if vhi <= vlo:
                        continue
                    b0 = VOFF[gi] + blk
                    tok0 = dil * (m0 + vlo)
                    tok1 = dil * (m0 + vhi)
                    if dil == 1:
                        self.dma(a_V[vlo:vhi, sl, b0, :], vsc[gi, tok0:tok1, head * 128:(head + 1) * 128], (),
                                 [a_VB[sl]], chan="av%d" % sl)
                    else:
                        self.dma(a_V[vlo:vhi, sl, b0:b0 + (dil - 1) * (nqb + 1) + 1:(nqb + 1), :],
                                 vsc[gi, tok0:tok1, head * 128:(head + 1) * 128].rearrange("(i r) d -> i r d", r=dil),
                                 (), [a_VB[sl]], chan="av%d" % sl)

        def attention(ti):
            t0 = ti * T
            enter_stage("A")
            load_windows(ti, 0, 0)
            load_windows(ti, 1, 1)
            pend = None
            for gi in range(3):
                wv, wB, kc = self.w_next(self.widx["in"][10 + 3 * gi])
                for j in range(4):
                    qi = gi * 4 + j
                    b = proj_fm(wv, wB, kc, j, uT_of, UT)
                    if pend is not None:
                        rot_B(a_QT[:, pend, :], a_QTB[pend], pend % 2)
                    rot_A(b, a_QT[:, qi, :], a_QTB[qi], qi % 2)
                    pend = qi
                self.w_done()
            rot_B(a_QT[:, pend, :], a_QTB[pend], pend % 2)
            for head in range(4):
                sl = head % 2
                bo = self.bank()
                bd = self.bank()
                self.memset(banks[bo][:, :], 0.0, [bankB[bo]])
                self.memset(banks[bd][:, :], 0.0, [bankB[bd]])
                jobs = []
                for gi, dil in enumerate(GROUPS):
                    L = S // dil
                    mq0 = t0 // dil
                    nqb = 4 if dil == 1 else 1
                    for kb in range(2):
                        units = []
                        for r in range(dil):
                            for qb in range(nqb):
                                blk = qb + kb
                                m0 = mq0 - 64 + 128 * blk
                                vlo, vhi = max(0, -m0), min(128, L - m0)
                                if vhi <= vlo:
                                    continue
                                units.append((r, qb, blk, m0, vlo, vhi))
                        if units:
                            jobs.append((gi, dil, kb, nqb, units))

                def scores(job, ps):
                    gi, dil, kb, nqb, units = job
                    qh = a_QT[:, gi * 4 + head, :]
                    bs = self.bank()
                    full = len(units) == dil * nqb
                    short = (dil == 16 and kb == 1)
                    if (not full) or short:
                        self.memset(banks[bs][:, :], -30000.0, [bankB[bs]])
                    nk = 32 if short else 128
                    for (r, qb, blk, m0, vlo, vhi) in units:
                        woff = KOFF[gi] + (dil * m0 + r) - (t0 - 64 * dil)
                        kap = a_KT[:, sl, woff:woff + (nk - 1) * dil + 1:dil]
                        if dil == 1:
                            qap = qh[:, qb * 128:(qb + 1) * 128]
                            oap = banks[bs][0:nk, qb * 128:(qb + 1) * 128]
                        else:
                            qap = qh[:, r::dil]
                            oap = banks[bs][0:nk, r::dil]
                        self.mm(oap, kap, qap, True, True, [a_KTB[sl], a_QTB[gi * 4 + head]], [bankB[bs]], skip=True)
                    self.act(a_PT[:, ps, :], banks[bs][:, :], AF.Exp, [bankB[bs]], [a_PTB[ps]], scale=ATT_SCALE)
                    self.release(bs)
                    self.tt(a_PT[:, ps, :], a_PT[:, ps, :], amask[:, gi * 2 + kb, :], ALU.mult, [a_PTB[ps], constB],
                            [a_PTB[ps]])
                    if dil == 1:
                        for (r, qb, blk, m0, vlo, vhi) in units:
                            if vlo > 0:
                                self.memset(a_PT[0:vlo, ps, qb * 128:(qb + 1) * 128], 0.0, [a_PTB[ps]])
                            if vhi < 128:
                                self.memset(a_PT[vhi:128, ps, qb * 128:(qb + 1) * 128], 0.0, [a_PTB[ps]])
                    else:
                        rows = set((u[4], u[5]) for u in units)
                        assert len(rows) == 1, rows
                        vlo, vhi = units[0][4], units[0][5]
                        if vlo > 0:
                            self.memset(a_PT[0:vlo, ps, :], 0.0, [a_PTB[ps]])
                        if vhi < 128 and not (short and vhi >= 32):
                            self.memset(a_PT[vhi:128, ps, :], 0.0, [a_PTB[ps]])

                def pv(job, ps):
                    gi, dil, kb, nqb, units = job
                    self.mm(banks[bd][:, :], onesb[:, :], a_PT[:, ps, :], False, False, [a_PTB[ps], constB],
                            [bankB[bd]], skip=True)
                    for (r, qb, blk, m0, vlo_, vhi_) in units:
                        bidx = VOFF[gi] + r * (nqb + 1) + blk
                        if dil == 1:
                            pap = a_PT[:, ps, qb * 128:(qb + 1) * 128]
                            oap = banks[bo][:, qb * 128:(qb + 1) * 128]
                        else:
                            pap = a_PT[:, ps, r::dil]
                            oap = banks[bo][:, r::dil]
                        self.mm(oap, a_V[:, sl, bidx, :], pap, False, False, [a_VB[sl], a_PTB[ps]], [bankB[bo]],
                                skip=True)

                pendj = None
                for n, job in enumerate(jobs):
                    scores(job, n % 2)
                    if pendj is not None:
                        pv(*pendj)
                    pendj = (job, n % 2)
                pv(*pendj)
                self.act(a_rd[:, :], banks[bd][:, :], AF.Ln, [bankB[bd]], [a_rdB])
                self.release(bd)
                self.act(a_rd[:, :], a_rd[:, :], AF.Exp, [a_rdB], [a_rdB], scale=-1.0)
                self.tt(a_out[:, head, :], banks[bo][:, :], a_rd[:, :], ALU.mult, [bankB[bo], a_rdB], [a_outB])
                self.release(bo)
                if head + 2 < 4:
                    load_windows(ti, head + 2, sl)

        def hgrn_finish(ti):
            alias_fence([yinB, mergedB], [xsB])
            sg8 = merged
            rs2 = [rs, sgt]
            rs2B = [rsB, sgtB]
            fin_banks = {}

            def load_ob(h):
                self.dma(obl[:, h % 2, :], obs[ti, :, h * T:(h + 1) * T], (), [oblB[h % 2]], chan="obl%d" % (h % 2))

            load_ob(0)
            load_ob(1)
            for g in range(2):
                wv, wB, kc = self.w_next(self.widx["in"][8 + g])
                for j in range(4):
                    h = g * 4 + j
                    b = proj_fm(wv, wB, kc, j, uT_of, UT)
                    self.act(sg8[:, h, :], banks[b][:, :], AF.Silu, [bankB[b]], [mergedB])
                    self.release(b)
                self.w_done()

            def front_dve(h):
                p = h % 2
                self.tt(h_OT[:, h, :], h_OT[:, h, :], obl[:, p, :], ALU.add, [oblB[p], h_OTBs[h]], [h_OTBs[h]])
                if h + 2 < 8:
                    load_ob(h + 2)

            def front_act_pe(h):
                p = h % 2
                self.act(sqb[:, p, :], h_OT[:, h, :], AF.Square, [h_OTBs[h]], [sqbB[p]])
                b2 = self.bank()
                self.mm(banks[b2][:, :], onesb[:, :], sqb[:, p, :], True, True, [sqbB[p], constB], [bankB[b2]])
                fin_banks[h] = b2

            def back_act(h):
                p = h % 2
                b2 = fin_banks.pop(h)
                self.act(rs2[p][:, :], banks[b2][:, :], AF.Ln, [bankB[b2], constB], [rs2B[p]], bias=epsg[:, 0:1],
                         scale=1.0 / 128)
                self.release(b2)
                self.act(rs2[p][:, :], rs2[p][:, :], AF.Exp, [rs2B[p]], [rs2B[p]], scale=-0.5)

            def back_dve(h):
                p = h % 2
                self.stt(h_OT[:, h, :], h_OT[:, h, :], gng[:, 0:1], rs2[p][:, :], ALU.mult, ALU.mult,
                         [h_OTBs[h], rs2B[p], constB], [h_OTBs[h]])
                self.tt(yin[:, h, :], h_OT[:, h, :], sg8[:, h, :], ALU.mult, [h_OTBs[h], mergedB], [yinB])

            front_dve(0)
            front_act_pe(0)
            front_dve(1)
            front_act_pe(1)
            for h in range(8):
                if h + 2 < 8:
                    front_dve(h + 2)
                back_act(h)
                if h + 2 < 8:
                    front_act_pe(h + 2)
                back_dve(h)

        def merge_and_out():
            for bq in range(2):
                who, whoB, _ = self.w_next(self.widx["ho"][bq])
                wgh, wghB, _ = self.w_next(self.widx["in"][19 + bq])
                for j in range(4):
                    jj = bq * 4 + j
                    p = j % 2
                    b = proj_fm(wgh, wghB, 8, j, uT_of, UT)
                    self.act(gtmp[:, p, :], banks[b][:, :], AF.Sigmoid, [bankB[b]], [gtmpB[p]])
                    self.release(b)
                    b = proj_fm(who, whoB, 8, j, lambda c: yin[:, c, :], [yinB])
                    self.tt(merged[:, jj, :], banks[b][:, :], gtmp[:, p, :], ALU.mult, [bankB[b], gtmpB[p]], [mergedB])
                    self.release(b)
                self.w_done()
                wao, waoB, _ = self.w_next(self.widx["ao"][bq])
                wga, wgaB, _ = self.w_next(self.widx["in"][21 + bq])
                for j in range(4):
                    jj = bq * 4 + j
                    p = j % 2
                    b = proj_fm(wga, wgaB, 8, j, uT_of, UT)
                    self.act(gtmp[:, p, :], banks[b][:, :], AF.Sigmoid, [bankB[b]], [gtmpB[p]])
                    self.release(b)
                    b = proj_fm(wao, waoB, 4, j, lambda c: a_out[:, c, :], [a_outB])
                    self.tt(gtmp[:, p, :], banks[b][:, :], gtmp[:, p, :], ALU.mult, [bankB[b], gtmpB[p]], [gtmpB[p]])
                    self.release(b)
                    self.tt(merged[:, jj, :], merged[:, jj, :], gtmp[:, p, :], ALU.add, [mergedB, gtmpB[p]], [mergedB],
                            eng="pool")
                self.w_done()
            nb = self.bank()
            for bq in range(2):
                wv, wB, _ = self.w_next(self.widx["wo"][bq])
                for j in range(4):
                    jj = bq * 4 + j
                    b = proj_fm(wv, wB, 8, j, lambda c: merged[:, c, :], [mergedB])
                    if jj > 0:
                        norm_mm(nb, jj - 1)
                    self.tt(hT[:, jj, :], hT[:, jj, :], banks[b][:, :], ALU.add, [hTB[jj], bankB[b]], [hTB[jj]])
                    self.release(b)
                    norm_sq(jj)
                self.w_done()
            norm_mm(nb, 7)
            return nb

        def cross_attention(nb_in):
            enter_stage("C")
            norm_end(nb_in, 1)
            for bq in range(2):
                wv, wB, _ = self.w_next(self.widx["xq"][bq])
                for j in range(4):
                    b = proj_fm(wv, wB, 8, j, uT_of, UT)
                    self.copy(c_qx[:, bq * 4 + j, :], banks[b][:, :], [bankB[b]], [c_qxB], eng="act")
                    self.release(b)
                self.w_done()
            for hx in range(4):
                bd = self.bank()
                for mb in range(2):
                    bs = self.bank()
                    for dc in range(2):
                        self.mm(banks[bs][:, :], kxT[:, hx * 2 + dc, mb * 128:(mb + 1) * 128], c_qx[:, hx * 2 + dc, :],
                                dc == 0, dc == 1, [kvxB, c_qxB], [bankB[bs]])
                    ps = (hx * 2 + mb) % 4
                    self.act(c_PT[:, ps, :], banks[bs][:, :], AF.Exp, [bankB[bs]], [c_PTB[ps]], scale=XA_SCALE)
                    self.release(bs)
                    self.mm(banks[bd][:, :], onesb[:, :], c_PT[:, ps, :], mb == 0, mb == 1, [c_PTB[ps], constB],
                            [bankB[bd]])
                self.act(c_rd[:, :], banks[bd][:, :], AF.Ln, [bankB[bd]], [c_rdB])
                self.release(bd)
                self.act(c_rd[:, :], c_rd[:, :], AF.Exp, [c_rdB], [c_rdB], scale=-1.0)
                for dc in range(2):
                    bo = self.bank()
                    for mb in range(2):
                        ps = (hx * 2 + mb) % 4
                        self.mm(banks[bo][:, :], vx[:, mb, (hx * 2 + dc) * 128:(hx * 2 + dc + 1) * 128], c_PT[:, ps, :],
                                mb == 0, mb == 1, [kvxB, c_PTB[ps]], [bankB[bo]])
                    self.tt(c_ox[:, hx * 2 + dc, :], banks[bo][:, :], c_rd[:, :], ALU.mult, [bankB[bo], c_rdB],
                            [c_oxB])
                    self.release(bo)
            nb = self.bank()
            for bq in range(2):
                wv, wB, _ = self.w_next(self.widx["xo"][bq])
                for j in range(4):
                    jj = bq * 4 + j
                    b = proj_fm(wv, wB, 8, j, lambda c: c_ox[:, c, :], [c_oxB])
                    if jj > 0:
                        norm_mm(nb, jj - 1)
                    self.tt(hT[:, jj, :], hT[:, jj, :], banks[b][:, :], ALU.add, [hTB[jj], bankB[b]], [hTB[jj]])
                    self.release(b)
                    norm_sq(jj)
                self.w_done()
            norm_mm(nb, 7)
            return nb

        def ffn(nb_in):
            norm_end(nb_in, 3)
            nbank = self.bank()
            for bq in range(8):
                wv, wB, _ = self.w_next(self.widx["f1"][bq])
                for j in range(4):
                    sl = j % 2
                    b = proj_fm(wv, wB, 8, j, uT_of, UT)
                    self.act(c_rl[:, sl, :], banks[b][:, :], AF.Relu, [bankB[b]], [c_rlB[sl]])
                    self.release(b)
                    self.tt(c_hid[:, bq * 4 + j, :], c_rl[:, sl, :], c_rl[:, sl, :], ALU.mult, [c_rlB[sl]], [c_hidB],
                            eng="pool")
                self.w_done()
            for nb in range(2):
                accs = [self.bank() for _ in range(4)]
                for kg in range(4):
                    wv, wB, _ = self.w_next(self.widx["f2"][nb * 4 + kg])
                    for j in range(4):
                        for c in range(8):
                            self.mm(banks[accs[j]][:, :], wv[:, c, j * 128:(j + 1) * 128], c_hid[:, kg * 8 + c, :],
                                    kg == 0 and c == 0, kg == 3 and c == 7, [wB, c_hidB], [bankB[accs[j]]])
                    self.w_done()
                for j in range(4):
                    jj = nb * 4 + j
                    self.tt(hT[:, jj, :], hT[:, jj, :], banks[accs[j]][:, :], ALU.add, [hTB[jj], bankB[accs[j]]],
                            [hTB[jj]])
                    self.release(accs[j])
                    norm_acc(nbank, jj)
            return nbank

        def final_out(ti, nb_in):
            t0 = ti * T
            b = nb_in
            self.act(rs[:, :], banks[b][:, :], AF.Ln, [bankB[b], constB], [rsB], bias=epsn[:, 0:1], scale=1.0 / D)
            self.release(b)
            self.act(rinvn[:, :], rs[:, :], AF.Exp, [rsB], [rinvnB], scale=-0.5)
            for half in range(2):
                bts = [self.bank() for _ in range(4)]
                for cc in range(4):
                    c = half * 4 + cc
                    sl = c % 2
                    self.stt(c_yT[:, sl, :], hT[:, c, :], gains[:, 4, c:c + 1], rinvn[:, :], ALU.mult, ALU.mult,
                             [hTB[c], rinvnB, constB], [c_yTB[sl]])
                    for s4 in range(4):
                        self.tr(banks[bts[s4]][:, cc * 128:(cc + 1) * 128], c_yT[:, sl, s4 * 128:(s4 + 1) * 128],
                                identf[:, :], [c_yTB[sl], constB], [bankB[bts[s4]]])
                for s4 in range(4):
                    ys = half * 4 + s4
                    self.copy(c_ysb[:, ys, :], banks[bts[s4]][:, :], [bankB[bts[s4]]], [c_ysbB[ys]],
                              eng=("act" if s4 % 2 else "dve"))
                    self.release(bts[s4])
                    self.dma(y[t0 + s4 * 128:t0 + (s4 + 1) * 128, half * 512:(half + 1) * 512], c_ysb[:, ys, :],
                             [c_ysbB[ys]], (), chan="yst%d" % ys, q="pool")

        for ti in range(NT):
            self.step('p2 load %d' % ti)
            load_x(ti)
            load_rope(ti)
            self.step('p2 prep %d' % ti)
            hgrn_prep(0, ti)
            self.step('p2 scan %d' % ti)
            hgrn_scan(0)
            self.step('p2 finish %d' % ti)
            hgrn_finish(ti)
            self.step('p2 attn %d' % ti)
            attention(ti)
            self.step('p2 merge %d' % ti)
            nb1 = merge_and_out()
            if ti + 1 < NT:
                issue_x(ti + 1)
            self.step('p2 cross %d' % ti)
            nb2 = cross_attention(nb1)
            self.step('p2 ffn %d' % ti)
            nb3 = ffn(nb2)
            self.step('p2 final %d' % ti)
            final_out(ti, nb3)
        assert self.wpos == len(self.wseq), (self.wpos, len(self.wseq))
        if self.P.disabled:
            self.P.disabled = False
            dummyB = Buf("dummy")
            self.dma(sgt[:, 0:256], obs[0, :, 0:256], (), [dummyB], chan="dmy")
            self.dma(sqb[:, 0, 0:256], kts[0, 0, :, 0:256], (), [dummyB], chan="dmy")
            self.dma(sqb[:, 1, 0:256], vsc[0, 0:128, 0:256], (), [dummyB], chan="dmy")

        P.finalize()
        chans = sorted(P.chan.keys())
        sems = {}
        for e in ("pe", "act", "dve", "pool", "sp"):
            sems[e] = self.es.enter_context(nc.semaphore("s_" + e))
        chansems = {c: self.es.enter_context(nc.semaphore("c_" + c)) for c in chans}
        with nc.Block() as block:
            @block.tensor
            def _(e):
                P.emit("pe", e, sems, chansems)

            @block.scalar
            def _(e):
                P.emit("act", e, sems, chansems)

            @block.vector
            def _(e):
                P.emit("dve", e, sems, chansems)

            @block.gpsimd
            def _(e):
                P.emit("pool", e, sems, chansems, final_wait=True)

            @block.sync
            def _(e):
                P.emit("sp", e, sems, chansems)
        self.es.close()
        return nc


def host_consts(S):
    half = 16
    inv = (500000.0 ** (-np.arange(half, dtype=np.float32) * 2.0 / 32)).astype(np.float32)
    pos = np.arange(S, dtype=np.float32)
    ang = (pos[:, None] * inv[None, :]).astype(np.float32)
    cos = np.cos(ang).astype(np.float32).T
    sin = np.sin(ang).astype(np.float32).T
    rope = np.zeros((2, 32, S), np.float32)
    rope[0, 0:16] = cos
    rope[0, 16:32] = cos
    rope[1, 0:16] = -sin
    rope[1, 16:32] = sin
    perm = np.zeros((32, 32), np.float32)
    for m in range(32):
        perm[(m + 16) % 32, m] = 1.0
    s = np.arange(64)[:, None]
    t = np.arange(64)[None, :]
    hmask = np.zeros((64, 2, 512), np.float32)
    hmask[:, 0, :] = np.tile((s <= t).astype(np.float32), (1, 8))
    hmask[:, 1, :] = np.tile((s >= t).astype(np.float32), (1, 8))
    i = np.arange(128)[:, None]
    tt = np.arange(512)[None, :]
    amask = np.zeros((128, 6, 512), np.float32)
    for gi, dil in enumerate(GROUPS):
        j = (tt % 128) if dil == 1 else (tt // dil)
        amask[:, gi * 2 + 0, :] = (i >= j)
        amask[:, gi * 2 + 1, :] = (i <= j)
    return dict(rope=rope, perm=perm, hmask=hmask, amask=amask, identf=np.eye(128, dtype=np.float32))


def chunkmajor(g):
    return np.ascontiguousarray(np.asarray(g, np.float32).reshape(8, 128).T)


_CACHE = {}
_LIMIT = 10 ** 9
_SKIP = set()
_DEBUG = False


def run_cores(xs, mems, wts, S):
    if S not in _CACHE:
        _CACHE[S] = Kern(S, limit=_LIMIT, debug=_DEBUG).build()
    nc = _CACHE[S]
    hc = host_consts(S)
    gains = np.stack([chunkmajor(wts["mix_norm_g"]), chunkmajor(wts["xa_norm_g"]), chunkmajor(wts["mem_norm_g"]),
                      chunkmajor(wts["ffn_norm_g"]), chunkmajor(wts["final_norm_g"])], axis=1)
    gng = np.asarray(wts["hgrn_gnorm_g"], np.float32).reshape(128, 1)
    lbl = np.ascontiguousarray(np.asarray(wts["hgrn_lb_logits"], np.float32).reshape(2, 2, 8, 128).transpose(3, 0, 1, 2))
    shared = {
        "w_in": np.ascontiguousarray(wts["w_in"].reshape(D, NIN)),
        "w_hgrn_o": np.ascontiguousarray(wts["w_hgrn_o"].reshape(D, D)),
        "w_attn_o": np.ascontiguousarray(wts["w_attn_o"].reshape(512, D)),
        "w_out": np.ascontiguousarray(wts["w_out"].reshape(D, D)),
        "w_xq": np.ascontiguousarray(wts["w_xq"].reshape(D, D)),
        "w_xkv": np.ascontiguousarray(wts["w_xkv"].reshape(D, 2 * D)),
        "w_xo": np.ascontiguousarray(wts["w_xo"].reshape(D, D)),
        "w_ffn1": np.ascontiguousarray(wts["w_ffn1"].reshape(D, 4 * D)),
        "w_ffn2": np.ascontiguousarray(wts["w_ffn2"].reshape(4 * D, D)),
        "gains": np.ascontiguousarray(gains), "gng": gng, "lbl": lbl,
        "rope": hc["rope"], "identf": hc["identf"], "perm": hc["perm"], "hmask": hc["hmask"], "amask": hc["amask"],
    }
    in_maps = []
    for xc, mc in zip(xs, mems):
        m = dict(shared)
        m["x"] = np.ascontiguousarray(xc, dtype=np.float32)
        m["mem"] = np.ascontiguousarray(mc, dtype=np.float32)
        in_maps.append(m)
    res = run_bass_kernel_spmd(nc, in_maps, core_ids=list(range(len(xs))))
    return [r["y"] for r in res.results]


def kernel(x_prompt, x_sample, mem_prompt, mem_sample, mix_norm_g, w_in, hgrn_lb_logits, hgrn_gnorm_g,
           w_hgrn_o, w_attn_o, w_out, xa_norm_g, mem_norm_g, w_xq, w_xkv, w_xo, ffn_norm_g, w_ffn1,
           w_ffn2, final_norm_g):
    f = lambda a: np.asarray(a, dtype=np.float32)
    x_prompt, x_sample, mem_prompt, mem_sample = f(x_prompt), f(x_sample), f(mem_prompt), f(mem_sample)
    wts = dict(mix_norm_g=f(mix_norm_g), w_in=f(w_in), hgrn_lb_logits=f(hgrn_lb_logits), hgrn_gnorm_g=f(hgrn_gnorm_g),
               w_hgrn_o=f(w_hgrn_o), w_attn_o=f(w_attn_o), w_out=f(w_out), xa_norm_g=f(xa_norm_g),
               mem_norm_g=f(mem_norm_g), w_xq=f(w_xq), w_xkv=f(w_xkv), w_xo=f(w_xo), ffn_norm_g=f(ffn_norm_g),
               w_ffn1=f(w_ffn1), w_ffn2=f(w_ffn2), final_norm_g=f(final_norm_g))
    S = x_prompt.shape[1]
    seqs = [x_prompt[0], x_prompt[1], x_sample[0], x_sample[1], x_sample[2], x_sample[3], x_prompt[0], x_prompt[1]]
    mems = [mem_prompt[0], mem_prompt[1], mem_sample[0], mem_sample[1], mem_sample[2], mem_sample[3],
            mem_prompt[0], mem_prompt[1]]
    ys = run_cores(seqs, mems, wts, S)
    y_prompt = np.stack([ys[0], ys[1]], axis=0).astype(np.float32)
    y_sample = np.stack([ys[2], ys[3], ys[4], ys[5]], axis=0).astype(np.float32)
    return (y_prompt, y_sample)
```

```python
import numpy as np
from contextlib import ExitStack
import concourse.bass as bass
import concourse.mybir as mybir
from concourse.bass_utils import run_bass_kernel_spmd

F32 = mybir.dt.float32
BF16 = mybir.dt.bfloat16
AF = mybir.ActivationFunctionType
ALU = mybir.AluOpType

D = 1024
T = 512
NIN = 11776
NMEM = 256
EPS = 1e-6
NSLOT = 4
GROUPS = (1, 4, 16)
ATT_SCALE = 128 ** -0.5
XA_SCALE = 256 ** -0.5


class Buf:
    __slots__ = ("name", "w", "r", "excl")

    def __init__(self, name, excl=False):
        self.name = name
        self.w = None
        self.r = {}
        self.excl = excl


class Sched:
    def __init__(self, same_engine_raw=True):
        self.streams = {e: [] for e in ("pe", "act", "dve", "pool", "sp")}
        self.chan = {}
        self.ser = same_engine_raw
        self.disabled = False

    def add(self, eng, fn, reads=(), writes=(), chan=None, extra=None):
        if self.disabled:
            return None
        deps = dict(extra) if extra else {}
        mychan = None if chan is None else "#" + chan

        def need(a, raw):
            if a is None:
                return
            s, i = a
            if s == eng:
                if not raw or not self.ser or eng in ("pe", "sp"):
                    return
            if mychan is not None and s == mychan:
                return
            if deps.get(s, 0) < i:
                deps[s] = i

        for b in reads:
            need(b.w, True)
            if b.excl:
                for s, i in b.r.items():
                    need((s, i), False)
        for b in writes:
            need(b.w, False)
            for s, i in b.r.items():
                need((s, i), False)
        lst = self.streams[eng]
        rec = [fn, deps, False, chan, 0]
        lst.append(rec)
        if chan is None:
            me = (eng, len(lst))
        else:
            c = self.chan.get(chan, 0) + 1
            self.chan[chan] = c
            me = (mychan, c)
        for b in reads:
            if b.r.get(me[0], 0) < me[1]:
                b.r[me[0]] = me[1]
        for b in writes:
            b.w = me
            b.r = {}
        return me

    def finalize(self):
        for lst in self.streams.values():
            for rec in lst:
                for s, i in rec[1].items():
                    if s[0] != "#":
                        self.streams[s][i - 1][2] = True
        for lst in self.streams.values():
            cum = 0
            for rec in lst:
                if rec[2]:
                    cum += 1
                rec[4] = cum

    def emit(self, eng, e, sems, chansems, final_wait=False):
        seen = {}
        for rec in self.streams[eng]:
            fn, deps, marked, chan, _ = rec
            for s, i in deps.items():
                if s[0] == "#":
                    sem = chansems[s[1:]]
                    v = 16 * i
                else:
                    sem = sems[s]
                    v = self.streams[s][i - 1][4]
                if seen.get(s, 0) >= v:
                    continue
                seen[s] = v
                e.wait_ge(sem, v)
            ins = fn(e)
            if chan is not None:
                ins.then_inc(chansems[chan], 16)
            elif marked:
                ins.then_inc(sems[eng], 1)
        if final_wait:
            for c, n in self.chan.items():
                e.wait_ge(chansems[c], 16 * n)


def weight_blocks():
    blocks = []
    index = {}

    def addw(key, src, K, N):
        ids = []
        for nb in range(N // 512):
            for kg in range(max(1, K // 1024)):
                kc = min(8, K // 128)
                ids.append(len(blocks))
                blocks.append((src, kg * 1024, kc, nb * 512))
        index[key] = ids

    addw("in", "w_in", 1024, NIN)
    addw("ho", "w_hgrn_o", 1024, 1024)
    addw("ao", "w_attn_o", 512, 1024)
    addw("wo", "w_out", 1024, 1024)
    addw("xq", "w_xq", 1024, 1024)
    addw("xkv", "w_xkv", 1024, 2048)
    addw("xo", "w_xo", 1024, 1024)
    addw("f1", "w_ffn1", 1024, 4096)
    addw("f2", "w_ffn2", 4096, 1024)
    return blocks, index


class StopBuild(Exception):
    pass


class Kern:
    def __init__(self, S, limit=99, debug=False):
        self.limit = limit
        self.debug = debug
        self.dbg_outs = []
        self.stepno = 0
        self.S = S
        self.NT = S // T
        self.nc = bass.Bass("TRN2", target_bir_lowering=False)
        self.P = Sched()
        self.es = ExitStack()
        self.blocks, self.widx = weight_blocks()

    def sb(self, name, shape, dt):
        return self.es.enter_context(self.nc.sbuf_tensor("sb_" + name, shape, dt))

    def din(self, name, shape, dt=F32):
        return self.nc.dram_tensor(name, shape, dt, kind="ExternalInput").ap()

    def mm(self, out, lhsT, rhs, start, stop, reads, writes, skip=False):
        self.P.add("pe", lambda e: e.matmul(out, lhsT, rhs, start=start, stop=stop, skip_group_check=skip),
                   reads, writes)

    def tr(self, out, in_, ident, reads, writes):
        self.P.add("pe", lambda e: e.transpose(out, in_, ident), reads, writes)

    def act(self, out, in_, func, reads, writes, bias=None, scale=None, eng="act"):
        kw = {}
        if bias is not None:
            kw["bias"] = bias
        if scale is not None:
            kw["scale"] = scale
        self.P.add("act", lambda e: e.activation(out, in_, func, **kw), reads, writes)

    def tt(self, out, in0, in1, op, reads, writes, eng="dve"):
        self.P.add(eng, lambda e: e.tensor_tensor(out, in0, in1, op), reads, writes)

    def ts(self, out, in0, s1, s2, op0, op1, reads, writes, eng="dve"):
        if op1 is None:
            self.P.add(eng, lambda e: e.tensor_scalar(out, in0, s1, None, op0), reads, writes)
        else:
            self.P.add(eng, lambda e: e.tensor_scalar(out, in0, s1, s2, op0, op1), reads, writes)

    def stt(self, out, in0, scalar, in1, op0, op1, reads, writes):
        self.P.add("dve", lambda e: e.scalar_tensor_tensor(out, in0, scalar, in1, op0, op1), reads, writes)

    def copy(self, out, in_, reads, writes, eng="dve"):
        if eng == "act":
            self.P.add("act", lambda e: e.activation(out, in_, AF.Copy), reads, writes)
        else:
            self.P.add(eng, lambda e: e.tensor_copy(out, in_), reads, writes)

    def memset(self, ap, val, writes, eng="dve"):
        self.P.add(eng, lambda e: e.memset(ap, val), (), writes)

    def dma(self, out, in_, reads, writes, chan, q="sp"):
        self.P.add(q, lambda e: e.dma_start(out=out, in_=in_), reads, writes, chan=chan)

    def step(self, name):
        self.stepno += 1
        self.P.disabled = (self.stepno > self.limit) or (self.stepno in _SKIP)
        if self.debug:
            print("step", self.stepno, name)

    def dbg(self, name, ap, bufs, dt=F32):
        if not self.debug:
            return
        shape = list(ap.shape)
        t = self.nc.dram_tensor(name, shape, dt, kind="ExternalOutput").ap()
        self.dbg_outs.append(name)
        self.dma(t, ap, bufs, (), chan="dbg", q="pool")

    def bank(self):
        i = self.free_banks.pop(0)
        return i

    def release(self, i):
        self.free_banks.append(i)

    def wseq_build(self):
        W = self.widx
        seq = []
        p1 = [W["in"][b] for b in (4, 0, 5, 1, 6, 7, 11, 12, 14, 15, 17, 18)]
        for _ in range(self.NT):
            seq += p1
        p2 = [W["in"][b] for b in (2, 0, 3, 1, 8, 9, 10, 13, 16)]
        for b in range(2):
            p2 += [W["ho"][b], W["in"][19 + b], W["ao"][b], W["in"][21 + b]]
        p2 += W["wo"] + W["xq"] + W["xo"] + W["f1"] + W["f2"]
        for _ in range(self.NT):
            seq += p2
        self.wseq = seq
        self.wpos = 0
        self.wissued = 0
        self.wdone = 0

    def w_issue(self):
        i = self.wissued
        slot = i % NSLOT
        blk = self.wseq[i]
        kc = self.blocks[blk][2]
        self.dma(self.wring[:, slot, 0:kc * 512], self.wbf[blk, :, 0:kc * 512], [self.wbfB[self.wgroup[blk]]],
                 [self.wringB[slot]], chan="w%d" % slot)
        self.wissued += 1

    def w_prefetch(self):
        while self.wissued < len(self.wseq) and self.wissued - NSLOT < self.wdone:
            self.w_issue()

    def w_next(self, blk):
        i = self.wpos
        assert self.wseq[i] == blk, (i, self.wseq[i], blk)
        self.w_prefetch()
        assert self.wissued > i, (i, self.wissued, self.wdone)
        slot = i % NSLOT
        self.wpos += 1
        kc = self.blocks[blk][2]
        view = self.wring[:, slot, 0:kc * 512].rearrange("p (c n) -> p c n", c=kc)
        return view, self.wringB[slot], kc

    def w_done(self):
        self.wdone = self.wpos
        self.w_prefetch()

    def build(self):
        nc, P, S, NT = self.nc, self.P, self.S, self.NT
        x = self.din("x", [S, D])
        mem = self.din("mem", [NMEM, D])
        wsrc = {
            "w_in": self.din("w_in", [D, NIN]), "w_hgrn_o": self.din("w_hgrn_o", [D, D]),
            "w_attn_o": self.din("w_attn_o", [512, D]), "w_out": self.din("w_out", [D, D]),
            "w_xq": self.din("w_xq", [D, D]), "w_xkv": self.din("w_xkv", [D, 2 * D]),
            "w_xo": self.din("w_xo", [D, D]), "w_ffn1": self.din("w_ffn1", [D, 4 * D]),
            "w_ffn2": self.din("w_ffn2", [4 * D, D]),
        }
        gains_d = self.din("gains", [128, 5, 8])
        gng_d = self.din("gng", [128, 1])
        lbl_d = self.din("lbl", [128, 2, 2, 8])
        rope_d = self.din("rope", [2, 32, S])
        identf_d = self.din("identf", [128, 128])
        perm_d = self.din("perm", [32, 32])
        hmask_d = self.din("hmask", [64, 2, 512])
        amask_d = self.din("amask", [128, 6, 512])
        y = nc.dram_tensor("y", [S, D], F32, kind="ExternalOutput").ap()
        nblk = len(self.blocks)
        obs = nc.dram_tensor("obs", [NT, 128, 8 * T], F32, kind="Internal").ap()
        vhs = nc.dram_tensor("vhs", [NT, 64, 8 * D], BF16, kind="Internal").ap()
        kts = nc.dram_tensor("kts", [3, 4, 128, S], BF16, kind="Internal").ap()
        vsc = nc.dram_tensor("vsc", [3, S, 512], BF16, kind="Internal").ap()
        self.wbf = nc.dram_tensor("wbf", [nblk, 128, 4096], BF16, kind="Internal").ap()

        hT = self.sb("hT", [128, 8, T], F32)
        hTB = [Buf("hT%d" % c) for c in range(8)]
        uT = self.sb("uT", [128, 8, T], BF16)
        uTBs = [Buf("uT%d" % c) for c in range(8)]
        UT = "uT-per-chunk"
        self.wring = self.sb("wring", [128, NSLOT, 4096], BF16)
        self.wringB = [Buf("wr%d" % i) for i in range(NSLOT)]
        identf = self.sb("identf", [128, 128], F32)
        identb = self.sb("identb", [128, 128], BF16)
        onesb = self.sb("onesb", [128, 128], BF16)
        permb = self.sb("permb", [32, 32], BF16)
        hmask = self.sb("hmask", [64, 2, 512], F32)
        amask = self.sb("amask", [128, 6, 512], BF16)
        gains = self.sb("gains", [128, 5, 8], F32)
        gng = self.sb("gng", [128, 1], F32)
        lbt = self.sb("lbt", [128, 2, 2, 8], F32)
        lb = self.sb("lb", [128, 2, 8], F32)
        oml = self.sb("oml", [128, 2, 8], F32)
        noml = self.sb("noml", [128, 2, 8], F32)
        epsn = self.sb("epsn", [128, 1], F32)
        constB = Buf("consts")
        rope = self.sb("rope", [32, 2, T], F32)
        ropeB = Buf("rope")
        sqb = self.sb("sqb", [128, 2, T], BF16)
        sqbB = [Buf("sqb0"), Buf("sqb1")]
        rs = self.sb("rs", [128, T], F32)
        rsB = Buf("rs")
        rinvn = self.sb("rinvn", [128, T], F32)
        rinvnB = Buf("rinvn")
        ym = self.sb("ym", [128, 16 * T], BF16)
        yin = ym[:, 0:8 * T].rearrange("p (c n) -> p c n", c=8)
        yinB = Buf("yin")
        obl = self.sb("obl", [128, 2, T], F32)
        oblB = [Buf("obl0"), Buf("obl1")]
        sgt = self.sb("sgt", [128, T], F32)
        sgtB = Buf("sgt")
        merged = ym[:, 8 * T:16 * T].rearrange("p (c n) -> p c n", c=8)
        xs = ym[:, :].bitcast(F32).rearrange("p (s d) -> p s d", s=4)
        xsB = Buf("xs")
        mergedB = Buf("merged")
        gtmp = self.sb("gtmp", [128, 2, T], F32)
        gtmpB = [Buf("gtmp0"), Buf("gtmp1")]
        Sst = self.sb("Sst", [128, 8, 128], F32)
        SstB = [Buf("Sst%d" % i) for i in range(8)]
        kxT = self.sb("kxT", [128, 8, NMEM], BF16)
        vx = self.sb("vx", [128, 2, D], BF16)
        kvxB = Buf("kvx")

        ARENA_B = 92672
        arena = self.sb("arena", [128, ARENA_B // 2], BF16)

        def carve(off, shape, dt, parts=128):
            n = int(np.prod(shape[1:]))
            esz = 4 if dt == F32 else 2
            assert off % 4 == 0
            assert off + n * esz <= ARENA_B, (off, n * esz)
            v = arena[0:parts, off // 2: off // 2 + n * esz // 2]
            if dt == F32:
                v = v.bitcast(F32)
            if len(shape) == 3:
                v = v.rearrange("p (a b) -> p a b", a=shape[1])
            elif len(shape) == 4:
                v = v.rearrange("p (a b c) -> p a b c", a=shape[1], b=shape[2])
            return v, off + n * esz

        banks = [self.es.enter_context(nc.psum_tensor("bank%d" % i, [128, 512], F32)) for i in range(8)]
        bankB = [Buf("bank%d" % i, excl=True) for i in range(8)]
        self.free_banks = list(range(8))

        K = 1024
        o = 0
        h_sf, o = carve(o, [128, 2, T], F32)
        h_kk, o = carve(o, [128, 2, T], F32)
        h_R, o = carve(o, [128, 2, T], F32)
        h_sq = h_R
        h_rinv4, o = carve(o, [128, 4, T], F32)
        h_KT, o = carve(o, [128, 8, T], BF16)
        h_QT, o = carve(o, [128, 8, T], BF16)
        h_Ktok, o = carve(o, [128, 8, 8, 128], BF16)
        h_V, o = carve(o, [128, 8, D], BF16)
        h_Sbf, o = carve(o, [128, 2, 8 * 128], BF16)
        h_Asb, o = carve(o, [128, 2, 8 * 64], BF16)
        h_eB, o = carve(o, [128, 8, 8], F32)
        h_OT, o = carve(o, [128, 8, T], F32)
        h_reb, o = carve(o, [128, 8, 8], F32)
        h_sfB = [Buf("h_sf0"), Buf("h_sf1")]
        h_kkB = [Buf("h_kk0"), Buf("h_kk1")]
        h_RB = [Buf("h_R0"), Buf("h_R1")]
        h_sqB = h_RB
        h_rinvB = [Buf("h_rinv%d" % i) for i in range(4)]
        h_KTB = [Buf("h_KT%d" % i) for i in range(8)]
        h_QTB = [Buf("h_QT%d" % i) for i in range(8)]
        h_KtokB = [Buf("h_Ktok%d" % i) for i in range(8)]
        h_VB = Buf("h_V")
        h_SbfB = [Buf("h_Sbf0"), Buf("h_Sbf1")]
        h_AsbB = [Buf("h_Asb0"), Buf("h_Asb1")]
        h_eBB = [Buf("h_eB%d" % i) for i in range(8)]
        h_rebB = [Buf("h_reb%d" % i) for i in range(8)]
        h_OTBs = [Buf("h_OT%d" % i) for i in range(8)]
        stageH = h_rebB + [h_VB] + h_OTBs + h_sfB + h_kkB + h_RB + h_SbfB + h_AsbB + h_rinvB + h_KTB + h_QTB + h_KtokB + h_eBB
        o = 0
        a_QT, o = carve(o, [128, 12, T], BF16)
        KTW = 640 + 1024 + 2560
        a_KT, o = carve(o, [128, 2, KTW], BF16)
        a_V, o = carve(o, [128, 2, 45, 128], BF16)
        a_PT, o = carve(o, [128, 2, T], BF16)
        a_out, o = carve(o, [128, 4, T], BF16)
        a_rd, o = carve(o, [128, T], F32)
        a_rot, o = carve(o, [128, 4, T], F32, parts=32)
        a_kst, o = carve(o, [128, 4, T], BF16)
        a_vst, o = carve(o, [128, 4, 512], BF16)
        a_QTB = [Buf("a_QT%d" % i) for i in range(12)]
        a_KTB = [Buf("a_KTw0"), Buf("a_KTw1")]
        a_VB = [Buf("a_Vw0"), Buf("a_Vw1")]
        a_PTB = [Buf("a_PT0"), Buf("a_PT1")]
        a_outB = Buf("a_out")
        a_rdB = Buf("a_rd")
        a_rotB = [Buf("a_rot0"), Buf("a_rot1")]
        a_kstB = [Buf("a_kst%d" % i) for i in range(4)]
        a_vstB = [Buf("a_vst%d" % i) for i in range(4)]
        stageA = a_QTB + a_KTB + a_VB + a_PTB + [a_outB, a_rdB] + a_rotB + a_kstB + a_vstB
        o = 0
        c_qx, o = carve(o, [128, 8, T], BF16)
        c_PT, o = carve(o, [128, 4, T], BF16)
        c_ox, o = carve(o, [128, 8, T], BF16)
        c_rd, o = carve(o, [128, T], F32)
        c_rl, o = carve(o, [128, 2, T], F32)
        c_hid, o = carve(o, [128, 32, T], BF16)
        c_yT, o = carve(o, [128, 2, T], F32)
        c_ysb, o = carve(o, [128, 4, 512], F32)
        c_qxB = Buf("c_qx")
        c_PTB = [Buf("c_PT%d" % i) for i in range(4)]
        c_oxB = Buf("c_ox")
        c_rdB = Buf("c_rd")
        c_rlB = [Buf("c_rl0"), Buf("c_rl1")]
        c_hidB = Buf("c_hid")
        c_yTB = [Buf("c_yT0"), Buf("c_yT1")]
        c_ysbB = [Buf("c_ysb%d" % i) for i in range(4)]
        stageC = [c_qxB, c_oxB, c_rdB, c_hidB] + c_PTB + c_rlB + c_yTB + c_ysbB
        o = 0
        p_st, o = carve(o, [128, 2, 4096], F32)
        p_bf, o = carve(o, [128, 2, 4096], BF16)
        p_stB = [Buf("p_st0"), Buf("p_st1")]
        p_bfB = [Buf("p_bf0"), Buf("p_bf1")]
        p_mem, o = carve(o, [128, 2, D], F32)
        p_memB = Buf("p_mem")
        stageP = p_stB + p_bfB + [p_memB]
        allstages = {"H": stageH, "A": stageA, "C": stageC, "P": stageP}
        self.cur_stage = [None]

        def enter_stage(name):
            if self.cur_stage[0] == name:
                return
            self.cur_stage[0] = name
            acc_r = {}
            for sn, lst in allstages.items():
                if sn == name:
                    continue
                for b in lst:
                    for s, i in b.r.items():
                        if acc_r.get(s, 0) < i:
                            acc_r[s] = i
                    if b.w is not None:
                        s, i = b.w
                        if acc_r.get(s, 0) < i:
                            acc_r[s] = i
            for b in allstages[name]:
                for s, i in acc_r.items():
                    if b.r.get(s, 0) < i:
                        b.r[s] = i

        def norm_sq(c, N=T):
            sl = c % 2
            self.act(sqb[:, sl, 0:N], hT[:, c, 0:N], AF.Square, [hTB[c]], [sqbB[sl]])

        def norm_mm(nb, c, N=T):
            sl = c % 2
            self.mm(banks[nb][:, 0:N], onesb[:, :], sqb[:, sl, 0:N], c == 0, c == 7, [sqbB[sl], constB], [bankB[nb]])

        def norm_acc(nb, c, N=T):
            norm_sq(c, N)
            norm_mm(nb, c, N)

        def norm_end(nb, gidx, N=T):
            self.act(rs[:, 0:N], banks[nb][:, 0:N], AF.Ln, [bankB[nb], constB], [rsB], bias=epsn[:, 0:1],
                     scale=1.0 / D)
            self.release(nb)
            self.act(rinvn[:, 0:N], rs[:, 0:N], AF.Exp, [rsB], [rinvnB], scale=-0.5)
            for c in range(8):
                self.stt(uT[:, c, 0:N], hT[:, c, 0:N], gains[:, gidx, c:c + 1], rinvn[:, 0:N], ALU.mult, ALU.mult,
                         [hTB[c], rinvnB, constB], [uTBs[c]])

        def rmsnorm(gidx, N=T):
            nb = self.bank()
            for c in range(8):
                norm_acc(nb, c, N)
            norm_end(nb, gidx, N)

        def alias_fence(dst, src):
            acc = {}
            for bb in src:
                for s_, i_ in list(bb.r.items()) + ([bb.w] if bb.w is not None else []):
                    if acc.get(s_, 0) < i_:
                        acc[s_] = i_
            for bb in dst:
                for s_, i_ in acc.items():
                    if bb.r.get(s_, 0) < i_:
                        bb.r[s_] = i_

        def issue_x(ti):
            alias_fence([xsB], [yinB, mergedB])
            t0 = ti * T
            self.dma(xs[:, :, :], x[t0:t0 + T, :].rearrange("(s p) d -> p s d", p=128), (), [xsB], chan="xs")

        def load_x(ti):
            nb = self.bank()
            for c in range(8):
                b = self.bank()
                for s4 in range(4):
                    self.tr(banks[b][:, s4 * 128:(s4 + 1) * 128], xs[:, s4, c * 128:(c + 1) * 128], identf[:, :],
                            [xsB, constB], [bankB[b]])
                if c > 0:
                    norm_mm(nb, c - 1)
                self.copy(hT[:, c, :], banks[b][:, :], [bankB[b]], [hTB[c]], eng="act")
                self.release(b)
                norm_sq(c)
            norm_mm(nb, 7)
            norm_end(nb, 0)

        def proj_fm(wv, wB, kc, j, rhs_of, rhsB, N=T):
            b = self.bank()
            for c in range(kc):
                self.mm(banks[b][:, 0:N], wv[:, c, j * 128:(j + 1) * 128], rhs_of(c), c == 0, c == kc - 1,
                        [wB] + ([uTBs[c]] if rhsB is UT else rhsB), [bankB[b]])
            return b

        uT_of = lambda c: uT[:, c, :]

        def hgrn_prep(direction, ti):
            enter_stage("H")
            self.memset(h_kk[:, :, :], 0.0, h_kkB, eng="pool")
            fblk0 = 2 if direction == 0 else 4

            def f_front(h, j, b):
                p = h % 2
                self.act(h_sf[:, p, :], banks[b][:, :], AF.Sigmoid, [bankB[b]], [h_sfB[p]])
                self.release(b)
                self.act(h_sf[:, p, :], h_sf[:, p, :], AF.Identity, [h_sfB[p], constB], [h_sfB[p]],
                         bias=lb[:, direction, h:h + 1], scale=oml[:, direction, h:h + 1])
                fg = h_sf[:, p, :]
                fgv = fg.rearrange("q (c t) -> q c t", c=8)
                d1s = h_kk[:, 0, :]
                d1e = h_kk[:, 1, :]
                self.copy(d1s.rearrange("q (c t) -> q c t", c=8)[:, :, 0], fgv[:, :, 0], [h_sfB[p]], [h_kkB[0]])
                self.copy(d1e.rearrange("q (c t) -> q c t", c=8)[:, :, 63], fgv[:, :, 63], [h_sfB[p]], [h_kkB[1]])
                if direction == 0:
                    Ppre, PpreB, Psuf, PsufB = h_rinv4[:, j, :], h_rinvB[j], h_R[:, p, :], h_RB[p]
                else:
                    Ppre, PpreB, Psuf, PsufB = h_R[:, p, :], h_RB[p], h_rinv4[:, j, :], h_rinvB[j]
                self.P.add("dve", lambda e: e.tensor_tensor_scan(Ppre, fg, d1s, 1.0, ALU.mult, ALU.max),
                           [h_sfB[p], h_kkB[0]], [PpreB])
                self.P.add("dve", lambda e: e.tensor_tensor_scan(Psuf[:, ::-1], fg[:, ::-1], d1e[:, ::-1], 1.0,
                                                                 ALU.mult, ALU.max), [h_sfB[p], h_kkB[1]], [PsufB])
                Pprev = Ppre.rearrange("q (c t) -> q c t", c=8)
                Psufv = Psuf.rearrange("q (c t) -> q c t", c=8)
                KTv = h_KT[:, h, :].rearrange("q (c t) -> q c t", c=8)
                if direction == 0:
                    self.copy(h_eB[:, h, :], Pprev[:, :, 63], [PpreB], [h_eBB[h]])
                    self.tt(KTv[:, :, 0:63], Psufv[:, :, 1:64], Psufv[:, :, 0:63], ALU.subtract, [PsufB], [h_KTB[h]],
                            eng="pool")
                    self.ts(KTv[:, :, 63:64], Psufv[:, :, 63:64], -1.0, 1.0, ALU.mult, ALU.add, [PsufB], [h_KTB[h]],
                            eng="pool")
                else:
                    self.copy(h_eB[:, h, :], Psufv[:, :, 0], [PsufB], [h_eBB[h]])
                    self.tt(KTv[:, :, 1:64], Pprev[:, :, 0:63], Pprev[:, :, 1:64], ALU.subtract, [PpreB], [h_KTB[h]],
                            eng="pool")
                    self.ts(KTv[:, :, 0:1], Pprev[:, :, 0:1], -1.0, 1.0, ALU.mult, ALU.add, [PpreB], [h_KTB[h]],
                            eng="pool")
                self.P.add("dve", lambda e: e.reciprocal(h_reb[:, h, :], h_eB[:, h, :]), [h_eBB[h]], [h_rebB[h]])

            def f_back(h):
                for half in range(2):
                    bt = self.bank()
                    btv = banks[bt][:, :].bitcast(BF16)
                    for cc in range(4):
                        c = half * 4 + cc
                        self.tr(btv[0:64, cc * 128:(cc + 1) * 128], h_KT[:, h, c * 64:(c + 1) * 64], identb[:, :],
                                [h_KTB[h], constB], [bankB[bt]])
                    self.copy(h_Ktok[0:64, h, half * 4:half * 4 + 4, :],
                              btv[0:64, 0:512].rearrange("q (c k) -> q c k", c=4), [bankB[bt]], [h_KtokB[h]],
                              eng="act")
                    self.release(bt)
                self.tt(h_KT[:, h, :].rearrange("q (c t) -> q c t", c=8), h_KT[:, h, :].rearrange("q (c t) -> q c t", c=8),
                        h_reb[:, h, :].unsqueeze(2).broadcast_to([128, 8, 64]), ALU.mult, [h_KTB[h], h_rebB[h]],
                        [h_KTB[h]])

            for g in range(2):
                wv, wB, kc = self.w_next(self.widx["in"][fblk0 + g])
                for j in range(4):
                    h = g * 4 + j
                    b = proj_fm(wv, wB, kc, j, uT_of, UT)
                    f_front(h, j, b)
                self.w_done()
                wv, wB, kc = self.w_next(self.widx["in"][0 + g])
                for j in range(4):
                    h = g * 4 + j
                    p = h % 2
                    b = proj_fm(wv, wB, kc, j, uT_of, UT)
                    f_back(h)
                    self.act(h_sq[:, p, :], banks[b][:, :], AF.Silu, [bankB[b]], [h_sqB[p]])
                    self.release(b)
                    self.tt(h_QT[:, h, :], h_sq[:, p, :], h_rinv4[:, j, :], ALU.mult, [h_sqB[p], h_rinvB[j]],
                            [h_QTB[h]], eng="pool")
                self.w_done()
            if direction == 0:
                self.dma(h_V[0:64, :, :].rearrange("q c n -> q (c n)"), vhs[ti, :, :], (), [h_VB], chan="vhl")
            for g in (range(2) if direction == 1 else ()):
                wv, wB, kc = self.w_next(self.widx["in"][6 + g])
                for c in range(8):
                    b = self.bank()
                    for kc_ in range(8):
                        self.mm(banks[b][0:64, :], uT[:, kc_, c * 64:(c + 1) * 64], wv[:, kc_, :], kc_ == 0, kc_ == 7,
                                [wB, uTBs[kc_]], [bankB[b]])
                    self.copy(h_V[0:64, c, g * 512:(g + 1) * 512], banks[b][0:64, :], [bankB[b]], [h_VB],
                              eng=("act" if c % 2 else "dve"))
                    self.release(b)
                self.w_done()

        def hgrn_scan(direction):
            order = list(range(8)) if direction == 0 else list(range(7, -1, -1))
            for n, c in enumerate(order):
                p = n % 2
                self.copy(h_Sbf[:, p, :].rearrange("q (h v) -> q h v", h=8), Sst[:, :, :], SstB, [h_SbfB[p]])
                ba = self.bank()
                for h in range(8):
                    self.mm(banks[ba][0:64, h * 64:(h + 1) * 64], h_KT[:, h, c * 64:(c + 1) * 64],
                            h_QT[:, h, c * 64:(c + 1) * 64], True, True, [h_KTB[h], h_QTB[h]], [bankB[ba]])
                bks = []
                for half in range(2):
                    bk = self.bank()
                    bks.append(bk)
                    for hh in range(4):
                        h = half * 4 + hh
                        self.mm(banks[bk][:, hh * 128:(hh + 1) * 128], h_Ktok[0:64, h, c, :],
                                h_V[0:64, c, h * 128:(h + 1) * 128], True, True, [h_KtokB[h], h_VB], [bankB[bk]])
                self.tt(h_Asb[0:64, p, :], banks[ba][0:64, :], hmask[:, direction, :], ALU.mult, [bankB[ba], constB],
                        [h_AsbB[p]])
                self.release(ba)
                for half in range(2):
                    for hh in range(4):
                        h = half * 4 + hh
                        self.stt(Sst[:, h, :], Sst[:, h, :], h_eB[:, h, c:c + 1], banks[bks[half]][:, hh * 128:(hh + 1) * 128],
                                 ALU.mult, ALU.add, [SstB[h], h_eBB[h], bankB[bks[half]], h_SbfB[p]], [SstB[h]])
                    self.release(bks[half])
                bo = self.bank()
                for h in range(8):
                    self.mm(banks[bo][:, h * 64:(h + 1) * 64], h_Sbf[:, p, h * 128:(h + 1) * 128],
                            h_QT[:, h, c * 64:(c + 1) * 64], True, False, [h_SbfB[p], h_QTB[h]], [bankB[bo]])
                    self.mm(banks[bo][:, h * 64:(h + 1) * 64], h_V[0:64, c, h * 128:(h + 1) * 128],
                            h_Asb[0:64, p, h * 64:(h + 1) * 64], False, True, [h_VB, h_AsbB[p]], [bankB[bo]])
                self.copy(h_OT[:, :, c * 64:(c + 1) * 64], banks[bo][:, :].rearrange("q (h t) -> q h t", h=8),
                          [bankB[bo]], h_OTBs, eng="act")
                self.release(bo)

        def rot_A(b, dst, dstB, k):
            self.copy(dst, banks[b][:, :], [bankB[b]], [dstB], eng="act")
            self.tt(a_rot[0:32, 2 * k, :], banks[b][0:32, :], rope[:, 0, :], ALU.mult, [bankB[b], ropeB], [a_rotB[k]])
            self.release(b)

        def rot_B(dst, dstB, k):
            b2 = self.bank()
            self.mm(banks[b2][0:32, :], permb[:, :], dst[0:32, :], True, True, [dstB, constB], [bankB[b2]])
            self.tt(a_rot[0:32, 2 * k + 1, :], banks[b2][0:32, :], rope[:, 1, :], ALU.mult, [bankB[b2], ropeB],
                    [a_rotB[k]])
            self.release(b2)
            self.tt(dst[0:32, :], a_rot[0:32, 2 * k, :], a_rot[0:32, 2 * k + 1, :], ALU.add, [a_rotB[k]], [dstB])

        def load_rope(ti):
            t0 = ti * T
            self.dma(rope[:, :, :], rope_d[:, :, t0:t0 + T].rearrange("a p t -> p a t"), (), [ropeB], chan="rope")

        self.step('consts a')
        enter_stage("P")
        self.dma(identf[:, :], identf_d[:, :], (), [constB], chan="c0")
        self.dma(gains[:, :, :], gains_d[:, :, :], (), [constB], chan="c0")
        self.dma(gng[:, :], gng_d[:, :], (), [constB], chan="c0")
        self.dma(lbt[:, :, :, :], lbl_d[:, :, :, :], (), [constB], chan="c0")
        self.dma(hmask[:, :, :], hmask_d[:, :, :], (), [constB], chan="c0")
        self.step('consts b')
        self.dma(p_st[:, 0, 0:3072], amask_d[:, :, :].rearrange("p a t -> p (a t)"), (), [p_stB[0]], chan="pst0")
        self.copy(amask[:, :, :].rearrange("p a t -> p (a t)"), p_st[:, 0, 0:3072], [p_stB[0]], [constB])
        self.step('consts c')
        self.dma(p_st[0:32, 1, 0:32], perm_d[:, :], (), [p_stB[1]], chan="pst1")
        self.copy(permb[:, :], p_st[0:32, 1, 0:32], [p_stB[1]], [constB])
        self.step('consts d')
        self.copy(identb[:, :], identf[:, :], [constB], [constB])
        self.memset(onesb[:, :], 1.0, [constB])
        onesf = self.sb("onesf", [128, 64], F32)
        self.memset(onesf[:, :], 1.0, [constB])
        self.memset(epsn[:, :], EPS, [constB])
        epsg = self.sb("epsg", [128, 1], F32)
        self.memset(epsg[:, :], EPS, [constB])
        self.step('consts e')
        self.tt(lb[:, :, :], lbt[:, :, 0, :], lbt[:, :, 1, :], ALU.subtract, [constB], [constB])
        self.step('consts e2')
        self.act(lb[:, :, :], lb[:, :, :], AF.Sigmoid, [constB], [constB])
        self.step('consts e3')
        self.ts(oml[:, :, :], lb[:, :, :], -1.0, 1.0, ALU.mult, ALU.add, [constB], [constB])
        self.step('consts e4')
        self.ts(noml[:, :, :], oml[:, :, :], -1.0, None, ALU.mult, None, [constB], [constB])

        self.step('wconv')
        p1_blocks = [self.widx["in"][i] for i in (4, 0, 5, 1, 6, 7, 11, 12, 14, 15, 17, 18)] + list(self.widx["xkv"])
        self.wgroup = {bi: (0 if bi in p1_blocks else 1) for bi in range(len(self.blocks))}
        self.wbfB = [Buf("wbf_g0"), Buf("wbf_g1")]

        def convert(bi):
            src, r0, kc, c0 = self.blocks[bi]
            w = wsrc[src]
            g = self.wgroup[bi]
            self.dma(self.wbf[bi, :, 0:kc * 512].rearrange("p (c n) -> p c n", c=kc),
                     w[r0:r0 + kc * 128, c0:c0 + 512].rearrange("(c p) n -> p c n", p=128), (), [self.wbfB[g]],
                     chan="cv%d" % g, q="pool")

        for bi in list(self.widx["xkv"]) + p1_blocks[:12]:
            convert(bi)
        conv_later = [bi for bi in range(len(self.blocks)) if self.wgroup[bi] == 1]

        self.step('memkv')
        self.dma(p_mem[:, :, :], mem[:, :].rearrange("(s p) d -> p s d", p=128), (), [p_memB], chan="pmem")
        for c in range(8):
            b = self.bank()
            for s2 in range(2):
                self.tr(banks[b][:, s2 * 128:(s2 + 1) * 128], p_mem[:, s2, c * 128:(c + 1) * 128], identf[:, :],
                        [p_memB, constB], [bankB[b]])
            self.copy(hT[:, c, 0:NMEM], banks[b][:, 0:NMEM], [bankB[b]], [hTB[c]], eng="act")
            self.release(b)
        rmsnorm(2, N=NMEM)
        self.wseq_pro = list(self.widx["xkv"])
        for n, blk in enumerate(self.widx["xkv"]):
            slot = n % NSLOT
            self.dma(self.wring[:, slot, :], self.wbf[blk, :, :], [self.wbfB[0]], [self.wringB[slot]],
                     chan="w%d" % slot)
            wv = self.wring[:, slot, :].rearrange("p (c n) -> p c n", c=8)
            if n < 2:
                for j in range(4):
                    b = proj_fm(wv, self.wringB[slot], 8, j, lambda c: uT[:, c, 0:NMEM], UT, N=NMEM)
                    self.copy(kxT[:, n * 4 + j, :], banks[b][:, 0:NMEM], [bankB[b]], [kvxB], eng="act")
                    self.release(b)
            else:
                for s2 in range(2):
                    b = self.bank()
                    for c in range(8):
                        self.mm(banks[b][:, :], uT[:, c, s2 * 128:(s2 + 1) * 128], wv[:, c, :], c == 0, c == 7,
                                [self.wringB[slot], uTBs[c]], [bankB[b]])
                    self.copy(vx[:, s2, (n - 2) * 512:(n - 1) * 512], banks[b][:, :], [bankB[b]], [kvxB], eng="act")
                    self.release(b)

        self.wseq_build()

        self.memset(Sst[:, :, :], 0.0, SstB)
        issue_x(NT - 1)
        for ti in range(NT - 1, -1, -1):
            t0 = ti * T
            self.step('p1 load %d' % ti)
            load_x(ti)
            issue_x(ti - 1 if ti > 0 else 0)
            load_rope(ti)
            self.step('p1 prep %d' % ti)
            hgrn_prep(1, ti)
            self.dma(vhs[ti, :, :], h_V[0:64, :, :].rearrange("q c n -> q (c n)"), [h_VB], (), chan="vhst", q="pool")
            self.step('p1 scan %d' % ti)
            hgrn_scan(1)
            self.step('p1 kv %d' % ti)
            for _ in range(3):
                if conv_later:
                    convert(conv_later.pop(0))
            self.dma(obs[ti, :, :], h_OT[:, :, :].rearrange("p h t -> p (h t)"), h_OTBs, (), chan="obst", q="sp")
            enter_stage("A")
            kpend = None

            def flush_k(pk):
                pgi, pj, psl = pk
                rot_B(a_kst[:, psl, :], a_kstB[psl], psl % 2)
                self.dma(kts[pgi, pj, :, t0:t0 + T], a_kst[:, psl, :], [a_kstB[psl]], (), chan="kst%d" % psl, q="pool")

            for gi in range(3):
                self.step('p1 k %d' % gi)
                wv, wB, kc = self.w_next(self.widx["in"][11 + 3 * gi])
                for j in range(4):
                    sl = j
                    b = proj_fm(wv, wB, kc, j, uT_of, UT)
                    if kpend is not None:
                        flush_k(kpend)
                    rot_A(b, a_kst[:, sl, :], a_kstB[sl], sl % 2)
                    kpend = (gi, j, sl)
                self.w_done()
                self.step('p1 v %d' % gi)
                wv, wB, kc = self.w_next(self.widx["in"][12 + 3 * gi])
                for s4 in range(4):
                    sl = s4
                    b = self.bank()
                    for c in range(8):
                        self.mm(banks[b][:, :], uT[:, c, s4 * 128:(s4 + 1) * 128], wv[:, c, :], c == 0, c == 7,
                                [wB, uTBs[c]], [bankB[b]])
                    self.copy(a_vst[:, sl, :], banks[b][:, :], [bankB[b]], [a_vstB[sl]], eng="act")
                    self.release(b)
                    self.dma(vsc[gi, t0 + s4 * 128:t0 + (s4 + 1) * 128, :], a_vst[:, sl, :], [a_vstB[sl]], (),
                             chan="vst%d" % sl, q="pool")
                self.w_done()
            flush_k(kpend)
        while conv_later:
            convert(conv_later.pop(0))
        fence = [("#" + c, self.P.chan.get(c, 0)) for c in ("obst", "kst0", "kst1", "kst2", "kst3", "vst0", "vst1", "vst2", "vst3", "vhst")]
        for bb in a_KTB + a_VB + oblB + [h_VB]:
            for s, i in fence:
                if i:
                    bb.r[s] = max(bb.r.get(s, 0), i)

        self.step('p2 init')
        self.memset(Sst[:, :, :], 0.0, SstB)
        enter_stage("A")
        self.memset(a_KT[:, :, :], 0.0, a_KTB, eng="pool")
        self.memset(a_V[:, :, :, :], 0.0, a_VB, eng="pool")

        KOFF = (0, 640, 1664)
        VOFF = (0, 5, 13)

        def load_windows(ti, head, sl):
            t0 = ti * T
            for gi, dil in enumerate(GROUPS):
                lo = t0 - 64 * dil
                hi = t0 + T + 64 * dil
                clo, chi = max(lo, 0), min(hi, S)
                self.dma(a_KT[:, sl, KOFF[gi] + (clo - lo):KOFF[gi] + (chi - lo)], kts[gi, head, :, clo:chi], (),
                         [a_KTB[sl]], chan="akt%d" % sl)
                L = S // dil
                mq0 = t0 // dil
                nqb = 4 if dil == 1 else 1
                for blk in range(nqb + 1):
                    m0 = mq0 - 64 + 128 * blk
                    vlo, vhi = max(0, -m0), min(128, L - m0)
                    if dil == 16 and blk == 1:
                        vhi = min(vhi, 32)
                    if vhi <= vlo:
                        continue
                    b0 = VOFF[gi] + blk
                    tok0 = dil * (m0 + vlo)
                    tok1 = dil * (m0 + vhi)
                    if dil == 1:
                        self.dma(a_V[vlo:vhi, sl, b0, :], vsc[gi, tok0:tok1, head * 128:(head + 1) * 128], (),
                                 [a_VB[sl]], chan="av%d" % sl)
                    else:
                        self.dma(a_V[vlo:vhi, sl, b0:b0 + (dil - 1) * (nqb + 1) + 1:(nqb + 1), :],
                                 vsc[gi, tok0:tok1, head * 128:(head + 1) * 128].rearrange("(i r) d -> i r d", r=dil),
                                 (), [a_VB[sl]], chan="av%d" % sl)

        def attention(ti):
            t0 = ti * T
            enter_stage("A")
            load_windows(ti, 0, 0)
            load_windows(ti, 1, 1)
            pend = None
            for gi in range(3):
                wv, wB, kc = self.w_next(self.widx["in"][10 + 3 * gi])
                for j in range(4):
                    qi = gi * 4 + j
                    b = proj_fm(wv, wB, kc, j, uT_of, UT)
                    if pend is not None:
                        rot_B(a_QT[:, pend, :], a_QTB[pend], pend % 2)
                    rot_A(b, a_QT[:, qi, :], a_QTB[qi], qi % 2)
                    pend = qi
                self.w_done()
            rot_B(a_QT[:, pend, :], a_QTB[pend], pend % 2)
            for head in range(4):
                sl = head % 2
                bo = self.bank()
                bd = self.bank()
                self.memset(banks[bo][:, :], 0.0, [bankB[bo]])
                self.memset(banks[bd][:, :], 0.0, [bankB[bd]])
                jobs = []
                for gi, dil in enumerate(GROUPS):
                    L = S // dil
                    mq0 = t0 // dil
                    nqb = 4 if dil == 1 else 1
                    for kb in range(2):
                        units = []
                        for r in range(dil):
                            for qb in range(nqb):
                                blk = qb + kb
                                m0 = mq0 - 64 + 128 * blk
                                vlo, vhi = max(0, -m0), min(128, L - m0)
                                if vhi <= vlo:
                                    continue
                                units.append((r, qb, blk, m0, vlo, vhi))
                        if units:
                            jobs.append((gi, dil, kb, nqb, units))

                def scores(job, ps):
                    gi, dil, kb, nqb, units = job
                    qh = a_QT[:, gi * 4 + head, :]
                    bs = self.bank()
                    full = len(units) == dil * nqb
                    short = (dil == 16 and kb == 1)
                    if (not full) or short:
                        self.memset(banks[bs][:, :], -30000.0, [bankB[bs]])
                    nk = 32 if short else 128
                    for (r, qb, blk, m0, vlo, vhi) in units:
                        woff = KOFF[gi] + (dil * m0 + r) - (t0 - 64 * dil)
                        kap = a_KT[:, sl, woff:woff + (nk - 1) * dil + 1:dil]
                        if dil == 1:
                            qap = qh[:, qb * 128:(qb + 1) * 128]
                            oap = banks[bs][0:nk, qb * 128:(qb + 1) * 128]
                        else:
                            qap = qh[:, r::dil]
                            oap = banks[bs][0:nk, r::dil]
                        self.mm(oap, kap, qap, True, True, [a_KTB[sl], a_QTB[gi * 4 + head]], [bankB[bs]], skip=True)
                    self.act(a_PT[:, ps, :], banks[bs][:, :], AF.Exp, [bankB[bs]], [a_PTB[ps]], scale=ATT_SCALE)
                    self.release(bs)
                    self.tt(a_PT[:, ps, :], a_PT[:, ps, :], amask[:, gi * 2 + kb, :], ALU.mult, [a_PTB[ps], constB],
                            [a_PTB[ps]])
                    if dil == 1:
                        for (r, qb, blk, m0, vlo, vhi) in units:
                            if vlo > 0:
                                self.memset(a_PT[0:vlo, ps, qb * 128:(qb + 1) * 128], 0.0, [a_PTB[ps]])
                            if vhi < 128:
                                self.memset(a_PT[vhi:128, ps, qb * 128:(qb + 1) * 128], 0.0, [a_PTB[ps]])
                    else:
                        rows = set((u[4], u[5]) for u in units)
                        assert len(rows) == 1, rows
                        vlo, vhi = units[0][4], units[0][5]
                        if vlo > 0:
                            self.memset(a_PT[0:vlo, ps, :], 0.0, [a_PTB[ps]])
                        if vhi < 128 and not (short and vhi >= 32):
                            self.memset(a_PT[vhi:128, ps, :], 0.0, [a_PTB[ps]])

                def pv(job, ps):
                    gi, dil, kb, nqb, units = job
                    self.mm(banks[bd][:, :], onesb[:, :], a_PT[:, ps, :], False, False, [a_PTB[ps], constB],
                            [bankB[bd]], skip=True)
                    for (r, qb, blk, m0, vlo_, vhi_) in units:
                        bidx = VOFF[gi] + r * (nqb + 1) + blk
                        if dil == 1:
                            pap = a_PT[:, ps, qb * 128:(qb + 1) * 128]
                            oap = banks[bo][:, qb * 128:(qb + 1) * 128]
                        else:
                            pap = a_PT[:, ps, r::dil]
                            oap = banks[bo][:, r::dil]
                        self.mm(oap, a_V[:, sl, bidx, :], pap, False, False, [a_VB[sl], a_PTB[ps]], [bankB[bo]],
                                skip=True)

                pendj = None
                for n, job in enumerate(jobs):
                    scores(job, n % 2)
                    if pendj is not None:
                        pv(*pendj)
                    pendj = (job, n % 2)
                pv(*pendj)
                self.act(a_rd[:, :], banks[bd][:, :], AF.Ln, [bankB[bd]], [a_rdB])
                self.release(bd)
                self.act(a_rd[:, :], a_rd[:, :], AF.Exp, [a_rdB], [a_rdB], scale=-1.0)
                self.tt(a_out[:, head, :], banks[bo][:, :], a_rd[:, :], ALU.mult, [bankB[bo], a_rdB], [a_outB])
                self.release(bo)
                if head + 2 < 4:
                    load_windows(ti, head + 2, sl)

        def hgrn_finish(ti):
            alias_fence([yinB, mergedB], [xsB])
            sg8 = merged
            rs2 = [rs, sgt]
            rs2B = [rsB, sgtB]
            fin_banks = {}

            def load_ob(h):
                self.dma(obl[:, h % 2, :], obs[ti, :, h * T:(h + 1) * T], (), [oblB[h % 2]], chan="obl%d" % (h % 2))

            load_ob(0)
            load_ob(1)
            for g in range(2):
                wv, wB, kc = self.w_next(self.widx["in"][8 + g])
                for j in range(4):
                    h = g * 4 + j
                    b = proj_fm(wv, wB, kc, j, uT_of, UT)
                    self.act(sg8[:, h, :], banks[b][:, :], AF.Silu, [bankB[b]], [mergedB])
                    self.release(b)
                self.w_done()

            def front_dve(h):
                p = h % 2
                self.tt(h_OT[:, h, :], h_OT[:, h, :], obl[:, p, :], ALU.add, [oblB[p], h_OTBs[h]], [h_OTBs[h]])
                if h + 2 < 8:
                    load_ob(h + 2)

            def front_act_pe(h):
                p = h % 2
                self.act(sqb[:, p, :], h_OT[:, h, :], AF.Square, [h_OTBs[h]], [sqbB[p]])
                b2 = self.bank()
                self.mm(banks[b2][:, :], onesb[:, :], sqb[:, p, :], True, True, [sqbB[p], constB], [bankB[b2]])
                fin_banks[h] = b2

            def back_act(h):
                p = h % 2
                b2 = fin_banks.pop(h)
                self.act(rs2[p][:, :], banks[b2][:, :], AF.Ln, [bankB[b2], constB], [rs2B[p]], bias=epsg[:, 0:1],
                         scale=1.0 / 128)
                self.release(b2)
                self.act(rs2[p][:, :], rs2[p][:, :], AF.Exp, [rs2B[p]], [rs2B[p]], scale=-0.5)

            def back_dve(h):
                p = h % 2
                self.stt(h_OT[:, h, :], h_OT[:, h, :], gng[:, 0:1], rs2[p][:, :], ALU.mult, ALU.mult,
                         [h_OTBs[h], rs2B[p], constB], [h_OTBs[h]])
                self.tt(yin[:, h, :], h_OT[:, h, :], sg8[:, h, :], ALU.mult, [h_OTBs[h], mergedB], [yinB])

            front_dve(0)
            front_act_pe(0)
            front_dve(1)
            front_act_pe(1)
            for h in range(8):
                if h + 2 < 8:
                    front_dve(h + 2)
                back_act(h)
                if h + 2 < 8:
                    front_act_pe(h + 2)
                back_dve(h)

        def merge_and_out():
            for bq in range(2):
                who, whoB, _ = self.w_next(self.widx["ho"][bq])
                wgh, wghB, _ = self.w_next(self.widx["in"][19 + bq])
                for j in range(4):
                    jj = bq * 4 + j
                    p = j % 2
                    b = proj_fm(wgh, wghB, 8, j, uT_of, UT)
                    self.act(gtmp[:, p, :], banks[b][:, :], AF.Sigmoid, [bankB[b]], [gtmpB[p]])
                    self.release(b)
                    b = proj_fm(who, whoB, 8, j, lambda c: yin[:, c, :], [yinB])
                    self.tt(merged[:, jj, :], banks[b][:, :], gtmp[:, p, :], ALU.mult, [bankB[b], gtmpB[p]], [mergedB])
                    self.release(b)
                self.w_done()
                wao, waoB, _ = self.w_next(self.widx["ao"][bq])
                wga, wgaB, _ = self.w_next(self.widx["in"][21 + bq])
                for j in range(4):
                    jj = bq * 4 + j
                    p = j % 2
                    b = proj_fm(wga, wgaB, 8, j, uT_of, UT)
                    self.act(gtmp[:, p, :], banks[b][:, :], AF.Sigmoid, [bankB[b]], [gtmpB[p]])
                    self.release(b)
                    b = proj_fm(wao, waoB, 4, j, lambda c: a_out[:, c, :], [a_outB])
                    self.tt(gtmp[:, p, :], banks[b][:, :], gtmp[:, p, :], ALU.mult, [bankB[b], gtmpB[p]], [gtmpB[p]])
                    self.release(b)
                    self.tt(merged[:, jj, :], merged[:, jj, :], gtmp[:, p, :], ALU.add, [mergedB, gtmpB[p]], [mergedB],
                            eng="pool")
                self.w_done()
            nb = self.bank()
            for bq in range(2):
                wv, wB, _ = self.w_next(self.widx["wo"][bq])
                for j in range(4):
                    jj = bq * 4 + j
                    b = proj_fm(wv, wB, 8, j, lambda c: merged[:, c, :], [mergedB])
                    if jj > 0:
                        norm_mm(nb, jj - 1)
                    self.tt(hT[:, jj, :], hT[:, jj, :], banks[b][:, :], ALU.add, [hTB[jj], bankB[b]], [hTB[jj]])
                    self.release(b)
                    norm_sq(jj)
                self.w_done()
            norm_mm(nb, 7)
            return nb

        def cross_attention(nb_in):
            enter_stage("C")
            norm_end(nb_in, 1)
            for bq in range(2):
                wv, wB, _ = self.w_next(self.widx["xq"][bq])
                for j in range(4):
                    b = proj_fm(wv, wB, 8, j, uT_of, UT)
                    self.copy(c_qx[:, bq * 4 + j, :], banks[b][:, :], [bankB[b]], [c_qxB], eng="act")
                    self.release(b)
                self.w_done()
            for hx in range(4):
                bd = self.bank()
                for mb in range(2):
                    bs = self.bank()
                    for dc in range(2):
                        self.mm(banks[bs][:, :], kxT[:, hx * 2 + dc, mb * 128:(mb + 1) * 128], c_qx[:, hx * 2 + dc, :],
                                dc == 0, dc == 1, [kvxB, c_qxB], [bankB[bs]])
                    ps = (hx * 2 + mb) % 4
                    self.act(c_PT[:, ps, :], banks[bs][:, :], AF.Exp, [bankB[bs]], [c_PTB[ps]], scale=XA_SCALE)
                    self.release(bs)
                    self.mm(banks[bd][:, :], onesb[:, :], c_PT[:, ps, :], mb == 0, mb == 1, [c_PTB[ps], constB],
                            [bankB[bd]])
                self.act(c_rd[:, :], banks[bd][:, :], AF.Ln, [bankB[bd]], [c_rdB])
                self.release(bd)
                self.act(c_rd[:, :], c_rd[:, :], AF.Exp, [c_rdB], [c_rdB], scale=-1.0)
                for dc in range(2):
                    bo = self.bank()
                    for mb in range(2):
                        ps = (hx * 2 + mb) % 4
                        self.mm(banks[bo][:, :], vx[:, mb, (hx * 2 + dc) * 128:(hx * 2 + dc + 1) * 128], c_PT[:, ps, :],
                                mb == 0, mb == 1, [kvxB, c_PTB[ps]], [bankB[bo]])
                    self.tt(c_ox[:, hx * 2 + dc, :], banks[bo][:, :], c_rd[:, :], ALU.mult, [bankB[bo], c_rdB],
                            [c_oxB])
                    self.release(bo)
            nb = self.bank()
            for bq in range(2):
                wv, wB, _ = self.w_next(self.widx["xo"][bq])
                for j in range(4):
                    jj = bq * 4 + j
                    b = proj_fm(wv, wB, 8, j, lambda c: c_ox[:, c, :], [c_oxB])
                    if jj > 0:
                        norm_mm(nb, jj - 1)
                    self.tt(hT[:, jj, :], hT[:, jj, :], banks[b][:, :], ALU.add, [hTB[jj], bankB[b]], [hTB[jj]])
                    self.release(b)
                    norm_sq(jj)
                self.w_done()
            norm_mm(nb, 7)
            return nb

        def ffn(nb_in):
            norm_end(nb_in, 3)
            nbank = self.bank()
            pend_mm = None
            for bq in range(8):
                wv, wB, _ = self.w_next(self.widx["f1"][bq])
                for j in range(4):
                    sl = j % 2
                    b = proj_fm(wv, wB, 8, j, uT_of, UT)
                    self.act(c_rl[:, sl, :], banks[b][:, :], AF.Relu, [bankB[b]], [c_rlB[sl]])
                    self.release(b)
                    self.tt(c_hid[:, bq * 4 + j, :], c_rl[:, sl, :], c_rl[:, sl, :], ALU.mult, [c_rlB[sl]], [c_hidB],
                            eng="pool")
                self.w_done()
            for nb in range(2):
                accs = [self.bank() for _ in range(4)]
                for kg in range(4):
                    wv, wB, _ = self.w_next(self.widx["f2"][nb * 4 + kg])
                    for j in range(4):
                        for c in range(8):
                            self.mm(banks[accs[j]][:, :], wv[:, c, j * 128:(j + 1) * 128], c_hid[:, kg * 8 + c, :],
                                    kg == 0 and c == 0, kg == 3 and c == 7, [wB, c_hidB], [bankB[accs[j]]])
                    self.w_done()
                    if kg == 0 and pend_mm is not None:
                        norm_mm(nbank, pend_mm)
                        pend_mm = None
                for j in range(4):
                    jj = nb * 4 + j
                    self.tt(hT[:, jj, :], hT[:, jj, :], banks[accs[j]][:, :], ALU.add, [hTB[jj], bankB[accs[j]]],
                            [hTB[jj]])
                    self.release(accs[j])
                    norm_sq(jj)
                    if j > 0:
                        norm_mm(nbank, jj - 1)
                pend_mm = nb * 4 + 3
            norm_mm(nbank, pend_mm)
            return nbank

        def final_out(ti, nb_in):
            t0 = ti * T
            b = nb_in
            self.act(rs[:, :], banks[b][:, :], AF.Ln, [bankB[b], constB], [rsB], bias=epsn[:, 0:1], scale=1.0 / D)
            self.release(b)
            self.act(rinvn[:, :], rs[:, :], AF.Exp, [rsB], [rinvnB], scale=-0.5)
            for half in range(2):
                bts = [self.bank() for _ in range(4)]
                for cc in range(4):
                    c = half * 4 + cc
                    sl = c % 2
                    self.stt(c_yT[:, sl, :], hT[:, c, :], gains[:, 4, c:c + 1], rinvn[:, :], ALU.mult, ALU.mult,
                             [hTB[c], rinvnB, constB], [c_yTB[sl]])
                    for s4 in range(4):
                        self.tr(banks[bts[s4]][:, cc * 128:(cc + 1) * 128], c_yT[:, sl, s4 * 128:(s4 + 1) * 128],
                                identf[:, :], [c_yTB[sl], constB], [bankB[bts[s4]]])
                for s4 in range(4):
                    self.copy(c_ysb[:, s4, :], banks[bts[s4]][:, :], [bankB[bts[s4]]], [c_ysbB[s4]],
                              eng=("act" if s4 % 2 else "dve"))
                    self.release(bts[s4])
                    self.dma(y[t0 + s4 * 128:t0 + (s4 + 1) * 128, half * 512:(half + 1) * 512], c_ysb[:, s4, :],
                             [c_ysbB[s4]], (), chan="yst%d" % s4, q="pool")

        for ti in range(NT):
            self.step('p2 load %d' % ti)
            load_x(ti)
            load_rope(ti)
            self.step('p2 prep %d' % ti)
            hgrn_prep(0, ti)
            self.step('p2 scan %d' % ti)
            hgrn_scan(0)
            self.step('p2 finish %d' % ti)
            hgrn_finish(ti)
            self.step('p2 attn %d' % ti)
            attention(ti)
            self.step('p2 merge %d' % ti)
            nb1 = merge_and_out()
            if ti + 1 < NT:
                issue_x(ti + 1)
            self.step('p2 cross %d' % ti)
            nb2 = cross_attention(nb1)
            self.step('p2 ffn %d' % ti)
            nb3 = ffn(nb2)
            self.step('p2 final %d' % ti)
            final_out(ti, nb3)
        assert self.wpos == len(self.wseq), (self.wpos, len(self.wseq))
        if self.P.disabled:
            self.P.disabled = False
            dummyB = Buf("dummy")
            self.dma(sgt[:, 0:256], obs[0, :, 0:256], (), [dummyB], chan="dmy")
            self.dma(sqb[:, 0, 0:256], kts[0, 0, :, 0:256], (), [dummyB], chan="dmy")
            self.dma(sqb[:, 1, 0:256], vsc[0, 0:128, 0:256], (), [dummyB], chan="dmy")

        P.finalize()
        chans = sorted(P.chan.keys())
        sems = {}
        for e in ("pe", "act", "dve", "pool", "sp"):
            sems[e] = self.es.enter_context(nc.semaphore("s_" + e))
        chansems = {c: self.es.enter_context(nc.semaphore("c_" + c)) for c in chans}
        with nc.Block() as block:
            @block.tensor
            def _(e):
                P.emit("pe", e, sems, chansems)

            @block.scalar
            def _(e):
                P.emit("act", e, sems, chansems)

            @block.vector
            def _(e):
                P.emit("dve", e, sems, chansems)

            @block.gpsimd
            def _(e):
                P.emit("pool", e, sems, chansems, final_wait=True)

            @block.sync
            def _(e):
                P.emit("sp", e, sems, chansems)
        self.es.close()
        return nc


def host_consts(S):
    half = 16
    inv = (500000.0 ** (-np.arange(half, dtype=np.float32) * 2.0 / 32)).astype(np.float32)
    pos = np.arange(S, dtype=np.float32)
    ang = (pos[:, None] * inv[None, :]).astype(np.float32)
    cos = np.cos(ang).astype(np.float32).T
    sin = np.sin(ang).astype(np.float32).T
    rope = np.zeros((2, 32, S), np.float32)
    rope[0, 0:16] = cos
    rope[0, 16:32] = cos
    rope[1, 0:16] = -sin
    rope[1, 16:32] = sin
    perm = np.zeros((32, 32), np.float32)
    for m in range(32):
        perm[(m + 16) % 32, m] = 1.0
    s = np.arange(64)[:, None]
    t = np.arange(64)[None, :]
    hmask = np.zeros((64, 2, 512), np.float32)
    hmask[:, 0, :] = np.tile((s <= t).astype(np.float32), (1, 8))
    hmask[:, 1, :] = np.tile((s >= t).astype(np.float32), (1, 8))
    i = np.arange(128)[:, None]
    tt = np.arange(512)[None, :]
    amask = np.zeros((128, 6, 512), np.float32)
    for gi, dil in enumerate(GROUPS):
        j = (tt % 128) if dil == 1 else (tt // dil)
        amask[:, gi * 2 + 0, :] = (i >= j)
        amask[:, gi * 2 + 1, :] = (i <= j)
    return dict(rope=rope, perm=perm, hmask=hmask, amask=amask, identf=np.eye(128, dtype=np.float32))


def chunkmajor(g):
    return np.ascontiguousarray(np.asarray(g, np.float32).reshape(8, 128).T)


_CACHE = {}
_LIMIT = 10 ** 9
_SKIP = set()
_DEBUG = False


def run_cores(xs, mems, wts, S):
    if S not in _CACHE:
        _CACHE[S] = Kern(S, limit=_LIMIT, debug=_DEBUG).build()
    nc = _CACHE[S]
    hc = host_consts(S)
    gains = np.stack([chunkmajor(wts["mix_norm_g"]), chunkmajor(wts["xa_norm_g"]), chunkmajor(wts["mem_norm_g"]),
                      chunkmajor(wts["ffn_norm_g"]), chunkmajor(wts["final_norm_g"])], axis=1)
    gng = np.asarray(wts["hgrn_gnorm_g"], np.float32).reshape(128, 1)
    lbl = np.ascontiguousarray(np.asarray(wts["hgrn_lb_logits"], np.float32).reshape(2, 2, 8, 128).transpose(3, 0, 1, 2))
    shared = {
        "w_in": np.ascontiguousarray(wts["w_in"].reshape(D, NIN)),
        "w_hgrn_o": np.ascontiguousarray(wts["w_hgrn_o"].reshape(D, D)),
        "w_attn_o": np.ascontiguousarray(wts["w_attn_o"].reshape(512, D)),
        "w_out": np.ascontiguousarray(wts["w_out"].reshape(D, D)),
        "w_xq": np.ascontiguousarray(wts["w_xq"].reshape(D, D)),
        "w_xkv": np.ascontiguousarray(wts["w_xkv"].reshape(D, 2 * D)),
        "w_xo": np.ascontiguousarray(wts["w_xo"].reshape(D, D)),
        "w_ffn1": np.ascontiguousarray(wts["w_ffn1"].reshape(D, 4 * D)),
        "w_ffn2": np.ascontiguousarray(wts["w_ffn2"].reshape(4 * D, D)),
        "gains": np.ascontiguousarray(gains), "gng": gng, "lbl": lbl,
        "rope": hc["rope"], "identf": hc["identf"], "perm": hc["perm"], "hmask": hc["hmask"], "amask": hc["amask"],
    }
    in_maps = []
    for xc, mc in zip(xs, mems):
        m = dict(shared)
        m["x"] = np.ascontiguousarray(xc, dtype=np.float32)
        m["mem"] = np.ascontiguousarray(mc, dtype=np.float32)
        in_maps.append(m)
    res = run_bass_kernel_spmd(nc, in_maps, core_ids=list(range(len(xs))))
    return [r["y"] for r in res.results]


def kernel(x_prompt, x_sample, mem_prompt, mem_sample, mix_norm_g, w_in, hgrn_lb_logits, hgrn_gnorm_g,
           w_hgrn_o, w_attn_o, w_out, xa_norm_g, mem_norm_g, w_xq, w_xkv, w_xo, ffn_norm_g, w_ffn1,
           w_ffn2, final_norm_g):
    f = lambda a: np.asarray(a, dtype=np.float32)
    x_prompt, x_sample, mem_prompt, mem_sample = f(x_prompt), f(x_sample), f(mem_prompt), f(mem_sample)
    wts = dict(mix_norm_g=f(mix_norm_g), w_in=f(w_in), hgrn_lb_logits=f(hgrn_lb_logits), hgrn_gnorm_g=f(hgrn_gnorm_g),
               w_hgrn_o=f(w_hgrn_o), w_attn_o=f(w_attn_o), w_out=f(w_out), xa_norm_g=f(xa_norm_g),
               mem_norm_g=f(mem_norm_g), w_xq=f(w_xq), w_xkv=f(w_xkv), w_xo=f(w_xo), ffn_norm_g=f(ffn_norm_g),
               w_ffn1=f(w_ffn1), w_ffn2=f(w_ffn2), final_norm_g=f(final_norm_g))
    S = x_prompt.shape[1]
    seqs = [x_prompt[0], x_prompt[1], x_sample[0], x_sample[1], x_sample[2], x_sample[3], x_prompt[0], x_prompt[1]]
    mems = [mem_prompt[0], mem_prompt[1], mem_sample[0], mem_sample[1], mem_sample[2], mem_sample[3],
            mem_prompt[0], mem_prompt[1]]
    ys = run_cores(seqs, mems, wts, S)
    y_prompt = np.stack([ys[0], ys[1]], axis=0).astype(np.float32)
    y_sample = np.stack([ys[2], ys[3], ys[4], ys[5]], axis=0).astype(np.float32)
    return (y_prompt, y_sample)
```
